# Optimizing a Trainium2 kernel written in Bass

```python
import math
import jax, jax.numpy as jnp
from jax import lax
import numpy as np

D_MODEL = 2048
BATCH = 8
SEQ = 2048
DEPTH = 1
DEC_BATCH = 32
DEC_SEQ = 4
PAST_LEN = 8192
PAGE_SIZE = 128

D_FF = 5632
RMS_EPS = 1e-6
RW_HEADS = 16
RW_HEAD_DIM = 64
RW_DIM = RW_HEADS * RW_HEAD_DIM
RW_LORA_W = 64
RW_LORA_A = 64
RW_LORA_G = 128
RW_PROJ = 3 * RW_DIM + RW_LORA_W + RW_LORA_A + RW_LORA_G
RW_LN_EPS = 64e-5
NSA_HEADS = 8
NSA_KV_HEADS = 2
NSA_HPG = NSA_HEADS // NSA_KV_HEADS
NSA_HEAD_DIM = 128
NSA_DIM = NSA_HEADS * NSA_HEAD_DIM
NSA_KV_DIM = NSA_KV_HEADS * NSA_HEAD_DIM
CMP_BLOCK = 32
CMP_HID = 256
SEL_BLOCK = 64
TOP_N = 16
WINDOW = 512
N_BUCKETS = 32
REL_MAX_EXACT = 16
REL_MAX_DIST = 1024
Q_BLOCK = 128
NEG_BIG = -1e30
FORCE_BONUS = 1e4
IN_COLS = RW_PROJ + NSA_DIM + 6 * NSA_KV_DIM + 3 * NSA_HEADS + 2 * D_MODEL

kernel_name = "rwkv7_nsa_parallel_macaron_step"


def rmsnorm(x, g):
    xf = x.astype(jnp.float32)
    y = xf * lax.rsqrt(jnp.mean(xf * xf, axis=-1, keepdims=True) + RMS_EPS)
    return (y * g.astype(jnp.float32)).astype(x.dtype)


def swiglu_half_step(x, pre_g, post_g, w1, w3, w2):
    h = rmsnorm(x, pre_g)
    f = (jax.nn.silu(h @ w1) * (h @ w3)) @ w2
    return x + 0.5 * rmsnorm(f, post_g)


def rel_bucket(dist):
    n = jnp.maximum(dist, 0)
    nf = jnp.maximum(n, 1).astype(jnp.float32)
    large = REL_MAX_EXACT + (jnp.log(nf / REL_MAX_EXACT) / math.log(REL_MAX_DIST / REL_MAX_EXACT)
                             * (N_BUCKETS - REL_MAX_EXACT)).astype(jnp.int32)
    large = jnp.minimum(large, N_BUCKETS - 1)
    return jnp.where(n < REL_MAX_EXACT, n, large)


def masked_softmax(s, mask):
    s = jnp.where(mask, s.astype(jnp.float32), NEG_BIG)
    e = jnp.where(mask, jnp.exp(s - jnp.max(s, axis=-1, keepdims=True)), 0.0)
    return e / jnp.maximum(jnp.sum(e, axis=-1, keepdims=True), 1e-30)


def rwkv7_time_mix(p, prev_row, s0, mu, w0, w2, a0, a2, g2, k_k, k_a, r_k, lnx_w, lnx_b):
    f32 = jnp.float32
    b, t, _ = p.shape
    p_prev = jnp.concatenate([prev_row[:, None].astype(p.dtype), p[:, :-1]], axis=1)
    xs = p + (p_prev - p) * mu
    sp = [RW_DIM, 2 * RW_DIM, 3 * RW_DIM, 3 * RW_DIM + RW_LORA_W, 3 * RW_DIM + RW_LORA_W + RW_LORA_A]
    r, k, v, wd, ad, gd = jnp.split(xs, sp, axis=-1)
    w = -jax.nn.softplus(-(w0 + jnp.tanh(wd) @ w2).astype(f32)) - 0.5
    decay = jnp.exp(-jnp.exp(w))
    a = jax.nn.sigmoid((a0 + ad @ a2).astype(f32))
    g = jax.nn.sigmoid(gd) @ g2
    k_mod = k.astype(f32) * (1.0 + (a - 1.0) * k_a)

    def heads(z):
        return z.astype(f32).reshape(b, t, RW_HEADS, RW_HEAD_DIM)

    kk = heads(k * k_k)
    kk = kk / jnp.maximum(jnp.sqrt(jnp.sum(kk * kk, axis=-1, keepdims=True)), 1e-12)
    r_h, k_h, v_h, w_h, a_h = heads(r), heads(k_mod), heads(v), heads(decay), heads(a)

    def step(S, inp):
        r_t, k_t, v_t, w_t, kk_t, a_t = inp
        sk = jnp.einsum('bhij,bhj->bhi', S, kk_t)
        S = (S * w_t[:, :, None, :] - sk[..., None] * (kk_t * a_t)[:, :, None, :]
             + v_t[..., None] * k_t[:, :, None, :])
        return S, jnp.einsum('bhij,bhj->bhi', S, r_t)

    seq_first = [jnp.moveaxis(z, 1, 0) for z in (r_h, k_h, v_h, w_h, kk, a_h)]
    s_fin, y = lax.scan(step, s0.astype(f32), tuple(seq_first))
    y = jnp.moveaxis(y, 0, 1)
    mean = jnp.mean(y, axis=-1, keepdims=True)
    var = jnp.mean(jnp.square(y - mean), axis=-1, keepdims=True)
    y = ((y - mean) * lax.rsqrt(var + RW_LN_EPS)).reshape(b, t, RW_DIM) * lnx_w + lnx_b
    bonus = jnp.sum(r_h * k_h * r_k, axis=-1, keepdims=True) * v_h
    y = (y + bonus.reshape(b, t, RW_DIM)) * g
    return y.astype(p.dtype), s_fin, p[:, -1]


def nsa_attention(q, gates, kv_all, win_all, q_off, k_off, rel_bias, cmp_pe, cmp_w1, cmp_w2):
    f32 = jnp.float32
    G, HPG, hd = NSA_KV_HEADS, NSA_HPG, NSA_HEAD_DIM
    b, t = q.shape[:2]
    L0 = kv_all.shape[1]
    L = -(-L0 // SEL_BLOCK) * SEL_BLOCK
    kv_all = jnp.pad(kv_all, ((0, 0), (0, L - L0), (0, 0), (0, 0), (0, 0)))
    n_cmp, n_sel = L // CMP_BLOCK, L // SEL_BLOCK
    top_n = min(TOP_N, n_sel)
    blocks = kv_all[:, :, :2].reshape(b, n_cmp, CMP_BLOCK, 2, G, hd) + cmp_pe[:, :, None, :]
    flat = blocks.transpose(0, 1, 3, 4, 2, 5).reshape(b, n_cmp, 2, G, CMP_BLOCK * hd)
    hid = jax.nn.gelu(jnp.einsum('bnkgf,kfc->bnkgc', flat, cmp_w1))
    kc = jnp.einsum('bnkgc,kcd->bnkgd', hid, cmp_w2)
    sel = kv_all[:, :, 2:].reshape(b, n_sel, SEL_BLOCK, 2, G, hd)
    win_pad = jnp.pad(win_all, ((0, 0), (WINDOW, 0), (0, 0), (0, 0), (0, 0)))
    qb = math.gcd(t, Q_BLOCK)
    nqb = t // qb
    q_blocks = q.reshape(b, nqb, qb, G, HPG, hd)
    g_blocks = jax.nn.sigmoid(gates.astype(f32)).reshape(b, nqb, qb, 3, G, HPG)
    scale = NSA_HEAD_DIM ** -0.5
    bias_g = rel_bias.reshape(N_BUCKETS, G, HPG).transpose(1, 0, 2)
    cmp_end = jnp.arange(n_cmp) * CMP_BLOCK + CMP_BLOCK - 1
    sel_ids = jnp.arange(n_sel)
    band = WINDOW + qb
    base = q_off - k_off

    def per_block(kc_b, sel_b, win_b, q_i, g_i, i):
        dt = q_i.dtype
        pos = q_off + i * qb + jnp.arange(qb)
        d_c = pos[:, None] - cmp_end[None, :]
        bias_c = jax.vmap(lambda tb: tb[rel_bucket(d_c)])(bias_g)
        s_c = jnp.einsum('qghd,ngd->ghqn', q_i, kc_b[:, 0]).astype(f32) * scale + bias_c.transpose(0, 3, 1, 2)
        p_c = masked_softmax(s_c, (d_c >= 0)[None, None])
        o_c = jnp.einsum('ghqn,ngd->qghd', p_c.astype(dt), kc_b[:, 1])
        imp = p_c.sum(axis=1).reshape(G, qb, n_sel, SEL_BLOCK // CMP_BLOCK).sum(axis=-1)
        cur = pos // SEL_BLOCK
        forced = (sel_ids[None, :] == 0) | (sel_ids[None, :] == cur[:, None]) | (sel_ids[None, :] == cur[:, None] - 1)
        imp = jnp.where(forced[None], imp + FORCE_BONUS, imp)
        imp = jnp.where((sel_ids[None, :] * SEL_BLOCK <= pos[:, None])[None], imp, NEG_BIG)
        idx = lax.top_k(imp, top_n)[1]
        gath = jax.vmap(lambda s, ix: s[ix])(sel_b.transpose(3, 0, 1, 2, 4), idx)
        gath = gath.reshape(G, qb, top_n * SEL_BLOCK, 2, hd)
        kpos = (idx[..., None] * SEL_BLOCK + jnp.arange(SEL_BLOCK)).reshape(G, qb, top_n * SEL_BLOCK)
        d_s = pos[None, :, None] - kpos
        bias_s = jax.vmap(lambda tb, d: tb[rel_bucket(d)])(bias_g, d_s)
        s_s = jnp.einsum('qghd,gqkd->ghqk', q_i, gath[..., 0, :]).astype(f32) * scale + bias_s.transpose(0, 3, 1, 2)
        p_s = masked_softmax(s_s, (d_s >= 0)[:, None])
        o_s = jnp.einsum('ghqk,gqkd->qghd', p_s.astype(dt), gath[..., 1, :])
        start = base + i * qb
        kw = lax.dynamic_slice_in_dim(win_b, start, band, axis=0)
        kpos_w = k_off - WINDOW + start + jnp.arange(band)
        d_w = pos[:, None] - kpos_w[None, :]
        mask_w = (d_w >= 0) & (d_w < WINDOW) & (kpos_w >= k_off)[None, :]
        bias_w = jax.vmap(lambda tb: tb[rel_bucket(d_w)])(bias_g)
        s_w = jnp.einsum('qghd,kgd->ghqk', q_i, kw[:, 0]).astype(f32) * scale + bias_w.transpose(0, 3, 1, 2)
        p_w = masked_softmax(s_w, mask_w[None, None])
        o_w = jnp.einsum('ghqk,kgd->qghd', p_w.astype(dt), kw[:, 1])
        o = g_i[:, 0, ..., None] * o_c + g_i[:, 1, ..., None] * o_s + g_i[:, 2, ..., None] * o_w
        return o.astype(dt)

    def per_seq(args):
        kc_b, sel_b, win_b, q_b, g_b = args
        return lax.map(lambda a: per_block(kc_b, sel_b, win_b, a[0], a[1], a[2]),
                       (q_b, g_b, jnp.arange(nqb)))

    out = lax.map(per_seq, (kc, sel, win_pad, q_blocks, g_blocks))
    return out.reshape(b, t, NSA_DIM)


def hybrid_layer(x, q_off, kv_past, win_past, s0, shift0, lw, rel_bias):
    (f1_pre, f1_post, f1_w1, f1_w3, f1_w2, mix_pre, mix_post, w_in,
     rw_mu, rw_w0, rw_w2, rw_a0, rw_a2, rw_g2, rw_k_k, rw_k_a, rw_r_k, rw_lnx_w, rw_lnx_b,
     cmp_pe, cmp_w1, cmp_w2, w_br_rw, w_br_nsa, w_out,
     f2_pre, f2_post, f2_w1, f2_w3, f2_w2) = lw
    b, t, _ = x.shape
    x = swiglu_half_step(x, f1_pre, f1_post, f1_w1, f1_w3, f1_w2)
    xn = rmsnorm(x, mix_pre)
    splits = np.cumsum([RW_PROJ, NSA_DIM, 4 * NSA_KV_DIM, 2 * NSA_KV_DIM, 3 * NSA_HEADS, D_MODEL]).tolist()
    p_rw, q, kv_rows, win_rows, nsa_gate, gate_rw, gate_nsa = jnp.split(xn @ w_in, splits, axis=-1)
    y_rw, s_fin, shift_new = rwkv7_time_mix(p_rw, shift0, s0, rw_mu, rw_w0, rw_w2, rw_a0, rw_a2, rw_g2,
                                            rw_k_k, rw_k_a, rw_r_k, rw_lnx_w, rw_lnx_b)
    kv_new = kv_rows.reshape(b, t, 4, NSA_KV_HEADS, NSA_HEAD_DIM)
    win_new = win_rows.reshape(b, t, 2, NSA_KV_HEADS, NSA_HEAD_DIM)
    kv_all = jnp.concatenate([kv_past.astype(x.dtype), kv_new], axis=1)
    win_all = jnp.concatenate([win_past.astype(x.dtype), win_new], axis=1)
    y_nsa = nsa_attention(q.reshape(b, t, NSA_HEADS, NSA_HEAD_DIM), nsa_gate.reshape(b, t, 3, NSA_HEADS),
                          kv_all, win_all, q_off, q_off - win_past.shape[1], rel_bias, cmp_pe, cmp_w1, cmp_w2)
    merged = jax.nn.sigmoid(gate_rw) * (y_rw @ w_br_rw) + jax.nn.sigmoid(gate_nsa) * (y_nsa @ w_br_nsa)
    x = x + rmsnorm(merged @ w_out, mix_post)
    x = swiglu_half_step(x, f2_pre, f2_post, f2_w1, f2_w3, f2_w2)
    keep = min(WINDOW, win_all.shape[1])
    return x, kv_new, win_all[:, win_all.shape[1] - keep:], s_fin, shift_new


def setup_inputs(seed: int = 0) -> dict:
    key = jax.random.key(seed)
    keys = jax.random.split(key, 48)
    cnt = [0]
    f32 = jnp.float32

    def nk():
        cnt[0] += 1
        return keys[cnt[0] - 1]

    def nrm(shape, scale):
        return scale * jax.random.normal(nk(), shape, f32)

    def gain(shape):
        return 1.0 + nrm(shape, 0.05)

    G, hd = NSA_KV_HEADS, NSA_HEAD_DIM
    n_pages = PAST_LEN // PAGE_SIZE
    n_pool = (DEC_BATCH * n_pages * 5) // 4
    win_buf = min(WINDOW, PAST_LEN)
    Dp = (DEPTH,)
    return {
        "x_prompt": nrm((BATCH, SEQ, D_MODEL), 1.0),
        "x_sample": nrm((DEC_BATCH, DEC_SEQ, D_MODEL), 1.0),
        "cache_kv": nrm(Dp + (n_pool, PAGE_SIZE, 4, G, hd), 1.0),
        "cache_win": nrm(Dp + (DEC_BATCH, win_buf, 2, G, hd), 1.0),
        "state_rwkv": nrm(Dp + (DEC_BATCH, RW_HEADS, RW_HEAD_DIM, RW_HEAD_DIM), 0.3),
        "state_shift": nrm(Dp + (DEC_BATCH, RW_PROJ), 1.0),
        "page_table": jax.random.permutation(nk(), n_pool)[:DEC_BATCH * n_pages].reshape(DEC_BATCH, n_pages).astype(jnp.int32),
        "ffn1_pre_g": gain(Dp + (D_MODEL,)),
        "ffn1_post_g": gain(Dp + (D_MODEL,)),
        "ffn1_w1": nrm(Dp + (D_MODEL, D_FF), D_MODEL ** -0.5),
        "ffn1_w3": nrm(Dp + (D_MODEL, D_FF), D_MODEL ** -0.5),
        "ffn1_w2": nrm(Dp + (D_FF, D_MODEL), D_FF ** -0.5),
        "mix_pre_g": gain(Dp + (D_MODEL,)),
        "mix_post_g": gain(Dp + (D_MODEL,)),
        "w_in": nrm(Dp + (D_MODEL, IN_COLS), D_MODEL ** -0.5),
        "rw_mu": jax.random.uniform(nk(), Dp + (RW_PROJ,), f32),
        "rw_w0": -0.6 + nrm(Dp + (RW_DIM,), 0.5),
        "rw_w2": nrm(Dp + (RW_LORA_W, RW_DIM), 0.5 * RW_LORA_W ** -0.5),
        "rw_a0": nrm(Dp + (RW_DIM,), 0.3),
        "rw_a2": nrm(Dp + (RW_LORA_A, RW_DIM), 0.5 * RW_LORA_A ** -0.5),
        "rw_g2": nrm(Dp + (RW_LORA_G, RW_DIM), RW_LORA_G ** -0.5),
        "rw_k_k": 0.85 + nrm(Dp + (RW_DIM,), 0.05),
        "rw_k_a": 1.0 + nrm(Dp + (RW_DIM,), 0.05),
        "rw_r_k": nrm(Dp + (RW_HEADS, RW_HEAD_DIM), 0.1),
        "rw_lnx_w": gain(Dp + (RW_DIM,)),
        "rw_lnx_b": nrm(Dp + (RW_DIM,), 0.02),
        "cmp_pe": nrm(Dp + (CMP_BLOCK, 2, hd), 0.1),
        "cmp_w1": nrm(Dp + (2, CMP_BLOCK * hd, CMP_HID), (CMP_BLOCK * hd) ** -0.5),
        "cmp_w2": nrm(Dp + (2, CMP_HID, hd), CMP_HID ** -0.5),
        "w_br_rw": nrm(Dp + (RW_DIM, D_MODEL), RW_DIM ** -0.5),
        "w_br_nsa": nrm(Dp + (NSA_DIM, D_MODEL), NSA_DIM ** -0.5),
        "w_out": nrm(Dp + (D_MODEL, D_MODEL), D_MODEL ** -0.5),
        "ffn2_pre_g": gain(Dp + (D_MODEL,)),
        "ffn2_post_g": gain(Dp + (D_MODEL,)),
        "ffn2_w1": nrm(Dp + (D_MODEL, D_FF), D_MODEL ** -0.5),
        "ffn2_w3": nrm(Dp + (D_MODEL, D_FF), D_MODEL ** -0.5),
        "ffn2_w2": nrm(Dp + (D_FF, D_MODEL), D_FF ** -0.5),
        "rel_bias": nrm((N_BUCKETS, NSA_HEADS), 0.5),
    }


def reference(x_prompt, x_sample, cache_kv, cache_win, state_rwkv, state_shift, page_table,
              ffn1_pre_g, ffn1_post_g, ffn1_w1, ffn1_w3, ffn1_w2, mix_pre_g, mix_post_g, w_in,
              rw_mu, rw_w0, rw_w2, rw_a0, rw_a2, rw_g2, rw_k_k, rw_k_a, rw_r_k, rw_lnx_w, rw_lnx_b,
              cmp_pe, cmp_w1, cmp_w2, w_br_rw, w_br_nsa, w_out,
              ffn2_pre_g, ffn2_post_g, ffn2_w1, ffn2_w3, ffn2_w2, rel_bias):
    G, hd = NSA_KV_HEADS, NSA_HEAD_DIM
    b_p = x_prompt.shape[0]
    b_s, n_pages = page_table.shape
    past_len = n_pages * cache_kv.shape[2]
    y_prompt, y_sample = x_prompt, x_sample
    kvp, kvs, wp, ws, rp, rs, sp, ss = [], [], [], [], [], [], [], []
    for l in range(DEPTH):
        lw = tuple(w[l] for w in (ffn1_pre_g, ffn1_post_g, ffn1_w1, ffn1_w3, ffn1_w2, mix_pre_g, mix_post_g, w_in,
                                  rw_mu, rw_w0, rw_w2, rw_a0, rw_a2, rw_g2, rw_k_k, rw_k_a, rw_r_k, rw_lnx_w, rw_lnx_b,
                                  cmp_pe, cmp_w1, cmp_w2, w_br_rw, w_br_nsa, w_out,
                                  ffn2_pre_g, ffn2_post_g, ffn2_w1, ffn2_w3, ffn2_w2))
        y_prompt, kv_p, win_p, rw_p, sh_p = hybrid_layer(
            y_prompt, 0,
            jnp.zeros((b_p, 0, 4, G, hd), x_prompt.dtype),
            jnp.zeros((b_p, 0, 2, G, hd), x_prompt.dtype),
            jnp.zeros((b_p, RW_HEADS, RW_HEAD_DIM, RW_HEAD_DIM), jnp.float32),
            jnp.zeros((b_p, RW_PROJ), x_prompt.dtype), lw, rel_bias)
        past_kv = cache_kv[l][page_table].reshape(b_s, past_len, 4, G, hd)
        y_sample, kv_s, win_s, rw_s, sh_s = hybrid_layer(
            y_sample, past_len, past_kv, cache_win[l], state_rwkv[l], state_shift[l], lw, rel_bias)
        kvp.append(kv_p); kvs.append(kv_s); wp.append(win_p); ws.append(win_s)
        rp.append(rw_p); rs.append(rw_s); sp.append(sh_p); ss.append(sh_s)
    return (y_prompt, y_sample, jnp.stack(kvp), jnp.stack(kvs), jnp.stack(wp), jnp.stack(ws),
            jnp.stack(rp), jnp.stack(rs), jnp.stack(sp), jnp.stack(ss))
```

```python
import numpy as np
from contextlib import ExitStack
import concourse.bass as bass
import concourse.mybir as mybir
from concourse.bass_utils import run_bass_kernel_spmd

F32 = mybir.dt.float32
BF16 = mybir.dt.bfloat16
I32 = mybir.dt.int32
AF = mybir.ActivationFunctionType
ALU = mybir.AluOpType
AX = mybir.AxisListType

D = 2048
DFF = 5632
SEQ = 2048
NCORE = 8
RW_DIM = 1024
RW_PROJ = 3328
IN_COLS = 10008
KC = D // 128
EPS = 1e-6

ENGS = ("pe", "act", "dve", "pool", "sp")
SAME_ENGINE_SYNC = True


class Buf:
    def __init__(self, name, h):
        self.name = name
        self.h = h
        self.last_ws = []
        self.readers = []
        self.group_deps = []
        self.sem = None
        self.sem_count = 0
        self.semholder = self

    def __getitem__(self, idx):
        return self.h[idx]


class SemHolder:
    def __init__(self, name):
        self.name = name
        self.sem = None
        self.sem_count = 0


class _AliasBuf:
    def __init__(self, base, view):
        object.__setattr__(self, "_base", base)
        object.__setattr__(self, "h", view.h)

    def __getattr__(self, k):
        return getattr(object.__getattribute__(self, "_base"), k)

    def __setattr__(self, k, v):
        setattr(object.__getattribute__(self, "_base"), k, v)

    def __getitem__(self, idx):
        return object.__getattribute__(self, "h")[idx]


def _alias(base, view):
    return _AliasBuf(base, view)


class Op:
    __slots__ = ("eng", "fn", "deps", "is_dma", "sem_buf", "count", "signaled", "value", "idx")


class Prog:
    def __init__(self, nc, stack):
        self.nc = nc
        self.stack = stack
        self.eng_ops = {e: [] for e in ENGS}
        self.bufs = []
        self.nops = 0

    def sbuf(self, name, shape, dtype):
        h = self.stack.enter_context(self.nc.sbuf_tensor(name, list(shape), dtype))
        b = Buf(name, h)
        self.bufs.append(b)
        return b

    def psum(self, name, shape, dtype):
        h = self.stack.enter_context(self.nc.psum_tensor(name, list(shape), dtype))
        b = Buf(name, h)
        self.bufs.append(b)
        return b

    def dram(self, name, shape, dtype, kind):
        h = self.nc.dram_tensor(name, list(shape), dtype, kind=kind)
        b = Buf(name, h.ap())
        b.t = h
        self.bufs.append(b)
        return b

    def op(self, eng, fn, reads=(), writes=(), dma=False, sem_buf=None):
        o = Op()
        o.eng, o.fn, o.is_dma, o.signaled, o.value = eng, fn, dma, False, None
        o.idx = self.nops
        self.nops += 1
        deps = []
        for b in reads:
            deps.extend(b.last_ws)
        for b in writes:
            concurrent = (dma and b.last_ws and not b.readers and all(w.is_dma for w in b.last_ws))
            if concurrent:
                deps.extend(b.group_deps)
            else:
                deps.extend(b.last_ws)
                deps.extend(b.readers)
        seen = set()
        o.deps = []
        for d in deps:
            if id(d) in seen:
                continue
            seen.add(id(d))
            if (not d.is_dma) and d.eng == eng:
                if eng == "pe" or eng == "sp" or not SAME_ENGINE_SYNC:
                    continue
            o.deps.append(d)
        for b in reads:
            b.readers.append(o)
        for b in writes:
            concurrent = (dma and b.last_ws and not b.readers and all(w.is_dma for w in b.last_ws))
            if concurrent:
                b.last_ws.append(o)
            else:
                b.group_deps = list(b.last_ws) + list(b.readers)
                b.last_ws = [o]
                b.readers = []
        if dma:
            if sem_buf is None:
                raise ValueError("dma needs sem_buf")
            hold = sem_buf.semholder
            o.sem_buf = hold
            hold.sem_count += 16
            o.count = hold.sem_count
        self.eng_ops[eng].append(o)
        return o

    def emit(self):
        nc = self.nc
        for e in ENGS:
            for o in self.eng_ops[e]:
                for d in o.deps:
                    if not d.is_dma:
                        d.signaled = True
        for e in ENGS:
            n = 0
            for o in self.eng_ops[e]:
                if o.is_dma:
                    continue
                if o.signaled:
                    n += 1
                    o.value = n
        esem = {e: self.stack.enter_context(nc.semaphore("es_" + e)) for e in ENGS}
        holders, seen = [], set()
        for b in self.bufs:
            h = b.semholder
            if id(h) not in seen and h.sem_count > 0:
                seen.add(id(h))
                holders.append(h)
        for h in holders:
            h.sem = self.stack.enter_context(nc.semaphore("ds_" + h.name))
        dma_bufs = holders
        prog = self

        def run_engine(e, eng):
            waited = {}
            for o in prog.eng_ops[e]:
                need = {}
                for d in o.deps:
                    if d.is_dma:
                        s, v = d.sem_buf.sem, d.count
                    else:
                        s, v = esem[d.eng], d.value
                    key = id(s)
                    if key not in need or need[key][1] < v:
                        need[key] = (s, v)
                for key, (s, v) in need.items():
                    if waited.get(key, 0) >= v:
                        continue
                    waited[key] = v
                    eng.wait_ge(s, v)
                ins = o.fn(eng)
                if o.is_dma:
                    ins.then_inc(o.sem_buf.sem, 16)
                elif o.signaled:
                    ins.then_inc(esem[e], 1)
            if e == "sp":
                for b in dma_bufs:
                    eng.wait_ge(b.sem, b.sem_count)

        with nc.Block() as block:
            @block.tensor
            def _(eng):
                run_engine("pe", eng)

            @block.scalar
            def _(eng):
                run_engine("act", eng)

            @block.vector
            def _(eng):
                run_engine("dve", eng)

            @block.gpsimd
            def _(eng):
                run_engine("pool", eng)

            @block.sync
            def _(eng):
                run_engine("sp", eng)


class K:
    def __init__(self, P):
        self.P = P

    def dma(self, dst_ap, src_ap, reads, writes, sem_buf, eng="sp", nc_ok=False):
        def fn(e):
            if nc_ok:
                return e.dma_start(out=dst_ap, in_=src_ap, allow_slow_non_contiguous=True)
            return e.dma_start(out=dst_ap, in_=src_ap)
        return self.P.op(eng, fn, reads, writes, dma=True, sem_buf=sem_buf)

    def mm(self, out_ap, lhsT_ap, rhs_ap, start, stop, reads, writes):
        return self.P.op("pe", lambda e: e.matmul(out_ap, lhsT_ap, rhs_ap, start=start, stop=stop), reads, writes)

    def tr(self, out_ap, in_ap, ident_ap, reads, writes):
        return self.P.op("pe", lambda e: e.transpose(out_ap, in_ap, ident_ap), reads, writes)

    def act(self, out_ap, in_ap, func, reads, writes, bias=None, scale=None, accum=None):
        def fn(e):
            kw = {}
            if bias is not None:
                kw["bias"] = bias
            if scale is not None:
                kw["scale"] = scale
            if accum is not None:
                kw["accum_out"] = accum
            return e.activation(out_ap, in_ap, func, **kw)
        return self.P.op("act", fn, reads, writes)

    def tt(self, out_ap, a_ap, b_ap, op, reads, writes, eng="dve"):
        return self.P.op(eng, lambda e: e.tensor_tensor(out_ap, a_ap, b_ap, op), reads, writes)

    def ts(self, out_ap, a_ap, s1, s2, op0, op1, reads, writes, eng="dve"):
        if op1 is None:
            return self.P.op(eng, lambda e: e.tensor_scalar(out_ap, a_ap, s1, None, op0), reads, writes)
        return self.P.op(eng, lambda e: e.tensor_scalar(out_ap, a_ap, s1, s2, op0, op1), reads, writes)

    def stt(self, out_ap, a_ap, s, b_ap, op0, op1, reads, writes, eng="dve"):
        return self.P.op(eng, lambda e: e.scalar_tensor_tensor(out_ap, a_ap, s, b_ap, op0, op1), reads, writes)

    def copy(self, out_ap, in_ap, reads, writes, eng="dve"):
        if eng == "act":
            return self.P.op("act", lambda e: e.copy(out_ap, in_ap), reads, writes)
        return self.P.op(eng, lambda e: e.tensor_copy(out_ap, in_ap), reads, writes)

    def reduce(self, out_ap, in_ap, op, reads, writes):
        return self.P.op("dve", lambda e: e.tensor_reduce(out_ap, in_ap, AX.X, op), reads, writes)

    def recip(self, out_ap, in_ap, reads, writes):
        return self.P.op("dve", lambda e: e.reciprocal(out_ap, in_ap), reads, writes)

    def memset(self, ap, val, writes, eng="dve"):
        return self.P.op(eng, lambda e: e.memset(ap, val), (), writes)


class Arena:
    def __init__(self, P, name, words):
        self.P = P
        self.t = P.stack.enter_context(P.nc.sbuf_tensor(name, [128, words], F32))
        self.words = words
        self.off = 0
        self.live = []
        self.pending = []
        self.n = 0
        self.holders = {}

    def reset(self):
        ops = list(self.pending)
        for b in self.live:
            ops += b.last_ws + b.readers
        best, dmas, seen = {}, [], set()
        for o in ops:
            if o.is_dma:
                if id(o) not in seen:
                    seen.add(id(o))
                    dmas.append(o)
            elif o.eng not in best or o.idx > best[o.eng].idx:
                best[o.eng] = o
        bd = {}
        for o in dmas:
            k = id(o.sem_buf)
            if k not in bd or o.count > bd[k].count:
                bd[k] = o
        self.pending = list(best.values()) + list(bd.values())
        self.live = []
        self.off = 0

    def reset_from(self, off):
        keep = [b for b in self.live if b._off < off]
        drop = [b for b in self.live if b._off >= off]
        self.live = drop
        save_pending = self.pending
        self.reset()
        self.live = keep
        self.off = off

    def alloc(self, name, words, view=None):
        assert self.off + words <= self.words, (name, self.off, words, self.words)
        ap = self.t[:, self.off:self.off + words]
        off0 = self.off
        self.off += words
        if view is not None:
            ap = view(ap)
        self.n += 1
        b = Buf("%s_%d" % (name, self.n), ap)
        if name not in self.holders:
            self.holders[name] = SemHolder("ar_" + name)
        b.semholder = self.holders[name]
        b._off = off0
        b.last_ws = list(self.pending)
        self.P.bufs.append(b)
        self.live.append(b)
        return b


class WeightStream:
    def __init__(self, kb, nslots, slot_elems):
        self.kb = kb
        self.base = [kb.P.sbuf("wslot%d" % i, [128, slot_elems], BF16) for i in range(nslots)]
        self.extra = []
        self.extra_tag = None
        self.plan = []
        self.where = {}
        self.held = {}
        self.issued = 0
        self.taken = 0

    def add(self, fn, tag=None):
        self.plan.append((fn, tag))

    def set_extra(self, slots, tag):
        self.extra = list(slots)
        self.extra_tag = tag

    def clear_extra(self):
        for s_ in self.extra:
            b = self.held.get(id(s_))
            assert b is None or b < self.taken, "extra slot still holds an untaken block"
            self.held.pop(id(s_), None)
        self.extra = []
        self.extra_tag = None

    def prefetch(self):
        while self.issued < len(self.plan):
            fn, tag = self.plan[self.issued]
            cands = self.base + (self.extra if (tag is not None and tag == self.extra_tag) else [])
            slot = None
            for c in cands:
                b = self.held.get(id(c))
                if b is None or b < self.taken - 1:
                    slot = c
                    break
            if slot is None:
                break
            for dst_ap, src_ap in fn(slot):
                self.kb.dma(dst_ap, src_ap, (), (slot,), slot, eng="pool")
            self.held[id(slot)] = self.issued
            self.where[self.issued] = slot
            self.issued += 1

    def take(self):
        self.prefetch()
        assert self.taken in self.where, "weight block not issued"
        slot = self.where.pop(self.taken)
        self.taken += 1
        return slot


def build(cfg):
    nc = bass.Bass("TRN2", target_bir_lowering=False)
    with ExitStack() as stack:
        P = Prog(nc, stack)
        kb = K(P)
        _build_body(nc, P, kb, cfg)
        P.emit()
    return nc


def _build_body(nc, P, kb, cfg):
    tiles = cfg["tiles"]
    SEQ_ = cfg.get("seq", SEQ)
    TMAX = max([t[2] for t in tiles] + [256])
    def din(name, shape, dt=F32):
        return P.dram(name, shape, dt, "ExternalInput")

    def dout(name, shape, dt=F32):
        return P.dram(name, shape, dt, "ExternalOutput")

    xp = din("xp", [SEQ, D])
    xs = din("xs", [16, D])
    ident_d = din("ident", [128, 128])
    ones_d = din("ones", [128, 128])
    f1_pre = din("ffn1_pre_g", [D]); f1_post = din("ffn1_post_g", [D])
    f1_w1 = din("ffn1_w1", [D, DFF]); f1_w3 = din("ffn1_w3", [D, DFF]); f1_w2 = din("ffn1_w2", [DFF, D])
    mix_pre = din("mix_pre_g", [D]); mix_post = din("mix_post_g", [D])
    w_in = din("w_in", [D, IN_COLS])

    yp = dout("y_prompt", [SEQ, D])
    ys = dout("y_sample", [16, D])
    kvp = dout("kv_prompt", [SEQ, 1024])
    kvs = dout("kv_sample", [16, 1024])
    winp = dout("win_prompt", [512, 512])
    shp = dout("shift_prompt", [RW_PROJ])
    shs = dout("shift_sample", [4, RW_PROJ])
    wins = dout("win_sample", [4, 512, 512])
    cwin = din("cache_win", [4, 512, 512])
    rwp = dout("rwkv_prompt", [16, 64, 64])
    rws = dout("rwkv_sample", [4, 16, 64, 64])
    st_rw = din("state_rwkv", [4, 16, 64, 64])
    st_sh = din("state_shift", [4, RW_PROJ])
    mask_d = din("rwmask", [64, 192])
    mask4_d = din("rwmask4", [4, 12])
    selhi_d = din("selhi", [128, 64])
    identb_d = din("identb", [128, 128], BF16)
    cmp_pe = din("cmp_pe", [32, 2, 128]); cmp_w1 = din("cmp_w1", [2, 4096, 256]); cmp_w2 = din("cmp_w2", [2, 256, 128])
    rel_bias = din("rel_bias", [32, 8])
    ohs_d = din("oh_sel", [33, 2176]); ohw_d = din("oh_win", [33, 768]); ohc_d = din("oh_cmp", [33, 128 * 67])
    mc_d = din("mc", [128, 67]); bc_d = din("bc", [128, 16, 32])
    NPOOL = cfg.get("npool", 2560)
    ckv = din("cache_kv", [NPOOL, 128, 1024])
    ptab = din("page_table", [4, 64], I32)
    ohss_d = din("oh_s_sel", [33, 8704]); ohsw_d = din("oh_s_win", [33, 1024]); ohsc_d = din("oh_s_cmp", [33, 1024])
    bs_d = din("bs", [4, 136]); iotap_d = din("iotap", [128, 1])
    bones_d = din("blockones", [128, 128])
    rw_mu = din("rw_mu", [RW_PROJ]); rw_w0 = din("rw_w0", [RW_DIM]); rw_w2 = din("rw_w2", [64, RW_DIM])
    rw_a0 = din("rw_a0", [RW_DIM]); rw_a2 = din("rw_a2", [64, RW_DIM]); rw_g2 = din("rw_g2", [128, RW_DIM])
    rw_k_k = din("rw_k_k", [RW_DIM]); rw_k_a = din("rw_k_a", [RW_DIM]); rw_r_k = din("rw_r_k", [RW_DIM])
    rw_lnx_w = din("rw_lnx_w", [RW_DIM]); rw_lnx_b = din("rw_lnx_b", [RW_DIM])
    w_br_rw = din("w_br_rw", [RW_DIM, D]); w_br_nsa = din("w_br_nsa", [1024, D]); w_out = din("w_out", [D, D])
    f2_pre = din("ffn2_pre_g", [D]); f2_post = din("ffn2_post_g", [D])
    f2_w1 = din("ffn2_w1", [D, DFF]); f2_w3 = din("ffn2_w3", [D, DFF]); f2_w2 = din("ffn2_w2", [DFF, D])

    psb = [P.psum("ps%d" % i, [128, 512], F32) for i in range(8)]
    ps_i = [0]

    ps_mod = [8]

    def ps_next():
        b = psb[ps_i[0] % ps_mod[0]]
        ps_i[0] += 1
        return b

    ident = P.sbuf("ident_sb", [128, 128], F32)
    ones = P.sbuf("ones_sb", [128, 128], F32)
    kb.dma(ident[:], ident_d[:], (), (ident,), ident)
    kb.dma(ones[:], ones_d[:], (), (ones,), ones)

    vstg = P.sbuf("vstg", [32, 128], F32)
    vstg2 = P.sbuf("vstg2", [32, 128], F32)

    def load_col(dst_ap, dst_buf, src_row_ap, k):
        kb.dma(vstg[0:k, :], src_row_ap.rearrange("(k p) -> k p", p=128), (), (vstg,), vstg)
        ps = ps_next()
        kb.tr(ps[:, 0:k], vstg[0:k, :], ident[0:k, 0:k], (vstg, ident), (ps,))
        kb.copy(dst_ap, ps[:, 0:k], (ps,), (dst_buf,))

    def store_col(dst_row_ap, src_ap, src_buf, k):
        ps = ps_next()
        kb.tr(ps[0:k, 0:128], src_ap, ident[:, :], (src_buf, ident), (ps,))
        kb.copy(vstg2[0:k, :], ps[0:k, 0:128], (ps,), (vstg2,))
        kb.dma(dst_row_ap.rearrange("(k p) -> k p", p=128), vstg2[0:k, :], (vstg2,), (), vstg2)

    def load_vec(name, d_buf, n):
        t = P.sbuf(name, [128, n // 128], F32)
        load_col(t[:, :], t, d_buf.h, n // 128)
        return t

    g_f1pre = load_vec("g_f1pre", f1_pre, D)
    g_f1post = load_vec("g_f1post", f1_post, D)
    g_mixpre = load_vec("g_mixpre", mix_pre, D)
    g_mixpost = load_vec("g_mixpost", mix_post, D)
    g_f2pre = load_vec("g_f2pre", f2_pre, D)
    g_f2post = load_vec("g_f2post", f2_post, D)
    v_mu = load_vec("v_mu", rw_mu, RW_PROJ)
    v_w0 = load_vec("v_w0", rw_w0, RW_DIM); v_a0 = load_vec("v_a0", rw_a0, RW_DIM)
    v_kk = load_vec("v_kk", rw_k_k, RW_DIM); v_ka = load_vec("v_ka", rw_k_a, RW_DIM)
    v_rk = load_vec("v_rk", rw_r_k, RW_DIM)
    v_lw = load_vec("v_lw", rw_lnx_w, RW_DIM); v_lb = load_vec("v_lb", rw_lnx_b, RW_DIM)
    w2z = P.sbuf("w2z", [128, RW_DIM], F32)
    a2z = P.sbuf("a2z", [128, RW_DIM], F32)
    kb.memset(w2z[:, :], 0.0, (w2z,))
    kb.memset(a2z[:, :], 0.0, (a2z,))
    kb.dma(w2z[0:64, :], rw_w2.h[:, :], (w2z,), (w2z,), w2z)
    kb.dma(a2z[64:128, :], rw_a2.h[:, :], (a2z,), (a2z,), a2z)
    selhi = P.sbuf("selhi_sb", [128, 64], F32)
    kb.dma(selhi[:], selhi_d.h[:, :], (), (selhi,), selhi)
    g2sb = P.sbuf("g2sb", [128, RW_DIM], F32)
    kb.dma(g2sb[:], rw_g2.h[:, :], (), (g2sb,), g2sb)
    rwmask = P.sbuf("rwmask_sb", [64, 192], F32)
    kb.dma(rwmask[:], mask_d.h[:, :], (), (rwmask,), rwmask)
    identb = P.sbuf("identb_sb", [128, 128], BF16)
    kb.dma(identb[:], identb_d.h[:, :], (), (identb,), identb)
    bones = P.sbuf("bones_sb", [128, 128], F32)
    kb.dma(bones[:], bones_d.h[:, :], (), (bones,), bones)

    xT = P.sbuf("xT", [128, KC, TMAX], F32)
    hT = P.sbuf("hT", [128, KC, TMAX], BF16)
    big1 = P.sbuf("big1", [128, 26 * TMAX], F32)
    gT = Buf("gTv", big1.h[:, 0:(DFF // 128) * TMAX // 2].bitcast(BF16).rearrange("p (k t) -> p k t", k=DFF // 128))
    pT = Buf("pTv", big1.h[:, :].rearrange("p (k t) -> p k t", k=26))
    gT = _alias(big1, gT); pT = _alias(big1, pT)
    mergedT = Buf("mTv", big1.h[:, 0:KC * TMAX // 2].bitcast(BF16).rearrange("p (k t) -> p k t", k=KC))
    mergedT = _alias(big1, mergedT)
    arena = Arena(P, "arena", cfg.get("arena_words", 17408))
    stage_box = [None]
    rstd = P.sbuf("rstd", [128, TMAX], F32)
    sq = P.sbuf("sq", [128, TMAX], F32)
    tmpA = P.sbuf("tmpA", [128, TMAX], F32)
    SLOT = 16 * 256
    ws = WeightStream(kb, 3, SLOT)

    def plan_ffn(w1, w3, w2, tag=None):
        w1v = w1.h.rearrange("(k p) n -> p k n", p=128)
        w3v = w3.h.rearrange("(k p) n -> p k n", p=128)
        w2v = w2.h.rearrange("(k p) n -> p k n", p=128)
        for j in range(DFF // 256):
            for wv in (w1v, w3v):
                def fn(slot, wv=wv, j=j):
                    dst = slot[:, 0:16 * 256].rearrange("p (k n) -> p k n", k=16)
                    return [(dst, wv[:, :, j * 256:(j + 1) * 256])]
                ws.add(fn, tag)
        for mb in range(D // 256):
            for half in range(4):
                def fn(slot, mb=mb, half=half):
                    dst = slot[:, 0:11 * 256].rearrange("p (k n) -> p k n", k=11)
                    return [(dst, w2v[:, half * 11:(half + 1) * 11, mb * 256:(mb + 1) * 256])]
                ws.add(fn, tag)

    WIN_BLOCKS = [(c0, min(256, 5888 - c0)) for c0 in range(0, 5888, 256)]

    def plan_win_a():
        wv = w_in.h.rearrange("(k p) n -> p k n", p=128)
        for (c0, w) in WIN_BLOCKS:
            def fn(slot, c0=c0, w=w):
                dst = slot[:, 0:16 * w].rearrange("p (k n) -> p k n", k=16)
                return [(dst, wv[:, :, c0:c0 + w])]
            ws.add(fn)

    def plan_merge():
        wv = w_in.h.rearrange("(k p) n -> p k n", p=128)
        brv = [w_br_rw.h.rearrange("(k p) n -> p k n", p=128), w_br_nsa.h.rearrange("(k p) n -> p k n", p=128)]
        wov = w_out.h.rearrange("(k p) n -> p k n", p=128)
        for mb in range(D // 256):
            for bi in range(2):
                c0 = (5912 if bi == 0 else 7960) + mb * 256

                def fn(slot, c0=c0):
                    dst = slot[:, 0:16 * 256].rearrange("p (k n) -> p k n", k=16)
                    return [(dst, wv[:, :, c0:c0 + 256])]
                ws.add(fn)

                def fn2(slot, bi=bi, mb=mb):
                    dst = slot[:, 0:8 * 256].rearrange("p (k n) -> p k n", k=8)
                    return [(dst, brv[bi][:, :, mb * 256:(mb + 1) * 256])]
                ws.add(fn2)
        for mb in range(D // 256):
            def fn3(slot, mb=mb):
                dst = slot[:, 0:16 * 256].rearrange("p (k n) -> p k n", k=16)
                return [(dst, wov[:, :, mb * 256:(mb + 1) * 256])]
            ws.add(fn3)

    def plan_cmp():
        for kv_ in range(2):
            for half in range(2):
                def fn(slot, kv_=kv_, half=half):
                    dst = slot[:, 0:16 * 256].rearrange("p (r c) -> p r c", r=16)
                    src = cmp_w1.h[kv_, half * 2048:(half + 1) * 2048, :].rearrange("(r p) c -> p r c", p=128)
                    return [(dst, src)]
                ws.add(fn)

    for ti_, (kind_, _, _) in enumerate(tiles):
        plan_ffn(f1_w1, f1_w3, f1_w2, (ti_, 1))
        plan_win_a()
        if kind_ == "p" and not cfg.get("skip_nsa"):
            plan_cmp()
        plan_merge()
        plan_ffn(f2_w1, f2_w3, f2_w2, (ti_, 2))

    def rms_stats(src_chunk_ap, src_bufs, T):
        ps = ps_next()
        for k in range(KC):
            kb.act(sq[:, :T], src_chunk_ap(k), AF.Square, src_bufs, (sq,))
            kb.mm(ps[:, :T], ones[:, :], sq[:, :T], k == 0, k == KC - 1, (ones, sq), (ps,))
        kb.ts(sq[:, :T], ps[:, :T], 1.0 / D, EPS, ALU.mult, ALU.add, (ps,), (sq,))
        kb.act(sq[:, :T], sq[:, :T], AF.Sqrt, (sq,), (sq,))
        P.op("dve", lambda e: e.reciprocal(rstd[:, :T], sq[:, :T]), (sq,), (rstd,))

    def pre_norm(gvec, T):
        rms_stats(lambda k: xT[:, k, :T], (xT,), T)
        for k in range(KC):
            kb.stt(hT[:, k, :T], xT[:, k, :T], gvec[:, k:k + 1], rstd[:, :T], ALU.mult, ALU.mult,
                   (xT, gvec, rstd), (hT,))

    def post_residual(fT, gpost, coef, T):
        rms_stats(lambda k: fT[:, k * TMAX:k * TMAX + T], (fT,), T)
        for k in range(KC):
            kb.stt(tmpA[:, :T], fT[:, k * TMAX:k * TMAX + T], gpost[:, k:k + 1], rstd[:, :T], ALU.mult, ALU.mult,
                   (fT, gpost, rstd), (tmpA,))
            kb.stt(xT[:, k, :T], tmpA[:, :T], coef, xT[:, k, :T], ALU.mult, ALU.add, (tmpA, xT), (xT,))

    def ffn(gpre, gpost, T, second=False, tag=None):
        fT = stage_box[0]
        nx = (arena.words - arena.off) // (SLOT // 2)
        if tag is not None and nx > 0 and not cfg.get("no_extra_slots"):
            ws.set_extra([arena.alloc("wsx%d" % i, SLOT // 2, view=lambda a: a.bitcast(BF16)) for i in range(nx)], tag)
        pre_norm(gpre, T)
        for j in range(DFF // 256):
            s1 = ws.take()
            s3 = ws.take()
            v1 = s1[:, 0:16 * 256].rearrange("p (k n) -> p k n", k=16)
            v3 = s3[:, 0:16 * 256].rearrange("p (k n) -> p k n", k=16)
            for mi in range(2):
                p1 = ps_next()
                p3 = ps_next()
                for k in range(KC):
                    kb.mm(p1[:, :T], v1[:, k, mi * 128:(mi + 1) * 128], hT[:, k, :T], k == 0, k == KC - 1, (s1, hT), (p1,))
                for k in range(KC):
                    kb.mm(p3[:, :T], v3[:, k, mi * 128:(mi + 1) * 128], hT[:, k, :T], k == 0, k == KC - 1, (s3, hT), (p3,))
                kb.act(tmpA[:, :T], p1[:, :T], AF.Silu, (p1,), (tmpA,))
                kb.tt(gT[:, j * 2 + mi, :T], tmpA[:, :T], p3[:, :T], ALU.mult, (tmpA, p3), (gT,))
        for mb in range(D // 256):
            pa = ps_next()
            pb = ps_next()
            for half in range(4):
                s2 = ws.take()
                v2 = s2[:, 0:11 * 256].rearrange("p (k n) -> p k n", k=11)
                for mi, pp in ((0, pa), (1, pb)):
                    for k in range(11):
                        kk = half * 11 + k
                        kb.mm(pp[:, :T], v2[:, k, mi * 128:(mi + 1) * 128], gT[:, kk, :T], kk == 0, kk == 43, (s2, gT), (pp,))
            for mi, pp in ((0, pa), (1, pb)):
                m = mb * 2 + mi
                kb.copy(fT[:, m * TMAX:m * TMAX + T], pp[:, :T], (pp,), (fT,), eng="act")
        ws.clear_extra()
        post_residual(fT, gpost, 0.5, T)

    def load_x(kind, t0, T):
        src = xp if kind == "p" else xs
        stage = stage_box[0]
        nsub = (T + 127) // 128
        st = stage[:, 0:nsub * D].rearrange("p (s f) -> p s f", s=nsub)
        rows = min(T, 128)
        if T >= 128:
            kb.dma(st, src.h[t0:t0 + T, :].rearrange("(s p) f -> p s f", p=128), (), (stage,), stage)
        else:
            kb.dma(stage[0:T, 0:D], src.h[t0:t0 + T, :], (), (stage,), stage)
        for k in range(KC):
            ps = ps_next()
            for s in range(nsub):
                kb.tr(ps[:, s * 128:s * 128 + rows], st[0:rows, s, k * 128:(k + 1) * 128], ident[0:rows, 0:rows],
                      (stage, ident), (ps,))
            kb.copy(xT[:, k, :T], ps[:, :T], (ps,), (xT,), eng=("act" if k % 2 else "dve"))

    def store_tokmajor(dst_rows_ap, src_chunk_ap, src_bufs, nchunks, T):
        stage = stage_box[0]
        nsub = (T + 127) // 128
        rows = min(T, 128)
        W = nchunks * 128
        st = stage[:, 0:nsub * W].rearrange("p (s f) -> p s f", s=nsub)
        for s in range(nsub):
            for c0 in range(0, nchunks, 4):
                ps = ps_next()
                nn = min(4, nchunks - c0)
                for c in range(nn):
                    kb.tr(ps[0:rows, c * 128:(c + 1) * 128], src_chunk_ap(c0 + c)[:, s * 128:s * 128 + rows], ident[:, :],
                          src_bufs + (ident,), (ps,))
                kb.copy(st[0:rows, s, c0 * 128:(c0 + nn) * 128], ps[0:rows, 0:nn * 128], (ps,), (stage,),
                        eng=("act" if (c0 // 4) % 2 else "dve"))
        if dst_rows_ap is None:
            return
        if T >= 128:
            kb.dma(dst_rows_ap.rearrange("(s p) f -> p s f", p=128), st, (stage,), (), stage)
        else:
            kb.dma(dst_rows_ap, stage[0:T, 0:W], (stage,), (), stage)

    qT = P.sbuf("qT", [128, 8, TMAX], BF16)
    yrwT = P.sbuf("yrwT", [128, 8, TMAX], BF16)
    ynsaT = P.sbuf("ynsaT", [128, 8, TMAX], BF16)
    carry = [P.sbuf("carry0", [128, 26], F32), P.sbuf("carry1", [128, 26], F32)]
    S0T = P.sbuf("S0T", [64, 16, 64], F32)
    kb.memset(carry[0][:, :], 0.0, (carry[0],))
    kb.memset(S0T[:, :, :], 0.0, (S0T,))
    kb.memset(ynsaT[:, :, :], 0.0, (ynsaT,))
    rwmask4 = P.sbuf("rwmask4_sb", [4, 12], F32)
    kb.dma(rwmask4[:], mask4_d.h[:, :], (), (rwmask4,), rwmask4)

    def win_proj_a(T, kvT):
        for (c0, w) in WIN_BLOCKS:
            s_ = ws.take()
            v = s_[:, 0:16 * w].rearrange("p (k n) -> p k n", k=16)
            for mi in range(w // 128):
                ps = ps_next()
                for k in range(KC):
                    kb.mm(ps[:, :T], v[:, k, mi * 128:(mi + 1) * 128], hT[:, k, :T], k == 0, k == KC - 1, (s_, hT), (ps,))
                ci = c0 // 128 + mi
                eng = "act" if ci % 2 else "dve"
                if ci < 26:
                    kb.copy(pT[:, ci, :T], ps[:, :T], (ps,), (pT,), eng=eng)
                elif ci < 34:
                    kb.copy(qT[:, ci - 26, :T], ps[:, :T], (ps,), (qT,), eng=eng)
                else:
                    kb.copy(kvT[:, ci - 34, :T], ps[:, :T], (ps,), (kvT,), eng=eng)

    def bc3(ap2, n):
        return ap2.unsqueeze(2).to_broadcast([ap2.shape[0], ap2.shape[1], n])

    def bcmid(ap2, n):
        return ap2.unsqueeze(1).to_broadcast([ap2.shape[0], n, ap2.shape[1]])

    def ts1(out_ap, in_ap, scalar, op, reads, writes, eng="dve"):
        return P.op(eng, lambda e: e.tensor_single_scalar(out_ap, in_ap, scalar, op), reads, writes)

    def rwkv(kind, t0, T, tile_idx):
        A = arena
        C = 64 if kind == "p" else 4
        nch = T // C
        nd = 5 if C == 64 else 1
        mk = rwmask if C == 64 else rwmask4
        mkU = mk[0:C, 0:2 * C]
        mkL = mk[0:C, 2 * C:3 * C]

        def fm(nm):
            return A.alloc(nm, 512, view=lambda a: a.rearrange("p (k c) -> p k c", k=8))

        logw = fm("logw"); a_ = fm("a_"); kk_ = fm("kk_")
        Eg = fm("Eg"); Einv = fm("Einv"); BtT = fm("BtT"); KtT = fm("KtT"); tmp = fm("tmp"); tmp2 = fm("tmp2")
        La, Lb = tmp, tmp2
        gate_, bonus_ = kk_, a_
        AR = A.alloc("AR", 1024, view=lambda a: a.rearrange("p (k two c) -> p k two c", k=8, two=2))
        ARH = A.alloc("ARH", 1024, view=lambda a: a.rearrange("p (k two c) -> p k two c", k=8, two=2))
        BtH = fm("BtH"); KtH = fm("KtH")
        gh = A.alloc("gh", 8)
        tw = A.alloc("tw", 64); sgd = A.alloc("sgd", 64)

        def tm(nm, w=64):
            return A.alloc(nm, 8 * w, view=lambda a: a.rearrange("p (h c) -> p h c", h=8))

        Vtok = tm("Vtok"); Bttok = tm("Bttok"); Kttok = tm("Kttok"); XT = tm("XT"); UT = tm("UT")
        Nm = tm("Nm"); NT = tm("NT"); Pa = tm("Pa"); PTa = tm("PTa"); Rm = tm("Rm")
        Yt, Yc = XT, Pa
        mb_off = A.off
        MBm = tm("MBm", 128); MKm = tm("MKm", 128)
        nat16 = _alias(MBm, Buf("nat16v", A.t[:, mb_off:mb_off + 1024].rearrange("p (h c) -> p h c", h=16)))
        st8 = A.alloc("st8", 32, view=lambda a: a.rearrange("p (h c) -> p h c", h=8))
        lvl = cfg.get("rw_stop", 99)

        def fmop(X, XH, h):
            return (X if h % 2 == 0 else XH), h // 2

        cur, nxt = carry[tile_idx % 2], carry[(tile_idx + 1) % 2]
        dtmp = tmpA
        if kind == "p":
            kb.copy(nxt[:, :], pT[:, :, T - 1], (pT,), (nxt,))
            for c in range(26):
                kb.tt(dtmp[:, 1:T], pT[:, c, 0:T - 1], pT[:, c, 1:T], ALU.subtract, (pT,), (dtmp,))
                kb.tt(dtmp[:, 0:1], cur[:, c:c + 1], pT[:, c, 0:1], ALU.subtract, (pT, cur), (dtmp,))
                kb.stt(pT[:, c, :T], dtmp[:, :T], v_mu[:, c:c + 1], pT[:, c, :T], ALU.mult, ALU.add, (dtmp, v_mu, pT), (pT,))
            if t0 + T == SEQ_:
                store_col(shp.h, nxt[:, :], nxt, 26)
        else:
            sh0 = A.alloc("sh0", 26 * 4, view=lambda a: a.rearrange("p (k s) -> p k s", k=26))
            sho = A.alloc("sho", 26 * 4, view=lambda a: a.rearrange("p (s k) -> p s k", s=4))
            for sq_i in range(4):
                load_col(sh0[:, :, sq_i], sh0, st_sh.h[sq_i, :], 26)
            p4 = pT[:, :, 0:16].rearrange("p k (s t) -> p k s t", t=4)
            for sq_i in range(4):
                kb.copy(sho[:, sq_i, :], p4[:, :, sq_i, 3], (pT,), (sho,))
                store_col(shs.h[sq_i, :], sho[:, sq_i, :], sho, 26)
            d4 = dtmp[:, 0:16].rearrange("p (s t) -> p s t", t=4)
            for c in range(26):
                kb.tt(d4[:, :, 1:4], p4[:, c, :, 0:3], p4[:, c, :, 1:4], ALU.subtract, (pT,), (dtmp,))
                kb.tt(d4[:, :, 0], sh0[:, c, :], p4[:, c, :, 0], ALU.subtract, (pT, sh0), (dtmp,))
                kb.stt(pT[:, c, :T], dtmp[:, :T], v_mu[:, c:c + 1], pT[:, c, :T], ALU.mult, ALU.add, (dtmp, v_mu, pT), (pT,))
        if lvl <= 1:
            return

        def heads_T(src, dst, src_bufs, dst_bufs):
            for g in range(2):
                ps = ps_next()
                for q in range(8):
                    kb.tr(ps[0:64, q * 64:(q + 1) * 64], src[0:64, g * 8 + q, :], ident[0:64, 0:64], src_bufs + (ident,), (ps,))
                kb.copy(dst[0:64, g * 8:g * 8 + 8, :], ps[0:64, :].rearrange("p (h c) -> p h c", h=8), (ps,), dst_bufs,
                        eng=("act" if g else "dve"))

        for ci in range(nch):
            cs = slice(ci * C, (ci + 1) * C)
            if kind == "s":
                kb.dma(nat16[0:64, :, :], st_rw.h[ci].rearrange("h i j -> i h j"), (), (nat16,), nat16)
                heads_T(nat16, S0T, (nat16,), (S0T,))
            r_ = pT[:, 0:8, cs]; k_ = pT[:, 8:16, cs]; v_ = pT[:, 16:24, cs]
            f3 = lambda b: b[:, :, 0:C]
            kb.act(tw[:, 0:C], pT[:, 24, cs], AF.Tanh, (pT,), (tw,))
            kb.act(sgd[:, 0:C], pT[:, 25, cs], AF.Sigmoid, (pT,), (sgd,))
            for m in range(8):
                ps = ps_next()
                msl = slice(m * 128, (m + 1) * 128)
                kb.mm(ps[:, 0:C], w2z[:, msl], tw[:, 0:C], True, True, (w2z, tw), (ps,))
                kb.act(logw[:, m, 0:C], ps[:, 0:C], AF.Sigmoid, (ps, v_w0), (logw,), bias=v_w0[:, m:m + 1])
                ps = ps_next()
                kb.mm(ps[:, 0:C], a2z[:, msl], pT[:, 24, cs], True, True, (a2z, pT), (ps,))
                kb.act(a_[:, m, 0:C], ps[:, 0:C], AF.Sigmoid, (ps, v_a0), (a_,), bias=v_a0[:, m:m + 1])
            if lvl <= 2:
                return
            kb.tt(f3(kk_), k_, bc3(v_kk[:, 0:8], C), ALU.mult, (pT, v_kk), (kk_,))
            kb.tt(f3(tmp), f3(kk_), f3(kk_), ALU.mult, (kk_,), (tmp,))
            ps = ps_next()
            for m in range(8):
                kb.mm(ps[:, m * 64:m * 64 + C], bones[:, :], tmp[:, m, 0:C], True, True, (bones, tmp), (ps,))
            psv = ps[:, :].rearrange("p (k c) -> p k c", k=8)[:, :, 0:C]
            kb.act(f3(tmp), psv, AF.Sqrt, (ps,), (tmp,))
            ts1(f3(tmp), f3(tmp), 1e-12, ALU.max, (tmp,), (tmp,))
            P.op("dve", lambda e: e.reciprocal(f3(tmp), f3(tmp)), (tmp,), (tmp,))
            kb.tt(f3(kk_), f3(kk_), f3(tmp), ALU.mult, (kk_, tmp), (kk_,))
            ts1(f3(tmp), f3(a_), 1.0, ALU.subtract, (a_,), (tmp,))
            kb.tt(f3(tmp), f3(tmp), bc3(v_ka[:, 0:8], C), ALU.mult, (tmp, v_ka), (tmp,))
            kb.stt(k_, f3(tmp), 1.0, k_, ALU.add, ALU.mult, (tmp, pT), (pT,))
            kb.tt(f3(a_), f3(kk_), f3(a_), ALU.mult, (kk_, a_), (a_,))
            ts1(f3(logw), f3(logw), -float(np.exp(-0.5)), ALU.mult, (logw,), (logw,))
            src, dst = logw, La
            sh = 1
            while sh < C:
                kb.copy(dst[:, :, 0:sh], src[:, :, 0:sh], (src,), (dst,), eng="act")
                kb.tt(dst[:, :, sh:C], src[:, :, sh:C], src[:, :, 0:C - sh], ALU.add, (src,), (dst,))
                src, dst = dst, (Lb if dst is La else La)
                sh *= 2
            L = src
            assert L is tmp2
            kb.act(f3(Eg), f3(L), AF.Exp, (L,), (Eg,))
            kb.act(f3(Einv), f3(L), AF.Exp, (L,), (Einv,), scale=-1.0)
            kb.tt(f3(tmp), f3(L), f3(logw), ALU.subtract, (L, logw), (tmp,))
            kb.act(f3(tmp), f3(tmp), AF.Exp, (tmp,), (tmp,))
            kb.tt(AR[:, :, 0, 0:C], f3(kk_), f3(tmp), ALU.mult, (kk_, tmp), (AR,))
            kb.tt(AR[:, :, 1, 0:C], r_, f3(Eg), ALU.mult, (pT, Eg), (AR,))
            kb.tt(f3(BtT), f3(a_), f3(Einv), ALU.mult, (a_, Einv), (BtT,))
            kb.tt(f3(KtT), k_, f3(Einv), ALU.mult, (pT, Einv), (KtT,))
            ps = ps_next()
            for m in range(8):
                kb.mm(ps[:, m * 64:m * 64 + C], g2sb[:, m * 128:(m + 1) * 128], sgd[:, 0:C], True, True, (g2sb, sgd), (ps,))
            kb.copy(f3(gate_), ps[:, :].rearrange("p (k c) -> p k c", k=8)[:, :, 0:C], (ps,), (gate_,), eng="act")
            kb.tt(f3(tmp), r_, k_, ALU.mult, (pT,), (tmp,))
            kb.tt(f3(tmp), f3(tmp), bc3(v_rk[:, 0:8], C), ALU.mult, (tmp, v_rk), (tmp,))
            ps = ps_next()
            for m in range(8):
                kb.mm(ps[:, m * 64:m * 64 + C], bones[:, :], tmp[:, m, 0:C], True, True, (bones, tmp), (ps,))
            kb.tt(f3(bonus_), ps[:, :].rearrange("p (k c) -> p k c", k=8)[:, :, 0:C], v_, ALU.mult, (ps, pT), (bonus_,))
            for (srcv, sb, dstv, db) in ((AR[:, :, 0, 0:C], AR, ARH[0:64, :, 0, 0:C], ARH), (AR[:, :, 1, 0:C], AR, ARH[0:64, :, 1, 0:C], ARH),
                                         (f3(BtT), BtT, BtH[0:64, :, 0:C], BtH), (f3(KtT), KtT, KtH[0:64, :, 0:C], KtH)):
                ps = ps_next()
                kb.mm(ps[0:64, 0:8 * C], selhi[:, 0:64], srcv, True, True, (selhi, sb), (ps,))
                kb.copy(dstv, ps[0:64, 0:8 * C].rearrange("p (k c) -> p k c", k=8), (ps,), (db,), eng="act")
            ps = ps_next()
            kb.mm(ps[0:64, 0:8], selhi[:, 0:64], Eg[:, :, C - 1], True, True, (selhi, Eg), (ps,))
            kb.copy(gh[0:64, 0:8], ps[0:64, 0:8], (ps,), (gh,))
            if lvl <= 3:
                return

            for hh in range(2):
                prs = [4 * hh + q for q in range(4)]
                heads = [8 * hh + q for q in range(8)]
                for (srcap, sb, dstb) in ((lambda pr: pT[:, 16 + pr, cs], pT, Vtok), (lambda pr: BtT[:, pr, 0:C], BtT, Bttok),
                                          (lambda pr: KtT[:, pr, 0:C], KtT, Kttok)):
                    ps = ps_next()
                    for q, pr in enumerate(prs):
                        kb.tr(ps[0:C, q * 128:(q + 1) * 128], srcap(pr), ident[:, :], (sb, ident), (ps,))
                    kb.copy(dstb[0:C, :, :], ps[0:C, :].rearrange("p (h c) -> p h c", h=8), (ps,), (dstb,), eng="act")

                def v4(ps_, width, w2):
                    return ps_[0:C, 0:4 * width].rearrange("p (h c) -> p h c", h=4)[:, :, 0:w2]

                for (X_, XH_, Mm) in ((BtT, BtH, MBm), (KtT, KtH, MKm)):
                    for hb in range(2):
                        ps_ = ps_next()
                        for hi in range(4):
                            hl = hb * 4 + hi
                            h = heads[hl]
                            Xs, pr = fmop(X_, XH_, h)
                            As, _ = fmop(AR, ARH, h)
                            kb.mm(ps_[0:C, hi * 128:hi * 128 + 2 * C], Xs[0:64, pr, 0:C], As[0:64, pr, :, 0:C], True, True, (Xs, As), (ps_,))
                        kb.tt(Mm[0:C, hb * 4:hb * 4 + 4, 0:2 * C], v4(ps_, 128, 2 * C), bcmid(mkU, 4), ALU.mult, (ps_, mk), (Mm,))
                if lvl <= 4:
                    return
                for hb in range(2):
                    ps_ = ps_next()
                    for hi in range(4):
                        hl = hb * 4 + hi
                        h = heads[hl]
                        As, pr = fmop(AR, ARH, h)
                        Bs, _ = fmop(BtT, BtH, h)
                        kb.mm(ps_[0:C, hi * 64:hi * 64 + C], As[0:64, pr, 0, 0:C], Bs[0:64, pr, 0:C], True, True, (As, Bs), (ps_,))
                    kb.stt(NT[0:C, hb * 4:hb * 4 + 4, 0:C], v4(ps_, 64, C), -1.0, bcmid(mkL, 4), ALU.mult, ALU.mult, (ps_, mk), (NT,))
                ts1(Nm[0:C, :, 0:C], MBm[0:C, :, 0:C], -1.0, ALU.mult, (MBm,), (Nm,))
                kb.tt(Rm[0:C, :, 0:C], Nm[0:C, :, 0:C], bcmid(ident[0:C, 0:C], 8), ALU.add, (Nm, ident), (Rm,))
                Pc, PTc, Pn, PTn = Nm, NT, Pa, PTa
                for step in range(nd):
                    last = step == nd - 1
                    for hb in range(2):
                        ps_ = ps_next()
                        for hi in range(4):
                            hl = hb * 4 + hi
                            kb.mm(ps_[0:C, hi * 64:hi * 64 + C], Pc[0:C, hl, 0:C], PTc[0:C, hl, 0:C], True, True, (Pc, PTc), (ps_,))
                        kb.copy(PTn[0:C, hb * 4:hb * 4 + 4, 0:C], v4(ps_, 64, C), (ps_,), (PTn,), eng="act")
                    if not last:
                        for hb in range(2):
                            ps_ = ps_next()
                            for hi in range(4):
                                hl = hb * 4 + hi
                                kb.mm(ps_[0:C, hi * 64:hi * 64 + C], PTc[0:C, hl, 0:C], Pc[0:C, hl, 0:C], True, True, (Pc, PTc), (ps_,))
                            kb.copy(Pn[0:C, hb * 4:hb * 4 + 4, 0:C], v4(ps_, 64, C), (ps_,), (Pn,), eng="dve")
                    for hb in range(2):
                        ps_ = ps_next()
                        for hi in range(4):
                            hl = hb * 4 + hi
                            kb.mm(ps_[0:C, hi * 64:hi * 64 + C], PTn[0:C, hl, 0:C], Rm[0:C, hl, 0:C], True, True, (PTn, Rm), (ps_,))
                        kb.tt(Rm[0:C, hb * 4:hb * 4 + 4, 0:C], Rm[0:C, hb * 4:hb * 4 + 4, 0:C], v4(ps_, 64, C), ALU.add, (Rm, ps_), (Rm,))
                    Pc, PTc, Pn, PTn = Pn, PTn, Pc, PTc
                if lvl <= 5:
                    return
                for hb in range(2):
                    ps_ = ps_next()
                    for hi in range(4):
                        hl = hb * 4 + hi
                        h = heads[hl]
                        As, pr = fmop(AR, ARH, h)
                        o = ps_[0:C, hi * 64:hi * 64 + 64]
                        kb.mm(o, As[0:64, pr, 0, 0:C], S0T[0:64, h, :], True, False, (As, S0T), (ps_,))
                        kb.mm(o, MKm[0:C, hl, 0:C], Vtok[0:C, hl, :], False, True, (MKm, Vtok), (ps_,))
                    kb.copy(XT[0:C, hb * 4:hb * 4 + 4, :], v4(ps_, 64, 64), (ps_,), (XT,), eng="act")
                for hb in range(2):
                    ps_ = ps_next()
                    for hi in range(4):
                        hl = hb * 4 + hi
                        kb.mm(ps_[0:C, hi * 64:hi * 64 + 64], Rm[0:C, hl, 0:C], XT[0:C, hl, :], True, True, (Rm, XT), (ps_,))
                    ts1(UT[0:C, hb * 4:hb * 4 + 4, :], v4(ps_, 64, 64), -1.0, ALU.mult, (ps_,), (UT,))
                for hb in range(2):
                    ps_ = ps_next()
                    for hi in range(4):
                        hl = hb * 4 + hi
                        h = heads[hl]
                        As, pr = fmop(AR, ARH, h)
                        o = ps_[0:C, hi * 64:hi * 64 + 64]
                        kb.mm(o, As[0:64, pr, 1, 0:C], S0T[0:64, h, :], True, False, (As, S0T), (ps_,))
                        kb.mm(o, MBm[0:C, hl, C:2 * C], UT[0:C, hl, :], False, False, (MBm, UT), (ps_,))
                        kb.mm(o, MKm[0:C, hl, C:2 * C], Vtok[0:C, hl, :], False, True, (MKm, Vtok), (ps_,))
                    kb.copy(Yt[0:C, hb * 4:hb * 4 + 4, :], v4(ps_, 64, 64), (ps_,), (Yt,), eng="act")
                for hb in range(2):
                    ps_ = ps_next()
                    for hi in range(4):
                        hl = hb * 4 + hi
                        o = ps_[0:64, hi * 64:hi * 64 + 64]
                        kb.mm(o, Bttok[0:C, hl, :], UT[0:C, hl, :], True, False, (Bttok, UT), (ps_,))
                        kb.mm(o, Kttok[0:C, hl, :], Vtok[0:C, hl, :], False, True, (Kttok, Vtok), (ps_,))
                    for hi in range(4):
                        hl = hb * 4 + hi
                        h = heads[hl]; pr = h // 2
                        gC = Eg[0:64, pr, C - 1:C] if h % 2 == 0 else gh[0:64, pr:pr + 1]
                        gb = Eg if h % 2 == 0 else gh
                        ts1(NT[0:64, hl, :], S0T[0:64, h, :], gC, ALU.mult, (S0T, gb), (NT,))
                        kb.stt(S0T[0:64, h, :], ps_[0:64, hi * 64:hi * 64 + 64], gC, NT[0:64, hl, :], ALU.mult, ALU.add, (ps_, gb, NT), (S0T,))
                if lvl <= 6:
                    return
                P.op("dve", lambda e: e.tensor_reduce(st8[0:C, :, 0], Yt[0:C, :, :], AX.X, ALU.add), (Yt,), (st8,))
                ts1(st8[0:C, :, 0], st8[0:C, :, 0], 1.0 / 64, ALU.mult, (st8,), (st8,))
                kb.tt(Yc[0:C, :, :], Yt[0:C, :, :], bc3(st8[0:C, :, 0], 64), ALU.subtract, (Yt, st8), (Yc,))
                kb.tt(Yt[0:C, :, :], Yc[0:C, :, :], Yc[0:C, :, :], ALU.mult, (Yc,), (Yt,))
                P.op("dve", lambda e: e.tensor_reduce(st8[0:C, :, 1], Yt[0:C, :, :], AX.X, ALU.add), (Yt,), (st8,))
                kb.ts(st8[0:C, :, 1], st8[0:C, :, 1], 1.0 / 64, 64e-5, ALU.mult, ALU.add, (st8,), (st8,))
                kb.act(st8[0:C, :, 1], st8[0:C, :, 1], AF.Sqrt, (st8,), (st8,))
                P.op("dve", lambda e: e.reciprocal(st8[0:C, :, 2], st8[0:C, :, 1]), (st8,), (st8,))
                kb.tt(Yc[0:C, :, :], Yc[0:C, :, :], bc3(st8[0:C, :, 2], 64), ALU.mult, (Yc, st8), (Yc,))
                ps_ = ps_next()
                for q, pr in enumerate(prs):
                    kb.tr(ps_[:, q * 64:q * 64 + C], Yc[0:C, q * 2:q * 2 + 2, :], ident[0:C, 0:C], (Yc, ident), (ps_,))
                for q, pr in enumerate(prs):
                    kb.act(tmp2[:, pr, 0:C], ps_[:, q * 64:q * 64 + C], AF.Identity, (ps_, v_lb, v_lw), (tmp2,),
                           bias=v_lb[:, pr:pr + 1], scale=v_lw[:, pr:pr + 1])
            if lvl <= 7:
                return
            kb.tt(f3(tmp2), f3(tmp2), f3(bonus_), ALU.add, (tmp2, bonus_), (tmp2,))
            kb.tt(yrwT[:, :, cs], f3(tmp2), f3(gate_), ALU.mult, (tmp2, gate_), (yrwT,))
            if kind == "s" or (t0 + T == SEQ_ and ci == nch - 1):
                heads_T(S0T, nat16, (S0T,), (nat16,))
                dst = rws.h[ci] if kind == "s" else rwp.h
                kb.dma(dst.rearrange("h i j -> i h j"), nat16[0:64, :, :], (nat16,), (), nat16)

    NEGM = -30000.0
    SCALE = 128 ** -0.5
    kselT = P.sbuf("kselT", [128, 2, SEQ], BF16)
    vsel = P.sbuf("vsel", [128, SEQ // 128, 2, 128], BF16)
    kwinT = P.sbuf("kwinT", [128, 2, 6, 128], BF16)
    vwin = P.sbuf("vwin", [128, 6, 2, 128], BF16)
    kcT = P.sbuf("kcT", [128, 2, 64], BF16)
    vc_all = P.sbuf("vc_all", [64, 2, 128], BF16)
    wgate = P.sbuf("wgate", [128, KC, 24], BF16)
    kb.dma(wgate[:, :, :], w_in.h.rearrange("(k p) n -> p k n", p=128)[:, :, 5888:5912], (), (wgate,), wgate, eng="pool")
    cw2 = P.sbuf("cw2", [128, 2, 2, 128], BF16)
    for kv_ in range(2):
        kb.dma(cw2[:, kv_, :, :], cmp_w2.h[kv_].rearrange("(k p) d -> p k d", p=128), (), (cw2,), cw2, eng="pool")
    peT = P.sbuf("peT", [128, 64], F32)
    mc_sb = P.sbuf("mc_sb", [128, 67], F32)
    kb.dma(mc_sb[:], mc_d.h[:, :], (), (mc_sb,), mc_sb)
    bc_sb = P.sbuf("bc_sb", [128, 16, 32], F32)
    kb.dma(bc_sb[:], bc_d.h[:, :, :], (), (bc_sb,), bc_sb)
    bs_sb = P.sbuf("bs_sb", [4, 136], F32)
    kb.dma(bs_sb[:], bs_d.h[:, :], (), (bs_sb,), bs_sb)
    iotap = P.sbuf("iotap_sb", [128, 1], F32)
    kb.dma(iotap[:], iotap_d.h[:, :], (), (iotap,), iotap)
    relb = P.sbuf("relb33", [33, 8], F32)
    kb.memset(relb[:, :], 1.0, (relb,))
    kb.dma(relb[0:32, :], rel_bias.h[:, :], (relb,), (relb,), relb)
    Big = P.dram("big_sel", [8, 128, 2048], F32, "Internal")
    BigW = P.dram("big_win", [8, 128, 640], F32, "Internal")
    BigC = P.dram("big_cmp", [8, 128 * 67], F32, "Internal")

    def nsa_setup():
        A = arena
        A.reset()
        pes = A.alloc("pes", 128)
        kb.dma(pes[0:64, 0:128], cmp_pe.h.rearrange("r k d -> (r k) d"), (), (pes,), pes)
        ps = ps_next()
        kb.tr(ps[:, 0:64], pes[0:64, 0:128], ident[0:64, 0:64], (pes, ident), (ps,))
        kb.copy(peT[:, :], ps[:, 0:64], (ps,), (peT,))
        for (oh_d, ncol, bigd, wrow) in ((ohs_d, 2176, Big, 2048), (ohw_d, 768, BigW, 640)):
            oh = A.alloc("oh", ncol)
            tr_ = A.alloc("trev", ncol)
            kb.dma(oh[0:33, 0:ncol], oh_d.h[:, :], (), (oh,), oh)
            for c0 in range(0, ncol, 512):
                w_ = min(512, ncol - c0)
                ps = ps_next()
                kb.mm(ps[0:8, 0:w_], relb[0:33, 0:8], oh[0:33, c0:c0 + w_], True, True, (relb, oh), (ps,))
                kb.copy(tr_[0:8, c0:c0 + w_], ps[0:8, 0:w_], (ps,), (tr_,))
            for q in range(128):
                kb.dma(bigd.h[:, q, :], tr_[0:8, 127 - q:127 - q + wrow], (tr_,), (bigd,), tr_)
        ncol = 128 * 67
        for c0 in range(0, ncol, 512):
            w_ = min(512, ncol - c0)
            oh = A.alloc("ohc", 512)
            tc_ = A.alloc("trc", 512)
            kb.dma(oh[0:33, 0:w_], ohc_d.h[:, c0:c0 + w_], (), (oh,), oh)
            ps = ps_next()
            kb.mm(ps[0:8, 0:w_], relb[0:33, 0:8], oh[0:33, 0:w_], True, True, (relb, oh), (ps,))
            kb.copy(tc_[0:8, 0:w_], ps[0:8, 0:w_], (ps,), (tc_,))
            kb.dma(BigC.h[:, c0:c0 + w_], tc_[0:8, 0:w_], (tc_,), (BigC,), tc_)
            if A.off + 1024 > A.words:
                A.reset()

    def psbf(ps):
        return ps.h.bitcast(BF16)

    def nsa_cache_update(t0, T, ti, kvT):
        A = arena
        nsub = T // 128
        for g in range(2):
            kb.copy(kselT[:, g, t0:t0 + T], kvT[:, 4 + g, :T], (kvT,), (kselT,), eng="act")
            for s_ in range(nsub):
                kt = t0 // 128 + s_
                kb.copy(kwinT[:, g, kt % 6, :], kvT[:, 8 + g, s_ * 128:(s_ + 1) * 128], (kvT,), (kwinT,), eng="act")
        for s_ in range(nsub):
            kt = t0 // 128 + s_
            ps = ps_next()
            for j, ch in enumerate((6, 7, 10, 11)):
                kb.tr(ps[:, j * 128:(j + 1) * 128], kvT[:, ch, s_ * 128:(s_ + 1) * 128], ident[:, :], (kvT, ident), (ps,))
            kb.copy(vsel[:, kt, :, :], ps[:, 0:256].rearrange("p (g d) -> p g d", g=2), (ps,), (vsel,))
            kb.copy(vwin[:, kt % 6, :, :], ps[:, 256:512].rearrange("p (g d) -> p g d", g=2), (ps,), (vwin,))
        nb = T // 32
        Xb = A.alloc("Xb", 4 * T // 2, view=lambda a: a.bitcast(BF16).rearrange("p (c n r) -> p c n r", c=4, r=32))
        hid = A.alloc("hid", 64, view=lambda a: a.rearrange("p (c n) -> p c n", c=2))
        hx = A.alloc("hx", 64, view=lambda a: a.rearrange("p (c n) -> p c n", c=2))
        hb = A.alloc("hb", 32, view=lambda a: a.bitcast(BF16).rearrange("p (c n) -> p c n", c=2))
        vct = A.alloc("vct", 128, view=lambda a: a.bitcast(BF16))
        pe3 = peT[:, :].rearrange("p (r k) -> p k r", k=2)
        for c in range(4):
            kv_ = c // 2
            kb.tt(Xb[:, c, :, :], kvT[:, c, :T].rearrange("p (n r) -> p n r", r=32), bcmid(pe3[:, kv_, :], nb), ALU.add,
                  (kvT, peT), (Xb,))
        for kv_ in range(2):
            slots = [ws.take(), ws.take()]
            wv = [s_[:, 0:16 * 256].rearrange("p (r c) -> p r c", r=16) for s_ in slots]
            for g in range(2):
                c = kv_ * 2 + g
                for cc in range(2):
                    ps = ps_next()
                    for r in range(32):
                        kb.mm(ps[:, 0:nb], wv[r // 16][:, r % 16, cc * 128:(cc + 1) * 128], Xb[:, c, :, r], r == 0, r == 31,
                              (slots[r // 16], Xb), (ps,))
                    kb.copy(hid[:, cc, 0:nb], ps[:, 0:nb], (ps,), (hid,))
                kb.tt(hx[:, :, 0:nb], hid[:, :, 0:nb], hid[:, :, 0:nb], ALU.mult, (hid,), (hx,))
                kb.ts(hx[:, :, 0:nb], hx[:, :, 0:nb], 0.044715, 1.0, ALU.mult, ALU.add, (hx,), (hx,))
                kb.tt(hx[:, :, 0:nb], hx[:, :, 0:nb], hid[:, :, 0:nb], ALU.mult, (hx, hid), (hx,))
                kb.act(hx[:, :, 0:nb], hx[:, :, 0:nb], AF.Sigmoid, (hx,), (hx,), scale=1.5957691216057308)
                kb.tt(hb[:, :, 0:nb], hx[:, :, 0:nb], hid[:, :, 0:nb], ALU.mult, (hx, hid), (hb,))
                ps = ps_next()
                if kv_ == 0:
                    for cc in range(2):
                        kb.mm(ps[:, 0:nb], cw2[:, 0, cc, :], hb[:, cc, 0:nb], cc == 0, cc == 1, (cw2, hb), (ps,))
                    kb.copy(kcT[:, g, ti * nb:(ti + 1) * nb], ps[:, 0:nb], (ps,), (kcT,))
                else:
                    for cc in range(2):
                        kb.mm(ps[0:nb, 0:128], hb[:, cc, 0:nb], cw2[:, 1, cc, :], cc == 0, cc == 1, (cw2, hb), (ps,))
                    kb.copy(vct[0:nb, g * 128:(g + 1) * 128], ps[0:nb, 0:128], (ps,), (vct,))
        kb.dma(vc_all[ti * nb:(ti + 1) * nb, :, :], vct[0:nb, 0:256].rearrange("p (g d) -> p g d", g=2), (vct,), (vc_all,), vct)

    def nsa_prompt(t0, T, ti):
        A = arena
        S = A.alloc("S", 2048)
        Bt = [A.alloc("bias0", 2048), A.alloc("bias1", 2048)]
        E = A.alloc("E", 1024, view=lambda a: a.bitcast(BF16))
        PTb = A.alloc("PTb", 1024, view=lambda a: a.bitcast(BF16).rearrange("p (k q) -> p k q", k=16))
        acc = A.alloc("acc", 1024)
        ob = A.alloc("ob", 128)
        pc = A.alloc("pc", 64); pcb = A.alloc("pcb", 32, view=lambda a: a.bitcast(BF16))
        imp = A.alloc("imp", 64); imp2 = A.alloc("imp2", 32); impw = A.alloc("impw", 32); selm = A.alloc("selm", 32)
        mx8 = A.alloc("mx8", 16)
        st = A.alloc("st", 8)
        gsig = A.alloc("gsig", 24)
        bi = [0]

        def softmax_pv(h, g, W, ktiles, kT_ap_fn, v_ap_fn, bias_src_ap, gate_col, first, mask_blocks):
            bt = Bt[bi[0] % 2]
            bi[0] += 1
            kb.dma(bt[:, 0:W], bias_src_ap, (Big, BigW), (bt,), bt)
            qap = qT[:, h, qs]
            c0 = 0
            while c0 < W:
                w_ = min(512, W - c0)
                ps = ps_next()
                for (kc0, kw, rhs_ap, rb) in kT_ap_fn(c0, w_):
                    kb.mm(ps[:, kc0 - c0:kc0 - c0 + kw], qap, rhs_ap, True, True, (qT, rb), (ps,))
                kb.stt(S[:, c0:c0 + w_], ps[:, 0:w_], SCALE, bt[:, c0:c0 + w_], ALU.mult, ALU.add, (ps, bt), (S,))
                c0 += w_
            kb.reduce(st[:, 0:1], S[:, 0:W], ALU.max, (S,), (st,))
            ts1(st[:, 1:2], st[:, 0:1], -1.0, ALU.mult, (st,), (st,))
            if mask_blocks is None:
                kb.act(E[:, 0:W], S[:, 0:W], AF.Exp, (S, st), (E,), bias=st[:, 1:2])
            else:
                kb.act(S[:, 0:W], S[:, 0:W], AF.Exp, (S, st), (S,), bias=st[:, 1:2])
                nb_ = W // 64
                kb.tt(E[:, 0:W].rearrange("p (b c) -> p b c", c=64), S[:, 0:W].rearrange("p (b c) -> p b c", c=64),
                      bc3(mask_blocks[:, 0:nb_], 64), ALU.mult, (S, mask_blocks), (E,))
            kb.reduce(st[:, 2:3], E[:, 0:W], ALU.add, (E,), (st,))
            kb.recip(st[:, 3:4], st[:, 2:3], (st,), (st,))
            kb.tt(st[:, 3:4], st[:, 3:4], gate_col, ALU.mult, (st, gsig), (st,))
            nk = len(ktiles)
            for k0 in range(0, nk, 8):
                ps = ps_next()
                pv = psbf(ps)
                kk_ = min(8, nk - k0)
                for j in range(kk_):
                    (cc0, cw, _) = ktiles[k0 + j]
                    kb.tr(pv[0:cw, j * 128:(j + 1) * 128], E[:, cc0:cc0 + cw], identb[:, :], (E, identb), (ps,))
                kb.copy(PTb[:, k0:k0 + kk_, :], pv[:, 0:kk_ * 128].rearrange("p (k q) -> p k q", k=kk_), (ps,), (PTb,),
                        eng=("act" if (k0 // 8) % 2 else "dve"))
            ps = ps_next()
            for j, (cc0, cw, kt) in enumerate(ktiles):
                vap, vb = v_ap_fn(kt, cw)
                kb.mm(ps[:, 0:128], PTb[0:cw, j, :], vap, j == 0, j == nk - 1, (PTb, vb), (ps,))
            hsl = slice(h * 128, (h + 1) * 128)
            if first:
                ts1(acc[:, hsl], ps[:, 0:128], st[:, 3:4], ALU.mult, (ps, st), (acc,))
            else:
                kb.stt(acc[:, hsl], ps[:, 0:128], st[:, 3:4], acc[:, hsl], ALU.mult, ALU.add, (ps, st, acc), (acc,))

        for qb in range(T // 128):
            i = t0 // 128 + qb
            q0 = i * 128
            qs = slice(qb * 128, (qb + 1) * 128)
            ps = ps_next()
            for k in range(KC):
                kb.mm(ps[:, 0:24], hT[:, k, qs], wgate[:, k, :], k == 0, k == KC - 1, (hT, wgate), (ps,))
            kb.act(gsig[:, 0:24], ps[:, 0:24], AF.Sigmoid, (ps,), (gsig,))
            Wc = 4 * i + 4
            for g in range(2):
                for hh in range(4):
                    h = g * 4 + hh
                    bt = Bt[bi[0] % 2]
                    bi[0] += 1
                    kb.dma(bt[:, 0:Wc], BigC.h[h].rearrange("(q j) -> q j", j=67)[:, 63 - 4 * i:67], (BigC,), (bt,), bt)
                    ps = ps_next()
                    kb.mm(ps[:, 0:Wc], qT[:, h, qs], kcT[:, g, 0:Wc], True, True, (qT, kcT), (ps,))
                    kb.stt(S[:, 0:Wc], ps[:, 0:Wc], SCALE, bt[:, 0:Wc], ALU.mult, ALU.add, (ps, bt), (S,))
                    kb.reduce(st[:, 0:1], S[:, 0:Wc], ALU.max, (S,), (st,))
                    ts1(st[:, 1:2], st[:, 0:1], -1.0, ALU.mult, (st,), (st,))
                    kb.act(pc[:, 0:Wc], S[:, 0:Wc], AF.Exp, (S, st), (pc,), bias=st[:, 1:2])
                    kb.tt(pc[:, 0:Wc], pc[:, 0:Wc], mc_sb[:, 63 - 4 * i:67], ALU.mult, (pc, mc_sb), (pc,))
                    kb.reduce(st[:, 2:3], pc[:, 0:Wc], ALU.add, (pc,), (st,))
                    ts1(st[:, 2:3], st[:, 2:3], 1e-30, ALU.max, (st,), (st,))
                    kb.recip(st[:, 3:4], st[:, 2:3], (st,), (st,))
                    ts1(pc[:, 0:Wc], pc[:, 0:Wc], st[:, 3:4], ALU.mult, (pc, st), (pc,))
                    if hh == 0:
                        kb.copy(imp[:, 0:Wc], pc[:, 0:Wc], (pc,), (imp,))
                    else:
                        kb.tt(imp[:, 0:Wc], imp[:, 0:Wc], pc[:, 0:Wc], ALU.add, (imp, pc), (imp,))
                    kb.copy(pcb[:, 0:Wc], pc[:, 0:Wc], (pc,), (pcb,))
                    ps = ps_next()
                    pv = psbf(ps)
                    kb.tr(pv[0:Wc, 0:128], pcb[:, 0:Wc], identb[:, :], (pcb, identb), (ps,))
                    kb.copy(PTb[0:Wc, 0, :], pv[0:Wc, 0:128], (ps,), (PTb,))
                    ps = ps_next()
                    kb.mm(ps[:, 0:128], PTb[0:Wc, 0, :], vc_all[0:Wc, g, :], True, True, (PTb, vc_all), (ps,))
                    ts1(acc[:, h * 128:(h + 1) * 128], ps[:, 0:128], gsig[:, h:h + 1], ALU.mult, (ps, gsig), (acc,))
                nsb = 2 * i + 2
                mask_blocks = None
                if nsb > 16:
                    kb.reduce(imp2[:, 0:nsb], imp[:, 0:Wc].rearrange("p (b two) -> p b two", two=2), ALU.add, (imp,), (imp2,))
                    if nsb < 32:
                        kb.memset(imp2[:, nsb:32], -1e30, (imp2,))
                    kb.tt(imp2[:, 0:nsb], imp2[:, 0:nsb], bc_sb[:, i, 0:nsb], ALU.add, (imp2, bc_sb), (imp2,))
                    P.op("dve", lambda e: e.max(mx8[:, 0:8], imp2[:, 0:32]), (imp2,), (mx8,))
                    P.op("dve", lambda e: e.match_replace(impw[:, 0:32], mx8[:, 0:8], imp2[:, 0:32], -3e38), (mx8, imp2), (impw,))
                    P.op("dve", lambda e: e.max(mx8[:, 8:16], impw[:, 0:32]), (impw,), (mx8,))
                    kb.reduce(st[:, 4:5], mx8[:, 8:16], ALU.min, (mx8,), (st,))
                    ts1(selm[:, 0:32], imp2[:, 0:32], st[:, 4:5], ALU.is_ge, (imp2, st), (selm,))
                    mask_blocks = selm
                for hh in range(4):
                    h = g * 4 + hh
                    Ws = q0 + 128
                    ktl = [(kt * 128, 128, kt) for kt in range(i + 1)]
                    softmax_pv(h, g, Ws, ktl,
                               lambda c0, w_: [(c0, w_, kselT[:, g, c0:c0 + w_], kselT)],
                               lambda kt, cw: (vsel[0:cw, kt, g, :], vsel),
                               Big.h[h, :, 1920 - q0:1920 - q0 + Ws], gsig[:, 8 + h:9 + h], False, mask_blocks)
                    kt0 = max(0, i - 4)
                    ktw = list(range(kt0, i + 1))
                    Ww = 128 * len(ktw)
                    ktlw = [(j * 128, 128, kt) for j, kt in enumerate(ktw)]
                    softmax_pv(h, g, Ww, ktlw,
                               lambda c0, w_: [(c0 + jj * 128, 128, kwinT[:, g, ktw[(c0 // 128) + jj] % 6, :], kwinT)
                                               for jj in range(w_ // 128)],
                               lambda kt, cw: (vwin[0:cw, kt % 6, g, :], vwin),
                               BigW.h[h, :, 640 - Ww:640], gsig[:, 16 + h:17 + h], False, None)
            for h0 in range(0, 8, 4):
                ps = ps_next()
                for j in range(4):
                    kb.tr(ps[:, j * 128:(j + 1) * 128], acc[:, (h0 + j) * 128:(h0 + j + 1) * 128], ident[:, :], (acc, ident), (ps,))
                kb.copy(ynsaT[:, h0:h0 + 4, qs], ps[:, :].rearrange("p (h q) -> p h q", h=4), (ps,), (ynsaT,), eng=("act" if h0 else "dve"))

    PAST = 8192
    newK = P.sbuf("newK", [128, 2, 2, 16], BF16)
    newVT = P.sbuf("newVT", [128, 2, 2, 16], BF16)
    smp = {}
    TSs = P.dram("tab_s_sel", [8, 8704], F32, "Internal")
    TWs = P.dram("tab_s_win", [8, 1024], F32, "Internal")
    TCs = P.dram("tab_s_cmp", [8, 1024], F32, "Internal")
    ckv_rows = ckv.h.rearrange("n p c -> (n p) c")

    def nsa_setup_sample():
        A = arena
        A.reset()
        for (oh_d, ncol, dst) in ((ohss_d, 8704, TSs), (ohsw_d, 1024, TWs), (ohsc_d, 1024, TCs)):
            for c0 in range(0, ncol, 512):
                oh = A.alloc("ohc", 512)
                tc_ = A.alloc("trc", 512)
                kb.dma(oh[0:33, 0:512], oh_d.h[:, c0:c0 + 512], (), (oh,), oh)
                ps = ps_next()
                kb.mm(ps[0:8, 0:512], relb[0:33, 0:8], oh[0:33, 0:512], True, True, (relb, oh), (ps,))
                kb.copy(tc_[0:8, 0:512], ps[0:8, 0:512], (ps,), (tc_,))
                kb.dma(dst.h[:, c0:c0 + 512], tc_[0:8, 0:512], (tc_,), (dst,), tc_)
                if A.off + 1024 > A.words:
                    A.reset()

    def nsa_sample_newrows(kvT):
        for w_, (kc, vc_) in enumerate(((4, 6), (8, 10))):
            for g in range(2):
                kb.copy(newK[:, w_, g, :], kvT[:, kc + g, 0:16], (kvT,), (newK,))
                kb.copy(newVT[:, w_, g, :], kvT[:, vc_ + g, 0:16], (kvT,), (newVT,))

    def page_index_bufs(A):
        return (A.alloc("pti", 64, view=lambda a: a.bitcast(I32)), A.alloc("ptf", 64),
                A.alloc("idx", 64, view=lambda a: a.bitcast(I32)))

    def page_index(A, s_, bufs):
        pti, ptf, idx = bufs
        kb.dma(pti[:, :], ptab.h[s_:s_ + 1, :].to_broadcast([128, 64]), (), (pti,), pti)
        kb.copy(ptf[:, :], pti[:, :], (pti,), (ptf,))
        kb.ts(ptf[:, :], ptf[:, :], 128.0, iotap[:, 0:1], ALU.mult, ALU.add, (ptf, iotap), (ptf,))
        kb.copy(idx[:, :], ptf[:, :], (ptf,), (idx,))
        return idx

    def gather(dst_buf, dst_ap, src_ap, idx, j, eoff=0):
        def fn(e):
            return e.indirect_dma_start(out=dst_ap, out_offset=None, in_=src_ap,
                                        in_offset=bass.IndirectOffsetOnAxis(ap=idx[:, j:j + 1], axis=0), element_offset=eoff)
        return P.op("pool", fn, (idx,), (dst_buf,), dma=True, sem_buf=dst_buf)

    def nsa_sample_compress():
        A = arena
        kcTs = A.alloc("kcTs", 1024, view=lambda a: a.bitcast(BF16).rearrange("p (s g n) -> p s g n", s=4, g=2))
        vcs = A.alloc("vcs", 1024, view=lambda a: a.bitcast(BF16).rearrange("p (s c g d) -> p s c g d", s=4, c=2, g=2))
        smp["kcTs"], smp["vcs"], smp["mark"] = kcTs, vcs, A.off
        W1b = A.alloc("W1b", 8192, view=lambda a: a.bitcast(BF16).rearrange("p (k r c) -> p k r c", k=2, r=32))
        for kv_ in range(2):
            for half in range(2):
                kb.dma(W1b[:, kv_, half * 16:(half + 1) * 16, :],
                       cmp_w1.h[kv_, half * 2048:(half + 1) * 2048, :].rearrange("(r p) c -> p r c", p=128), (), (W1b,), W1b, eng="pool")
        pgs = [A.alloc("pgA", 1024), A.alloc("pgB", 1024)]
        Xb = A.alloc("Xb8", 2048, view=lambda a: a.bitcast(BF16).rearrange("p (c n r) -> p c n r", c=4, r=32))
        hid = A.alloc("hid", 64, view=lambda a: a.rearrange("p (c n) -> p c n", c=2))
        hx = A.alloc("hx", 64, view=lambda a: a.rearrange("p (c n) -> p c n", c=2))
        hb = A.alloc("hb", 32, view=lambda a: a.bitcast(BF16).rearrange("p (c n) -> p c n", c=2))
        vct = A.alloc("vct", 128, view=lambda a: a.bitcast(BF16))
        pe3 = peT[:, :].rearrange("p (r k) -> p k r", k=2)
        nb = 32
        pib = page_index_bufs(A)
        for s_ in range(4):
            idx = page_index(A, s_, pib)
            for grp in range(8):
                for pj in range(8):
                    j = grp * 8 + pj
                    pg = pgs[j % 2]
                    gather(pg, pg[:, 0:1024], ckv_rows[:, :], idx, j)
                    ps = ps_next()
                    for c in range(4):
                        kb.tr(ps[:, c * 128:(c + 1) * 128], pg[:, c * 128:(c + 1) * 128], ident[:, :], (pg, ident), (ps,))
                    for c in range(4):
                        kb.tt(Xb[:, c, pj * 4:(pj + 1) * 4, :], ps[:, c * 128:(c + 1) * 128].rearrange("p (n r) -> p n r", r=32),
                              bcmid(pe3[:, c // 2, :], 4), ALU.add, (ps, peT), (Xb,), eng=("dve"))
                for kv_ in range(2):
                    for g in range(2):
                        c = kv_ * 2 + g
                        for cc in range(2):
                            ps = ps_next()
                            for r in range(32):
                                kb.mm(ps[:, 0:nb], W1b[:, kv_, r, cc * 128:(cc + 1) * 128], Xb[:, c, :, r], r == 0, r == 31, (W1b, Xb), (ps,))
                            kb.copy(hid[:, cc, 0:nb], ps[:, 0:nb], (ps,), (hid,), eng="act")
                        kb.tt(hx[:, :, 0:nb], hid[:, :, 0:nb], hid[:, :, 0:nb], ALU.mult, (hid,), (hx,))
                        kb.ts(hx[:, :, 0:nb], hx[:, :, 0:nb], 0.044715, 1.0, ALU.mult, ALU.add, (hx,), (hx,))
                        kb.tt(hx[:, :, 0:nb], hx[:, :, 0:nb], hid[:, :, 0:nb], ALU.mult, (hx, hid), (hx,))
                        kb.act(hx[:, :, 0:nb], hx[:, :, 0:nb], AF.Sigmoid, (hx,), (hx,), scale=1.5957691216057308)
                        kb.tt(hb[:, :, 0:nb], hx[:, :, 0:nb], hid[:, :, 0:nb], ALU.mult, (hx, hid), (hb,))
                        ps = ps_next()
                        if kv_ == 0:
                            for cc in range(2):
                                kb.mm(ps[:, 0:nb], cw2[:, 0, cc, :], hb[:, cc, 0:nb], cc == 0, cc == 1, (cw2, hb), (ps,))
                            kb.copy(kcTs[:, s_, g, grp * nb:(grp + 1) * nb], ps[:, 0:nb], (ps,), (kcTs,))
                        else:
                            for cc in range(2):
                                kb.mm(ps[0:nb, 0:128], hb[:, cc, 0:nb], cw2[:, 1, cc, :], cc == 0, cc == 1, (cw2, hb), (ps,))
                            kb.copy(vct[0:nb, g * 128:(g + 1) * 128], ps[0:nb, 0:128], (ps,), (vct,))
                po = (grp % 4) * nb
                kb.dma(vcs[po:po + nb, s_, grp // 4, :, :], vct[0:nb, 0:256].rearrange("p (g d) -> p g d", g=2), (vct,), (vcs,), vct)

    def nsa_sample_attend():
        A = arena
        kcTs, vcs = smp["kcTs"], smp["vcs"]
        KTs = A.alloc("KTs", 4104, view=lambda a: a.bitcast(BF16))
        Vs = A.alloc("Vs", 65 * 64, view=lambda a: a.bitcast(BF16).rearrange("p (k d) -> p k d", d=128))
        KW = A.alloc("KW", 264, view=lambda a: a.bitcast(BF16))
        VW = A.alloc("VW", 5 * 64, view=lambda a: a.bitcast(BF16).rearrange("p (k d) -> p k d", d=128))
        pg1 = A.alloc("pgA", 1024)
        pgs = [pg1, pg1]
        wst = pg1
        S = A.alloc("S4", 1024)
        bt = A.alloc("bias4", 1024)
        E = A.alloc("E4", 512, view=lambda a: a.bitcast(BF16))
        PTb = A.alloc("PTb4", 32, view=lambda a: a.bitcast(BF16).rearrange("p (k q) -> p k q", q=4))
        acc = A.alloc("acc4", 1024)
        pc = A.alloc("pc4", 256); pcb = A.alloc("pcb4", 128, view=lambda a: a.bitcast(BF16))
        imp = A.alloc("imp4", 256); imp2 = A.alloc("imp24", 136); impw = A.alloc("impw4", 136); selm = A.alloc("selm4", 136)
        mx8 = A.alloc("mx84", 16)
        st = A.alloc("st4", 16)
        gsig = A.alloc("gsig4", 24)
        ps_mod[0] = 7
        pacc = psb[7]

        def attend(h, q_ap, groups, gate_col, first, hsl):
            multi = len(groups) > 1

            def load_bias(gi):
                (W, kT_ap, kT_buf, bias_fn, mask_ap, vt) = groups[gi]
                for t in range(4):
                    kb.dma(bt[t:t + 1, 0:W], bias_fn(t), (TSs, TWs, TCs), (bt,), bt)

            def scores(gi):
                (W, kT_ap, kT_buf, bias_fn, mask_ap, vt) = groups[gi]
                if bias_fn is not None:
                    load_bias(gi)
                c0 = 0
                while c0 < W:
                    w_ = min(512, W - c0)
                    ps = ps_next()
                    kb.mm(ps[0:4, 0:w_], q_ap, kT_ap[:, c0:c0 + w_], True, True, (qT, kT_buf), (ps,))
                    if bias_fn is not None:
                        kb.stt(S[0:4, c0:c0 + w_], ps[0:4, 0:w_], SCALE, bt[0:4, c0:c0 + w_], ALU.mult, ALU.add, (ps, bt), (S,))
                    else:
                        kb.ts(S[0:4, c0:c0 + w_], ps[0:4, 0:w_], SCALE, crow[0:4, h:h + 1], ALU.mult, ALU.add, (ps, crow), (S,))
                    c0 += w_
                return W

            if not multi:
                W = scores(0)
                kb.reduce(st[0:4, 0:1], S[0:4, 0:W], ALU.max, (S,), (st,))
            else:
                kb.copy(st[0:4, 7:8], crow[0:4, h:h + 1], (crow,), (st,))
                for gi in range(len(groups)):
                    (W, kT_ap, kT_buf, bias_fn, mask_ap, vt) = groups[gi]
                    if bias_fn is not None:
                        load_bias(gi)
                        kb.reduce(st[0:4, 5:6], bt[0:4, 0:W], ALU.max, (bt,), (st,))
                        kb.tt(st[0:4, 7:8], st[0:4, 7:8], st[0:4, 5:6], ALU.max, (st,), (st,))
                    c0 = 0
                    while c0 < W:
                        w_ = min(512, W - c0)
                        ps = ps_next()
                        kb.mm(ps[0:4, 0:w_], q_ap, kT_ap[:, c0:c0 + w_], True, True, (qT, kT_buf), (ps,))
                        if gi == 0 and c0 == 0:
                            kb.reduce(st[0:4, 0:1], ps[0:4, 0:w_], ALU.max, (ps,), (st,))
                        else:
                            kb.reduce(st[0:4, 5:6], ps[0:4, 0:w_], ALU.max, (ps,), (st,))
                            kb.tt(st[0:4, 0:1], st[0:4, 0:1], st[0:4, 5:6], ALU.max, (st,), (st,))
                        c0 += w_
                kb.stt(st[0:4, 0:1], st[0:4, 0:1], SCALE, st[0:4, 7:8], ALU.mult, ALU.add, (st,), (st,))
            ts1(st[0:4, 1:2], st[0:4, 0:1], -1.0, ALU.mult, (st,), (st,))
            nmm = sum(len(gp[5]) for gp in groups)
            imm = 0
            for gi in range(len(groups)):
                (W, kT_ap, kT_buf, bias_fn, mask_ap, vt) = groups[gi]
                if multi:
                    scores(gi)
                if mask_ap is None:
                    kb.act(E[0:4, 0:W], S[0:4, 0:W], AF.Exp, (S, st), (E,), bias=st[0:4, 1:2])
                else:
                    kb.act(S[0:4, 0:W], S[0:4, 0:W], AF.Exp, (S, st), (S,), bias=st[0:4, 1:2])
                    if W == 1024:
                        kb.tt(E[0:4, 0:W].rearrange("p (b c) -> p b c", c=64), S[0:4, 0:W].rearrange("p (b c) -> p b c", c=64),
                              mask_ap, ALU.mult, (S, selm), (E,))
                    else:
                        kb.tt(E[0:4, 0:W], S[0:4, 0:W], mask_ap, ALU.mult, (S, selm), (E,))
                if gi == 0:
                    kb.reduce(st[0:4, 2:3], E[0:4, 0:W], ALU.add, (E,), (st,))
                else:
                    kb.reduce(st[0:4, 5:6], E[0:4, 0:W], ALU.add, (E,), (st,))
                    kb.tt(st[0:4, 2:3], st[0:4, 2:3], st[0:4, 5:6], ALU.add, (st,), (st,))
                for k0 in range(0, len(vt), 16):
                    ps = ps_next()
                    pv = psbf(ps)
                    kk_ = min(16, len(vt) - k0)
                    for j in range(kk_):
                        (cc0, cw, _, _) = vt[k0 + j]
                        kb.tr(pv[0:cw, j * 4:(j + 1) * 4], E[0:4, cc0:cc0 + cw], identb[0:4, 0:4], (E, identb), (ps,))
                    kb.copy(PTb[:, 0:kk_, :], pv[:, 0:kk_ * 4].rearrange("p (k q) -> p k q", q=4), (ps,), (PTb,), eng="act")
                    for j in range(kk_):
                        (cc0, cw, v_ap, v_buf) = vt[k0 + j]
                        kb.mm(pacc[0:4, 0:128], PTb[0:cw, j, :], v_ap, imm == 0, imm == nmm - 1, (PTb, v_buf), (pacc,))
                        imm += 1
            ts1(st[0:4, 2:3], st[0:4, 2:3], 1e-30, ALU.max, (st,), (st,))
            kb.recip(st[0:4, 3:4], st[0:4, 2:3], (st,), (st,))
            kb.tt(st[0:4, 4:5], st[0:4, 3:4], gate_col, ALU.mult, (st, gsig), (st,))
            if first:
                ts1(acc[0:4, hsl], pacc[0:4, 0:128], st[0:4, 4:5], ALU.mult, (pacc, st), (acc,))
            else:
                kb.stt(acc[0:4, hsl], pacc[0:4, 0:128], st[0:4, 4:5], acc[0:4, hsl], ALU.mult, ALU.add, (pacc, st, acc), (acc,))

        crow = A.alloc("crow4", 8)
        kb.dma(crow[0:4, 0:8], TSs.h[:, 0:1].rearrange("h o -> o h").to_broadcast([4, 8]), (TSs,), (crow,), crow, nc_ok=True)

        pib = page_index_bufs(A)
        for s_ in range(4):
            cols = slice(s_ * 4, s_ * 4 + 4)
            idx = page_index(A, s_, pib)
            ps = ps_next()
            for k in range(KC):
                kb.mm(ps[0:4, 0:24], hT[:, k, cols], wgate[:, k, :], k == 0, k == KC - 1, (hT, wgate), (ps,))
            kb.act(gsig[0:4, 0:24], ps[0:4, 0:24], AF.Sigmoid, (ps,), (gsig,))
            for g in range(2):
                for j in range(64):
                    pg = pgs[j % 2]
                    gather(pg, pg[:, 0:1024], ckv_rows[:, :], idx, j)
                    if j % 4 == 0:
                        psk = ps_next()
                    kb.tr(psk[:, (j % 4) * 128:(j % 4 + 1) * 128], pg[:, 512 + g * 128:640 + g * 128], ident[:, :], (pg, ident), (psk,))
                    kb.copy(Vs[:, j, :], pg[:, 768 + g * 128:896 + g * 128], (pg,), (Vs,), eng="dve")
                    if j % 4 == 3:
                        kb.copy(KTs[:, (j - 3) * 128:(j + 1) * 128], psk[:, :], (psk,), (KTs,), eng="act")
                kb.copy(KTs[:, PAST:PAST + 4], newK[:, 0, g, cols], (newK,), (KTs,))
                ps = ps_next()
                pvn = psbf(ps)
                kb.tr(pvn[0:4, 0:128], newVT[:, 0, g, cols], identb[:, :], (newVT, identb), (ps,))
                kb.tr(pvn[0:4, 128:256], newVT[:, 1, g, cols], identb[:, :], (newVT, identb), (ps,))
                kb.copy(Vs[0:4, 64, :], pvn[0:4, 0:128], (ps,), (Vs,))
                for j in range(4):
                    kb.dma(wst[:, 0:256].rearrange("p (two d) -> p two d", two=2),
                           cwin.h[s_, j * 128:(j + 1) * 128, :].rearrange("p (two g d) -> p two g d", two=2, g=2)[:, :, g, :], (), (wst,), wst, eng="pool")
                    ps = ps_next()
                    kb.tr(ps[:, 0:128], wst[:, 0:128], ident[:, :], (wst, ident), (ps,))
                    kb.copy(KW[:, j * 128:(j + 1) * 128], ps[:, 0:128], (ps,), (KW,), eng="act")
                    kb.copy(VW[:, j, :], wst[:, 128:256], (wst,), (VW,))
                kb.copy(KW[:, 512:516], newK[:, 1, g, cols], (newK,), (KW,))
                kb.copy(VW[0:4, 4, :], pvn[0:4, 128:256], (ps,), (VW,))
                for hh in range(4):
                    h = g * 4 + hh
                    hsl = slice(h * 128, (h + 1) * 128)
                    kb.dma(bt[0:4, 0:256], TCs.h[h].rearrange("(t n) -> t n", n=256), (TCs,), (bt,), bt)
                    ps = ps_next()
                    kb.mm(ps[0:4, 0:256], qT[:, h, cols], kcTs[:, s_, g, :], True, True, (qT, kcTs), (ps,))
                    kb.stt(S[0:4, 0:256], ps[0:4, 0:256], SCALE, bt[0:4, 0:256], ALU.mult, ALU.add, (ps, bt), (S,))
                    kb.reduce(st[0:4, 0:1], S[0:4, 0:256], ALU.max, (S,), (st,))
                    ts1(st[0:4, 1:2], st[0:4, 0:1], -1.0, ALU.mult, (st,), (st,))
                    kb.act(pc[0:4, 0:256], S[0:4, 0:256], AF.Exp, (S, st), (pc,), bias=st[0:4, 1:2])
                    kb.reduce(st[0:4, 2:3], pc[0:4, 0:256], ALU.add, (pc,), (st,))
                    kb.recip(st[0:4, 3:4], st[0:4, 2:3], (st,), (st,))
                    ts1(pc[0:4, 0:256], pc[0:4, 0:256], st[0:4, 3:4], ALU.mult, (pc, st), (pc,))
                    if hh == 0:
                        kb.copy(imp[0:4, 0:256], pc[0:4, 0:256], (pc,), (imp,))
                    else:
                        kb.tt(imp[0:4, 0:256], imp[0:4, 0:256], pc[0:4, 0:256], ALU.add, (imp, pc), (imp,))
                    kb.copy(pcb[0:4, 0:256], pc[0:4, 0:256], (pc,), (pcb,))
                    ps = ps_next()
                    pv = psbf(ps)
                    for j in range(2):
                        kb.tr(pv[:, j * 4:(j + 1) * 4], pcb[0:4, j * 128:(j + 1) * 128], identb[0:4, 0:4], (pcb, identb), (ps,))
                    kb.copy(PTb[:, 0:2, :], pv[:, 0:8].rearrange("p (k q) -> p k q", q=4), (ps,), (PTb,))
                    for j in range(2):
                        kb.mm(pacc[0:4, 0:128], PTb[:, j, :], vcs[:, s_, j, g, :], j == 0, j == 1, (PTb, vcs), (pacc,))
                    ts1(acc[0:4, hsl], pacc[0:4, 0:128], gsig[0:4, h:h + 1], ALU.mult, (pacc, gsig), (acc,))
                kb.copy(imp2[0:4, 0:136], bs_sb[0:4, 0:136], (bs_sb,), (imp2,))
                kb.reduce(impw[0:4, 0:128], imp[0:4, 0:256].rearrange("p (b two) -> p b two", two=2), ALU.add, (imp,), (impw,))
                kb.tt(imp2[0:4, 0:128], imp2[0:4, 0:128], impw[0:4, 0:128], ALU.add, (imp2, impw), (imp2,))
                P.op("dve", lambda e: e.max(mx8[0:4, 0:8], imp2[0:4, 0:136]), (imp2,), (mx8,))
                P.op("dve", lambda e: e.match_replace(impw[0:4, 0:136], mx8[0:4, 0:8], imp2[0:4, 0:136], -3e38), (mx8, imp2), (impw,))
                P.op("dve", lambda e: e.max(mx8[0:4, 8:16], impw[0:4, 0:136]), (impw,), (mx8,))
                kb.reduce(st[0:4, 6:7], mx8[0:4, 8:16], ALU.min, (mx8,), (st,))
                ts1(selm[0:4, 0:136], imp2[0:4, 0:136], st[0:4, 6:7], ALU.is_ge, (imp2, st), (selm,))
                for hh in range(4):
                    h = g * 4 + hh
                    hsl = slice(h * 128, (h + 1) * 128)
                    q_ap = qT[:, h, cols]
                    groups = []
                    for gi in range(8):
                        k0 = gi * 1024
                        groups.append((1024, KTs[:, k0:k0 + 1024], KTs,
                                       (None if gi < 7 else (lambda t, k0=k0, h=h: TSs.h[h:h + 1, k0 - t + 3:k0 - t + 3 + 1024])),
                                       selm[0:4, gi * 16:(gi + 1) * 16].unsqueeze(2).to_broadcast([4, 16, 64]),
                                       [(kt * 128, 128, Vs[:, gi * 8 + kt, :], Vs) for kt in range(8)]))
                    groups.append((4, KTs[:, PAST:PAST + 4], KTs,
                                   (lambda t, h=h: TSs.h[h:h + 1, PAST - t + 3:PAST - t + 3 + 4]),
                                   selm[0:4, 128:129].to_broadcast([4, 4]),
                                   [(0, 4, Vs[0:4, 64, :], Vs)]))
                    attend(h, q_ap, groups, gsig[0:4, 8 + h:9 + h], False, hsl)
                    gw = [(516, KW[:, 0:516], KW, (lambda t, h=h: TWs.h[h:h + 1, 3 - t:3 - t + 516]), None,
                           [(j * 128, 128, VW[:, j, :], VW) for j in range(4)] + [(512, 4, VW[0:4, 4, :], VW)])]
                    attend(h, q_ap, gw, gsig[0:4, 16 + h:17 + h], False, hsl)
            for h0 in range(0, 8, 4):
                ps = ps_next()
                for j in range(4):
                    kb.tr(ps[:, j * 4:(j + 1) * 4], acc[0:4, (h0 + j) * 128:(h0 + j + 1) * 128], ident[0:4, 0:4], (acc, ident), (ps,))
                kb.copy(ynsaT[:, h0:h0 + 4, cols], ps[:, 0:16].rearrange("p (h q) -> p h q", h=4), (ps,), (ynsaT,))
        ps_mod[0] = 8

    def merge_and_out(T):
        A = arena
        fT = A.alloc("fTm", KC * TMAX)
        t1 = A.alloc("mt1", TMAX); t2 = A.alloc("mt2", TMAX)
        for mb in range(D // 256):
            for bi, (yT, ) in enumerate(((yrwT,), (ynsaT,))):
                sg = ws.take()
                sb_ = ws.take()
                vg = sg[:, 0:16 * 256].rearrange("p (k n) -> p k n", k=16)
                vb = sb_[:, 0:8 * 256].rearrange("p (k n) -> p k n", k=8)
                for mi in range(2):
                    pg = ps_next(); pb = ps_next()
                    for k in range(KC):
                        kb.mm(pg[:, :T], vg[:, k, mi * 128:(mi + 1) * 128], hT[:, k, :T], k == 0, k == KC - 1, (sg, hT), (pg,))
                    for k in range(8):
                        kb.mm(pb[:, :T], vb[:, k, mi * 128:(mi + 1) * 128], yT[:, k, :T], k == 0, k == 7, (sb_, yT), (pb,))
                    m = mb * 2 + mi
                    tt_ = t1 if mi == 0 else t2
                    kb.act(tmpA[:, :T], pg[:, :T], AF.Sigmoid, (pg,), (tmpA,))
                    if bi == 0:
                        kb.tt(tt_[:, :T], tmpA[:, :T], pb[:, :T], ALU.mult, (tmpA, pb), (tt_,))
                    else:
                        kb.tt(tmpA[:, :T], tmpA[:, :T], pb[:, :T], ALU.mult, (tmpA, pb), (tmpA,))
                        kb.tt(mergedT[:, m, :T], tmpA[:, :T], tt_[:, :T], ALU.add, (tmpA, tt_), (mergedT,))
        for mb in range(D // 256):
            so = ws.take()
            vo = so[:, 0:16 * 256].rearrange("p (k n) -> p k n", k=16)
            for mi in range(2):
                po = ps_next()
                for k in range(KC):
                    kb.mm(po[:, :T], vo[:, k, mi * 128:(mi + 1) * 128], mergedT[:, k, :T], k == 0, k == KC - 1, (so, mergedT), (po,))
                m = mb * 2 + mi
                kb.copy(fT[:, m * TMAX:m * TMAX + T], po[:, :T], (po,), (fT,), eng="act")
        post_residual(fT, g_mixpost, 1.0, T)

    if not cfg.get("skip_nsa"):
        nsa_setup()
        nsa_setup_sample()
    for ti, (kind, t0, T) in enumerate(tiles):
        arena.reset()
        stage_box[0] = arena.alloc("stage", KC * TMAX)
        load_x(kind, t0, T)
        ffn(g_f1pre, g_f1post, T, tag=(ti, 1))
        pre_norm(g_mixpre, T)
        arena.reset()
        stage_box[0] = arena.alloc("stage", 8 * TMAX)
        kvT = arena.alloc("kvT", 12 * TMAX, view=lambda a: a.rearrange("p (k t) -> p k t", k=12))
        win_proj_a(T, kvT)
        if kind == "p":
            if not cfg.get("skip_nsa"):
                nsa_cache_update(t0, T, ti, kvT)
            store_tokmajor(kvp.h[t0:t0 + T, :], lambda c: kvT[:, c, :], (kvT,), 8, T)
            if t0 + T > SEQ_ - 512:
                w0 = t0 - (SEQ_ - 512)
                store_tokmajor(winp.h[w0:w0 + T, :], lambda c: kvT[:, 8 + c, :], (kvT,), 4, T)
        else:
            if not cfg.get("skip_nsa"):
                nsa_sample_newrows(kvT)
            store_tokmajor(kvs.h[0:T, :], lambda c: kvT[:, c, :], (kvT,), 8, T)
            store_tokmajor(None, lambda c: kvT[:, 8 + c, :], (kvT,), 4, T)
            stage = stage_box[0]
            for sq_i in range(4):
                kb.dma(wins.h[sq_i, 508:512, :], stage[sq_i * 4:sq_i * 4 + 4, 0:512], (stage,), (), stage)
                kb.dma(wins.h[sq_i, 0:508, :], cwin.h[sq_i, 4:512, :], (), (), stage)
        arena.reset()
        if cfg.get("skip_rwkv"):
            kb.memset(yrwT[:, :, :], 0.0, (yrwT,))
        else:
            rwkv(kind, t0, T, ti)
        arena.reset()
        if kind == "p" and not cfg.get("skip_nsa"):
            nsa_prompt(t0, T, ti)
        if kind == "s" and not cfg.get("skip_nsa"):
            nsa_sample_compress()
            arena.reset_from(smp["mark"])
            nsa_sample_attend()
        arena.reset()
        merge_and_out(T)
        arena.reset()
        stage_box[0] = arena.alloc("stage", KC * TMAX)
        ffn(g_f2pre, g_f2post, T, second=True, tag=(ti, 2))
        if kind == "p":
            store_tokmajor(yp.h[t0:t0 + T, :], lambda c: xT[:, c, :], (xT,), KC, T)
        else:
            store_tokmajor(ys.h[0:T, :], lambda c: xT[:, c, :], (xT,), KC, T)


_NC_CACHE = {}


def _rwmask(C):
    r = np.arange(C)[:, None]
    c = np.arange(C)[None, :]
    return np.concatenate([(r < c), (r <= c), (r > c)], axis=1).astype(np.float32)


def _bucket(d):
    import math
    n = np.maximum(d, 0)
    nf = np.maximum(n, 1).astype(np.float32)
    large = 16 + (np.log(nf / np.float32(16)) / np.float32(math.log(64)) * np.float32(16)).astype(np.int32)
    large = np.minimum(large, 31)
    return np.where(n < 16, n, large)


def _onehot33(d, valid, masked):
    oh = np.zeros((33,) + d.shape, np.float32)
    b = _bucket(d)
    for k in range(32):
        oh[k] = ((b == k) & valid).astype(np.float32)
    oh[32] = np.where(masked, -30000.0, 0.0)
    return oh


def _nsa_consts():
    import ml_dtypes
    y = np.arange(2176); d = 2047 - y
    oh_sel = _onehot33(d, d >= 0, d < 0)
    y = np.arange(768); d = 639 - y
    oh_win = _onehot33(d, (d >= 0) & (d < 512), (d < 0) | (d >= 512))
    q = np.arange(128)[:, None]; jj = np.arange(67)[None, :]
    d = q - 32 * (jj - 63) - 31
    oh_cmp = _onehot33(d, d >= 0, np.zeros_like(d, bool)).reshape(33, 128 * 67)
    mc = (d >= 0).astype(np.float32)
    bc = np.zeros((128, 16, 32), np.float32)
    for i in range(16):
        pos = 128 * i + np.arange(128)[:, None]
        cur = pos // 64
        sb = np.arange(32)[None, :]
        forced = (sb == 0) | (sb == cur) | (sb == cur - 1)
        vis = sb * 64 <= pos
        bc[:, i, :] = np.where(vis, np.where(forced, 1e4, 0.0), -1e30)
    x = np.arange(8704); d = 8195 - x
    oh_s_sel = _onehot33(d, d >= 0, d < 0)
    x = np.arange(1024); d = 515 - x
    oh_s_win = _onehot33(d, (d >= 0) & (d < 512), (d < 0) | (d >= 512))
    t = np.arange(4)[:, None]; n = np.arange(256)[None, :]
    d = 8192 + t - (32 * n + 31)
    oh_s_cmp = _onehot33(d, d >= 0, np.zeros_like(d, bool)).reshape(33, 1024)
    bs = np.zeros((4, 136), np.float32)
    bs[:, [0, 127, 128]] = 1e4
    bs[:, 129:] = -1e30
    return {"oh_s_sel": oh_s_sel, "oh_s_win": oh_s_win, "oh_s_cmp": oh_s_cmp, "bs": bs,
            "iotap": np.arange(128, dtype=np.float32).reshape(128, 1),
            "oh_sel": oh_sel, "oh_win": oh_win, "oh_cmp": oh_cmp, "mc": mc, "bc": bc,
            "identb": np.eye(128, dtype=np.float32).astype(ml_dtypes.bfloat16)}


def _consts():
    bones = np.zeros((128, 128), np.float32)
    bones[:64, :64] = 1.0
    bones[64:, 64:] = 1.0
    return {"ident": np.eye(128, dtype=np.float32), "ones": np.ones((128, 128), dtype=np.float32),
            "rwmask": _rwmask(64), "rwmask4": _rwmask(4), "blockones": bones,
            "selhi": np.concatenate([np.zeros((64, 64), np.float32), np.eye(64, dtype=np.float32)], 0),
            **_nsa_consts()}


def kernel(**inputs):
    TP = 256
    tiles = [("p", t0, TP) for t0 in range(0, SEQ, TP)] + [("s", 0, 16)]
    cfg = {"tiles": tiles}
    nc = build(cfg)
    f32 = np.float32

    def w(name):
        return np.ascontiguousarray(np.asarray(inputs[name], dtype=f32)[0])

    shared = {k: w(k) for k in
              ("ffn1_pre_g", "ffn1_post_g", "ffn1_w1", "ffn1_w3", "ffn1_w2", "mix_pre_g", "mix_post_g", "w_in",
               "rw_mu", "rw_w0", "rw_w2", "rw_a0", "rw_a2", "rw_g2", "rw_k_k", "rw_k_a", "rw_lnx_w", "rw_lnx_b",
               "w_br_rw", "w_br_nsa", "w_out", "ffn2_pre_g", "ffn2_post_g", "ffn2_w1", "ffn2_w3", "ffn2_w2")}
    shared["rw_r_k"] = w("rw_r_k").reshape(RW_DIM)
    for k in ("cmp_pe", "cmp_w1", "cmp_w2"):
        shared[k] = w(k)
    shared["rel_bias"] = np.ascontiguousarray(np.asarray(inputs["rel_bias"], dtype=f32))
    shared["cache_kv"] = np.ascontiguousarray(np.asarray(inputs["cache_kv"], dtype=f32)[0]).reshape(2560, 128, 1024)
    page_table = np.asarray(inputs["page_table"], dtype=np.int32)
    shared.update(_consts())
    x_prompt = np.asarray(inputs["x_prompt"], dtype=f32)
    x_sample = np.asarray(inputs["x_sample"], dtype=f32)
    cache_win = np.asarray(inputs["cache_win"], dtype=f32)
    state_rwkv = np.asarray(inputs["state_rwkv"], dtype=f32)
    state_shift = np.asarray(inputs["state_shift"], dtype=f32)
    in_maps = []
    for c in range(NCORE):
        m = dict(shared)
        m["xp"] = np.ascontiguousarray(x_prompt[c])
        m["xs"] = np.ascontiguousarray(x_sample[4 * c:4 * c + 4].reshape(16, D))
        m["cache_win"] = np.ascontiguousarray(cache_win[0, 4 * c:4 * c + 4].reshape(4, 512, 512))
        m["state_rwkv"] = np.ascontiguousarray(state_rwkv[0, 4 * c:4 * c + 4])
        m["state_shift"] = np.ascontiguousarray(state_shift[0, 4 * c:4 * c + 4])
        m["page_table"] = np.ascontiguousarray(page_table[4 * c:4 * c + 4])
        in_maps.append(m)
    res = run_bass_kernel_spmd(nc, in_maps, core_ids=list(range(NCORE)))
    R = res.results
    y_prompt = np.stack([R[c]["y_prompt"] for c in range(NCORE)], 0)
    y_sample = np.concatenate([R[c]["y_sample"].reshape(4, 4, D) for c in range(NCORE)], 0)
    kv_prompt = np.stack([R[c]["kv_prompt"].reshape(SEQ, 4, 2, 128) for c in range(NCORE)], 0)[None]
    kv_sample = np.concatenate([R[c]["kv_sample"].reshape(4, 4, 4, 2, 128) for c in range(NCORE)], 0)[None]
    win_prompt = np.stack([R[c]["win_prompt"].reshape(512, 2, 2, 128) for c in range(NCORE)], 0)[None]
    win_sample = np.concatenate([R[c]["win_sample"].reshape(4, 512, 2, 2, 128) for c in range(NCORE)], 0)[None]
    rwkv_prompt = np.stack([R[c]["rwkv_prompt"] for c in range(NCORE)], 0)[None]
    rwkv_sample = np.concatenate([R[c]["rwkv_sample"] for c in range(NCORE)], 0)[None]
    shift_prompt = np.stack([R[c]["shift_prompt"] for c in range(NCORE)], 0)[None]
    shift_sample = np.concatenate([R[c]["shift_sample"] for c in range(NCORE)], 0)[None]
    return (y_prompt, y_sample, kv_prompt, kv_sample, win_prompt, win_sample,
            rwkv_prompt, rwkv_sample, shift_prompt, shift_sample)
```

```python
import numpy as np
from contextlib import ExitStack
import concourse.bass as bass
import concourse.mybir as mybir
from concourse.bass_utils import run_bass_kernel_spmd

F32 = mybir.dt.float32
BF16 = mybir.dt.bfloat16
I32 = mybir.dt.int32
AF = mybir.ActivationFunctionType
ALU = mybir.AluOpType
AX = mybir.AxisListType

D = 2048
DFF = 5632
SEQ = 2048
NCORE = 8
RW_DIM = 1024
RW_PROJ = 3328
IN_COLS = 10008
KC = D // 128
EPS = 1e-6

ENGS = ("pe", "act", "dve", "pool", "sp")
SAME_ENGINE_SYNC = True


class Buf:
    def __init__(self, name, h):
        self.name = name
        self.h = h
        self.last_ws = []
        self.readers = []
        self.group_deps = []
        self.sem = None
        self.sem_count = 0
        self.semholder = self

    def __getitem__(self, idx):
        return self.h[idx]


class SemHolder:
    def __init__(self, name):
        self.name = name
        self.sem = None
        self.sem_count = 0


class _AliasBuf:
    def __init__(self, base, view):
        object.__setattr__(self, "_base", base)
        object.__setattr__(self, "h", view.h)

    def __getattr__(self, k):
        return getattr(object.__getattribute__(self, "_base"), k)

    def __setattr__(self, k, v):
        setattr(object.__getattribute__(self, "_base"), k, v)

    def __getitem__(self, idx):
        return object.__getattribute__(self, "h")[idx]


def _alias(base, view):
    return _AliasBuf(base, view)


class Op:
    __slots__ = ("eng", "fn", "deps", "is_dma", "sem_buf", "count", "signaled", "value", "idx")


class Prog:
    def __init__(self, nc, stack):
        self.nc = nc
        self.stack = stack
        self.eng_ops = {e: [] for e in ENGS}
        self.bufs = []
        self.nops = 0

    def sbuf(self, name, shape, dtype):
        h = self.stack.enter_context(self.nc.sbuf_tensor(name, list(shape), dtype))
        b = Buf(name, h)
        self.bufs.append(b)
        return b

    def psum(self, name, shape, dtype):
        h = self.stack.enter_context(self.nc.psum_tensor(name, list(shape), dtype))
        b = Buf(name, h)
        self.bufs.append(b)
        return b

    def dram(self, name, shape, dtype, kind):
        h = self.nc.dram_tensor(name, list(shape), dtype, kind=kind)
        b = Buf(name, h.ap())
        b.t = h
        self.bufs.append(b)
        return b

    def op(self, eng, fn, reads=(), writes=(), dma=False, sem_buf=None):
        o = Op()
        o.eng, o.fn, o.is_dma, o.signaled, o.value = eng, fn, dma, False, None
        o.idx = self.nops
        self.nops += 1
        deps = []
        for b in reads:
            deps.extend(b.last_ws)
        for b in writes:
            concurrent = (dma and b.last_ws and not b.readers and all(w.is_dma for w in b.last_ws))
            if concurrent:
                deps.extend(b.group_deps)
            else:
                deps.extend(b.last_ws)
                deps.extend(b.readers)
        seen = set()
        o.deps = []
        for d in deps:
            if id(d) in seen:
                continue
            seen.add(id(d))
            if (not d.is_dma) and d.eng == eng:
                if eng == "pe" or eng == "sp" or not SAME_ENGINE_SYNC:
                    continue
            o.deps.append(d)
        for b in reads:
            b.readers.append(o)
        for b in writes:
            concurrent = (dma and b.last_ws and not b.readers and all(w.is_dma for w in b.last_ws))
            if concurrent:
                b.last_ws.append(o)
            else:
                b.group_deps = list(b.last_ws) + list(b.readers)
                b.last_ws = [o]
                b.readers = []
        if dma:
            if sem_buf is None:
                raise ValueError("dma needs sem_buf")
            hold = sem_buf.semholder
            o.sem_buf = hold
            hold.sem_count += 16
            o.count = hold.sem_count
        self.eng_ops[eng].append(o)
        return o

    def emit(self):
        nc = self.nc
        for e in ENGS:
            for o in self.eng_ops[e]:
                for d in o.deps:
                    if not d.is_dma:
                        d.signaled = True
        for e in ENGS:
            n = 0
            for o in self.eng_ops[e]:
                if o.is_dma:
                    continue
                if o.signaled:
                    n += 1
                    o.value = n
        esem = {e: self.stack.enter_context(nc.semaphore("es_" + e)) for e in ENGS}
        holders, seen = [], set()
        for b in self.bufs:
            h = b.semholder
            if id(h) not in seen and h.sem_count > 0:
                seen.add(id(h))
                holders.append(h)
        for h in holders:
            h.sem = self.stack.enter_context(nc.semaphore("ds_" + h.name))
        dma_bufs = holders
        prog = self

        def run_engine(e, eng):
            waited = {}
            for o in prog.eng_ops[e]:
                need = {}
                for d in o.deps:
                    if d.is_dma:
                        s, v = d.sem_buf.sem, d.count
                    else:
                        s, v = esem[d.eng], d.value
                    key = id(s)
                    if key not in need or need[key][1] < v:
                        need[key] = (s, v)
                for key, (s, v) in need.items():
                    if waited.get(key, 0) >= v:
                        continue
                    waited[key] = v
                    eng.wait_ge(s, v)
                ins = o.fn(eng)
                if o.is_dma:
                    ins.then_inc(o.sem_buf.sem, 16)
                elif o.signaled:
                    ins.then_inc(esem[e], 1)
            if e == "sp":
                for b in dma_bufs:
                    eng.wait_ge(b.sem, b.sem_count)

        with nc.Block() as block:
            @block.tensor
            def _(eng):
                run_engine("pe", eng)

            @block.scalar
            def _(eng):
                run_engine("act", eng)

            @block.vector
            def _(eng):
                run_engine("dve", eng)

            @block.gpsimd
            def _(eng):
                run_engine("pool", eng)

            @block.sync
            def _(eng):
                run_engine("sp", eng)


class K:
    def __init__(self, P):
        self.P = P

    def dma(self, dst_ap, src_ap, reads, writes, sem_buf, eng="sp", nc_ok=False):
        def fn(e):
            if nc_ok:
                return e.dma_start(out=dst_ap, in_=src_ap, allow_slow_non_contiguous=True)
            return e.dma_start(out=dst_ap, in_=src_ap)
        return self.P.op(eng, fn, reads, writes, dma=True, sem_buf=sem_buf)

    def mm(self, out_ap, lhsT_ap, rhs_ap, start, stop, reads, writes):
        return self.P.op("pe", lambda e: e.matmul(out_ap, lhsT_ap, rhs_ap, start=start, stop=stop), reads, writes)

    def tr(self, out_ap, in_ap, ident_ap, reads, writes):
        return self.P.op("pe", lambda e: e.transpose(out_ap, in_ap, ident_ap), reads, writes)

    def act(self, out_ap, in_ap, func, reads, writes, bias=None, scale=None, accum=None):
        def fn(e):
            kw = {}
            if bias is not None:
                kw["bias"] = bias
            if scale is not None:
                kw["scale"] = scale
            if accum is not None:
                kw["accum_out"] = accum
            return e.activation(out_ap, in_ap, func, **kw)
        return self.P.op("act", fn, reads, writes)

    def tt(self, out_ap, a_ap, b_ap, op, reads, writes, eng="dve"):
        return self.P.op(eng, lambda e: e.tensor_tensor(out_ap, a_ap, b_ap, op), reads, writes)

    def ts(self, out_ap, a_ap, s1, s2, op0, op1, reads, writes, eng="dve"):
        if op1 is None:
            return self.P.op(eng, lambda e: e.tensor_scalar(out_ap, a_ap, s1, None, op0), reads, writes)
        return self.P.op(eng, lambda e: e.tensor_scalar(out_ap, a_ap, s1, s2, op0, op1), reads, writes)

    def stt(self, out_ap, a_ap, s, b_ap, op0, op1, reads, writes, eng="dve"):
        return self.P.op(eng, lambda e: e.scalar_tensor_tensor(out_ap, a_ap, s, b_ap, op0, op1), reads, writes)

    def copy(self, out_ap, in_ap, reads, writes, eng="dve"):
        if eng == "act":
            return self.P.op("act", lambda e: e.copy(out_ap, in_ap), reads, writes)
        return self.P.op(eng, lambda e: e.tensor_copy(out_ap, in_ap), reads, writes)

    def reduce(self, out_ap, in_ap, op, reads, writes):
        return self.P.op("dve", lambda e: e.tensor_reduce(out_ap, in_ap, AX.X, op), reads, writes)

    def recip(self, out_ap, in_ap, reads, writes):
        return self.P.op("dve", lambda e: e.reciprocal(out_ap, in_ap), reads, writes)

    def memset(self, ap, val, writes, eng="dve"):
        return self.P.op(eng, lambda e: e.memset(ap, val), (), writes)


class Arena:
    def __init__(self, P, name, words):
        self.P = P
        self.t = P.stack.enter_context(P.nc.sbuf_tensor(name, [128, words], F32))
        self.words = words
        self.off = 0
        self.live = []
        self.pending = []
        self.n = 0
        self.holders = {}

    def reset(self):
        ops = list(self.pending)
        for b in self.live:
            ops += b.last_ws + b.readers
        best, dmas, seen = {}, [], set()
        for o in ops:
            if o.is_dma:
                if id(o) not in seen:
                    seen.add(id(o))
                    dmas.append(o)
            elif o.eng not in best or o.idx > best[o.eng].idx:
                best[o.eng] = o
        bd = {}
        for o in dmas:
            k = id(o.sem_buf)
            if k not in bd or o.count > bd[k].count:
                bd[k] = o
        self.pending = list(best.values()) + list(bd.values())
        self.live = []
        self.off = 0

    def reset_from(self, off):
        keep = [b for b in self.live if b._off < off]
        drop = [b for b in self.live if b._off >= off]
        self.live = drop
        save_pending = self.pending
        self.reset()
        self.live = keep
        self.off = off

    def alloc(self, name, words, view=None):
        assert self.off + words <= self.words, (name, self.off, words, self.words)
        ap = self.t[:, self.off:self.off + words]
        off0 = self.off
        self.off += words
        if view is not None:
            ap = view(ap)
        self.n += 1
        b = Buf("%s_%d" % (name, self.n), ap)
        if name not in self.holders:
            self.holders[name] = SemHolder("ar_" + name)
        b.semholder = self.holders[name]
        b._off = off0
        b.last_ws = list(self.pending)
        self.P.bufs.append(b)
        self.live.append(b)
        return b


class WeightStream:
    def __init__(self, kb, nslots, slot_elems):
        self.kb = kb
        self.base = [kb.P.sbuf("wslot%d" % i, [128, slot_elems], BF16) for i in range(nslots)]
        self.extra = []
        self.extra_tag = None
        self.plan = []
        self.where = {}
        self.held = {}
        self.issued = 0
        self.taken = 0

    def add(self, fn, tag=None):
        self.plan.append((fn, tag))

    def set_extra(self, slots, tag):
        self.extra = list(slots)
        self.extra_tag = tag

    def clear_extra(self):
        for s_ in self.extra:
            b = self.held.get(id(s_))
            assert b is None or b < self.taken, "extra slot still holds an untaken block"
            self.held.pop(id(s_), None)
        self.extra = []
        self.extra_tag = None

    def prefetch(self):
        while self.issued < len(self.plan):
            fn, tag = self.plan[self.issued]
            cands = self.base + (self.extra if (tag is not None and tag == self.extra_tag) else [])
            slot = None
            for c in cands:
                b = self.held.get(id(c))
                if b is None or b < self.taken - 1:
                    slot = c
                    break
            if slot is None:
                break
            for dst_ap, src_ap in fn(slot):
                self.kb.dma(dst_ap, src_ap, (), (slot,), slot, eng="pool")
            self.held[id(slot)] = self.issued
            self.where[self.issued] = slot
            self.issued += 1

    def take(self):
        self.prefetch()
        assert self.taken in self.where, "weight block not issued"
        slot = self.where.pop(self.taken)
        self.taken += 1
        return slot


def build(cfg):
    nc = bass.Bass("TRN2", target_bir_lowering=False)
    with ExitStack() as stack:
        P = Prog(nc, stack)
        kb = K(P)
        _build_body(nc, P, kb, cfg)
        P.emit()
    return nc


def _build_body(nc, P, kb, cfg):
    tiles = cfg["tiles"]
    SEQ_ = cfg.get("seq", SEQ)
    TMAX = max([t[2] for t in tiles] + [256])
    def din(name, shape, dt=F32):
        return P.dram(name, shape, dt, "ExternalInput")

    def dout(name, shape, dt=F32):
        return P.dram(name, shape, dt, "ExternalOutput")

    xp = din("xp", [SEQ, D])
    xs = din("xs", [16, D])
    ident_d = din("ident", [128, 128])
    ones_d = din("ones", [128, 128])
    f1_pre = din("ffn1_pre_g", [D]); f1_post = din("ffn1_post_g", [D])
    f1_w1 = din("ffn1_w1", [D, DFF]); f1_w3 = din("ffn1_w3", [D, DFF]); f1_w2 = din("ffn1_w2", [DFF, D])
    mix_pre = din("mix_pre_g", [D]); mix_post = din("mix_post_g", [D])
    w_in = din("w_in", [D, IN_COLS])

    yp = dout("y_prompt", [SEQ, D])
    ys = dout("y_sample", [16, D])
    kvp = dout("kv_prompt", [SEQ, 1024])
    kvs = dout("kv_sample", [16, 1024])
    winp = dout("win_prompt", [512, 512])
    shp = dout("shift_prompt", [RW_PROJ])
    shs = dout("shift_sample", [4, RW_PROJ])
    wins = dout("win_sample", [4, 512, 512])
    cwin = din("cache_win", [4, 512, 512])
    rwp = dout("rwkv_prompt", [16, 64, 64])
    rws = dout("rwkv_sample", [4, 16, 64, 64])
    st_rw = din("state_rwkv", [4, 16, 64, 64])
    st_sh = din("state_shift", [4, RW_PROJ])
    mask_d = din("rwmask", [64, 192])
    mask4_d = din("rwmask4", [4, 12])
    selhi_d = din("selhi", [128, 64])
    identb_d = din("identb", [128, 128], BF16)
    cmp_pe = din("cmp_pe", [32, 2, 128]); cmp_w1 = din("cmp_w1", [2, 4096, 256]); cmp_w2 = din("cmp_w2", [2, 256, 128])
    rel_bias = din("rel_bias", [32, 8])
    ohs_d = din("oh_sel", [33, 2176]); ohw_d = din("oh_win", [33, 768]); ohc_d = din("oh_cmp", [33, 128 * 67])
    mc_d = din("mc", [128, 67]); bc_d = din("bc", [128, 16, 32])
    NPOOL = cfg.get("npool", 2560)
    ckv = din("cache_kv", [NPOOL, 128, 1024])
    ptab = din("page_table", [4, 64], I32)
    ohss_d = din("oh_s_sel", [33, 8704]); ohsw_d = din("oh_s_win", [33, 1024]); ohsc_d = din("oh_s_cmp", [33, 1024])
    bs_d = din("bs", [4, 136]); iotap_d = din("iotap", [128, 1])
    bones_d = din("blockones", [128, 128])
    rw_mu = din("rw_mu", [RW_PROJ]); rw_w0 = din("rw_w0", [RW_DIM]); rw_w2 = din("rw_w2", [64, RW_DIM])
    rw_a0 = din("rw_a0", [RW_DIM]); rw_a2 = din("rw_a2", [64, RW_DIM]); rw_g2 = din("rw_g2", [128, RW_DIM])
    rw_k_k = din("rw_k_k", [RW_DIM]); rw_k_a = din("rw_k_a", [RW_DIM]); rw_r_k = din("rw_r_k", [RW_DIM])
    rw_lnx_w = din("rw_lnx_w", [RW_DIM]); rw_lnx_b = din("rw_lnx_b", [RW_DIM])
    w_br_rw = din("w_br_rw", [RW_DIM, D]); w_br_nsa = din("w_br_nsa", [1024, D]); w_out = din("w_out", [D, D])
    f2_pre = din("ffn2_pre_g", [D]); f2_post = din("ffn2_post_g", [D])
    f2_w1 = din("ffn2_w1", [D, DFF]); f2_w3 = din("ffn2_w3", [D, DFF]); f2_w2 = din("ffn2_w2", [DFF, D])

    psb = [P.psum("ps%d" % i, [128, 512], F32) for i in range(8)]
    ps_i = [0]

    ps_mod = [8]

    def ps_next():
        b = psb[ps_i[0] % ps_mod[0]]
        ps_i[0] += 1
        return b

    ident = P.sbuf("ident_sb", [128, 128], F32)
    ones = P.sbuf("ones_sb", [128, 128], F32)
    kb.dma(ident[:], ident_d[:], (), (ident,), ident)
    kb.dma(ones[:], ones_d[:], (), (ones,), ones)

    vstg = P.sbuf("vstg", [32, 128], F32)
    vstg2 = P.sbuf("vstg2", [32, 128], F32)

    def load_col(dst_ap, dst_buf, src_row_ap, k):
        kb.dma(vstg[0:k, :], src_row_ap.rearrange("(k p) -> k p", p=128), (), (vstg,), vstg)
        ps = ps_next()
        kb.tr(ps[:, 0:k], vstg[0:k, :], ident[0:k, 0:k], (vstg, ident), (ps,))
        kb.copy(dst_ap, ps[:, 0:k], (ps,), (dst_buf,))

    def store_col(dst_row_ap, src_ap, src_buf, k):
        ps = ps_next()
        kb.tr(ps[0:k, 0:128], src_ap, ident[:, :], (src_buf, ident), (ps,))
        kb.copy(vstg2[0:k, :], ps[0:k, 0:128], (ps,), (vstg2,))
        kb.dma(dst_row_ap.rearrange("(k p) -> k p", p=128), vstg2[0:k, :], (vstg2,), (), vstg2)

    def load_vec(name, d_buf, n):
        t = P.sbuf(name, [128, n // 128], F32)
        load_col(t[:, :], t, d_buf.h, n // 128)
        return t

    g_f1pre = load_vec("g_f1pre", f1_pre, D)
    g_f1post = load_vec("g_f1post", f1_post, D)
    g_mixpre = load_vec("g_mixpre", mix_pre, D)
    g_mixpost = load_vec("g_mixpost", mix_post, D)
    g_f2pre = load_vec("g_f2pre", f2_pre, D)
    g_f2post = load_vec("g_f2post", f2_post, D)
    v_mu = load_vec("v_mu", rw_mu, RW_PROJ)
    v_w0 = load_vec("v_w0", rw_w0, RW_DIM); v_a0 = load_vec("v_a0", rw_a0, RW_DIM)
    v_kk = load_vec("v_kk", rw_k_k, RW_DIM); v_ka = load_vec("v_ka", rw_k_a, RW_DIM)
    v_rk = load_vec("v_rk", rw_r_k, RW_DIM)
    v_lw = load_vec("v_lw", rw_lnx_w, RW_DIM); v_lb = load_vec("v_lb", rw_lnx_b, RW_DIM)
    w2z = P.sbuf("w2z", [128, RW_DIM], F32)
    a2z = P.sbuf("a2z", [128, RW_DIM], F32)
    kb.memset(w2z[:, :], 0.0, (w2z,))
    kb.memset(a2z[:, :], 0.0, (a2z,))
    kb.dma(w2z[0:64, :], rw_w2.h[:, :], (w2z,), (w2z,), w2z)
    kb.dma(a2z[64:128, :], rw_a2.h[:, :], (a2z,), (a2z,), a2z)
    selhi = P.sbuf("selhi_sb", [128, 64], F32)
    kb.dma(selhi[:], selhi_d.h[:, :], (), (selhi,), selhi)
    g2sb = P.sbuf("g2sb", [128, RW_DIM], F32)
    kb.dma(g2sb[:], rw_g2.h[:, :], (), (g2sb,), g2sb)
    rwmask = P.sbuf("rwmask_sb", [64, 192], F32)
    kb.dma(rwmask[:], mask_d.h[:, :], (), (rwmask,), rwmask)
    identb = P.sbuf("identb_sb", [128, 128], BF16)
    kb.dma(identb[:], identb_d.h[:, :], (), (identb,), identb)
    bones = P.sbuf("bones_sb", [128, 128], F32)
    kb.dma(bones[:], bones_d.h[:, :], (), (bones,), bones)

    xT = P.sbuf("xT", [128, KC, TMAX], F32)
    hT = P.sbuf("hT", [128, KC, TMAX], BF16)
    big1 = P.sbuf("big1", [128, 26 * TMAX], F32)
    gT = Buf("gTv", big1.h[:, 0:(DFF // 128) * TMAX // 2].bitcast(BF16).rearrange("p (k t) -> p k t", k=DFF // 128))
    pT = Buf("pTv", big1.h[:, :].rearrange("p (k t) -> p k t", k=26))
    gT = _alias(big1, gT); pT = _alias(big1, pT)
    mergedT = Buf("mTv", big1.h[:, 0:KC * TMAX // 2].bitcast(BF16).rearrange("p (k t) -> p k t", k=KC))
    mergedT = _alias(big1, mergedT)
    arena = Arena(P, "arena", cfg.get("arena_words", 17408))
    stage_box = [None]
    rstd = P.sbuf("rstd", [128, TMAX], F32)
    sq = P.sbuf("sq", [128, TMAX], F32)
    tmpA = P.sbuf("tmpA", [128, TMAX], F32)
    SLOT = 16 * 256
    ws = WeightStream(kb, 3, SLOT)

    def plan_ffn(w1, w3, w2, tag=None):
        w1v = w1.h.rearrange("(k p) n -> p k n", p=128)
        w3v = w3.h.rearrange("(k p) n -> p k n", p=128)
        w2v = w2.h.rearrange("(k p) n -> p k n", p=128)
        for j in range(DFF // 256):
            for wv in (w1v, w3v):
                def fn(slot, wv=wv, j=j):
                    dst = slot[:, 0:16 * 256].rearrange("p (k n) -> p k n", k=16)
                    return [(dst, wv[:, :, j * 256:(j + 1) * 256])]
                ws.add(fn, tag)
        for mb in range(D // 256):
            for half in range(4):
                def fn(slot, mb=mb, half=half):
                    dst = slot[:, 0:11 * 256].rearrange("p (k n) -> p k n", k=11)
                    return [(dst, w2v[:, half * 11:(half + 1) * 11, mb * 256:(mb + 1) * 256])]
                ws.add(fn, tag)

    WIN_BLOCKS = [(c0, min(256, 5888 - c0)) for c0 in range(0, 5888, 256)]

    def plan_win_a():
        wv = w_in.h.rearrange("(k p) n -> p k n", p=128)
        for (c0, w) in WIN_BLOCKS:
            def fn(slot, c0=c0, w=w):
                dst = slot[:, 0:16 * w].rearrange("p (k n) -> p k n", k=16)
                return [(dst, wv[:, :, c0:c0 + w])]
            ws.add(fn)

    def plan_merge():
        wv = w_in.h.rearrange("(k p) n -> p k n", p=128)
        brv = [w_br_rw.h.rearrange("(k p) n -> p k n", p=128), w_br_nsa.h.rearrange("(k p) n -> p k n", p=128)]
        wov = w_out.h.rearrange("(k p) n -> p k n", p=128)
        for mb in range(D // 256):
            for bi in range(2):
                c0 = (5912 if bi == 0 else 7960) + mb * 256

                def fn(slot, c0=c0):
                    dst = slot[:, 0:16 * 256].rearrange("p (k n) -> p k n", k=16)
                    return [(dst, wv[:, :, c0:c0 + 256])]
                ws.add(fn)

                def fn2(slot, bi=bi, mb=mb):
                    dst = slot[:, 0:8 * 256].rearrange("p (k n) -> p k n", k=8)
                    return [(dst, brv[bi][:, :, mb * 256:(mb + 1) * 256])]
                ws.add(fn2)
        for mb in range(D // 256):
            def fn3(slot, mb=mb):
                dst = slot[:, 0:16 * 256].rearrange("p (k n) -> p k n", k=16)
                return [(dst, wov[:, :, mb * 256:(mb + 1) * 256])]
            ws.add(fn3)

    def plan_cmp():
        for kv_ in range(2):
            for half in range(2):
                def fn(slot, kv_=kv_, half=half):
                    dst = slot[:, 0:16 * 256].rearrange("p (r c) -> p r c", r=16)
                    src = cmp_w1.h[kv_, half * 2048:(half + 1) * 2048, :].rearrange("(r p) c -> p r c", p=128)
                    return [(dst, src)]
                ws.add(fn)

    for ti_, (kind_, _, _) in enumerate(tiles):
        plan_ffn(f1_w1, f1_w3, f1_w2, (ti_, 1))
        plan_win_a()
        if kind_ == "p" and not cfg.get("skip_nsa"):
            plan_cmp()
        plan_merge()
        plan_ffn(f2_w1, f2_w3, f2_w2, (ti_, 2))

    def rms_stats(src_chunk_ap, src_bufs, T):
        ps = ps_next()
        for k in range(KC):
            kb.act(sq[:, :T], src_chunk_ap(k), AF.Square, src_bufs, (sq,))
            kb.mm(ps[:, :T], ones[:, :], sq[:, :T], k == 0, k == KC - 1, (ones, sq), (ps,))
        kb.ts(sq[:, :T], ps[:, :T], 1.0 / D, EPS, ALU.mult, ALU.add, (ps,), (sq,))
        kb.act(sq[:, :T], sq[:, :T], AF.Sqrt, (sq,), (sq,))
        P.op("dve", lambda e: e.reciprocal(rstd[:, :T], sq[:, :T]), (sq,), (rstd,))

    def pre_norm(gvec, T):
        rms_stats(lambda k: xT[:, k, :T], (xT,), T)
        for k in range(KC):
            kb.stt(hT[:, k, :T], xT[:, k, :T], gvec[:, k:k + 1], rstd[:, :T], ALU.mult, ALU.mult,
                   (xT, gvec, rstd), (hT,))

    def post_residual(fT, gpost, coef, T):
        rms_stats(lambda k: fT[:, k * TMAX:k * TMAX + T], (fT,), T)
        for k in range(KC):
            kb.stt(tmpA[:, :T], fT[:, k * TMAX:k * TMAX + T], gpost[:, k:k + 1], rstd[:, :T], ALU.mult, ALU.mult,
                   (fT, gpost, rstd), (tmpA,))
            kb.stt(xT[:, k, :T], tmpA[:, :T], coef, xT[:, k, :T], ALU.mult, ALU.add, (tmpA, xT), (xT,))

    def ffn(gpre, gpost, T, second=False, tag=None):
        fT = stage_box[0]
        nx = (arena.words - arena.off) // (SLOT // 2)
        if tag is not None and nx > 0 and not cfg.get("no_extra_slots"):
            ws.set_extra([arena.alloc("wsx%d" % i, SLOT // 2, view=lambda a: a.bitcast(BF16)) for i in range(nx)], tag)
        pre_norm(gpre, T)
        for j in range(DFF // 256):
            s1 = ws.take()
            s3 = ws.take()
            v1 = s1[:, 0:16 * 256].rearrange("p (k n) -> p k n", k=16)
            v3 = s3[:, 0:16 * 256].rearrange("p (k n) -> p k n", k=16)
            for mi in range(2):
                p1 = ps_next()
                p3 = ps_next()
                for k in range(KC):
                    kb.mm(p1[:, :T], v1[:, k, mi * 128:(mi + 1) * 128], hT[:, k, :T], k == 0, k == KC - 1, (s1, hT), (p1,))
                for k in range(KC):
                    kb.mm(p3[:, :T], v3[:, k, mi * 128:(mi + 1) * 128], hT[:, k, :T], k == 0, k == KC - 1, (s3, hT), (p3,))
                kb.act(tmpA[:, :T], p1[:, :T], AF.Silu, (p1,), (tmpA,))
                kb.tt(gT[:, j * 2 + mi, :T], tmpA[:, :T], p3[:, :T], ALU.mult, (tmpA, p3), (gT,))
        for mb in range(D // 256):
            pa = ps_next()
            pb = ps_next()
            for half in range(4):
                s2 = ws.take()
                v2 = s2[:, 0:11 * 256].rearrange("p (k n) -> p k n", k=11)
                for mi, pp in ((0, pa), (1, pb)):
                    for k in range(11):
                        kk = half * 11 + k
                        kb.mm(pp[:, :T], v2[:, k, mi * 128:(mi + 1) * 128], gT[:, kk, :T], kk == 0, kk == 43, (s2, gT), (pp,))
            for mi, pp in ((0, pa), (1, pb)):
                m = mb * 2 + mi
                kb.copy(fT[:, m * TMAX:m * TMAX + T], pp[:, :T], (pp,), (fT,), eng="act")
        ws.clear_extra()
        post_residual(fT, gpost, 0.5, T)

    def load_x(kind, t0, T):
        src = xp if kind == "p" else xs
        stage = stage_box[0]
        nsub = (T + 127) // 128
        st = stage[:, 0:nsub * D].rearrange("p (s f) -> p s f", s=nsub)
        rows = min(T, 128)
        if T >= 128:
            kb.dma(st, src.h[t0:t0 + T, :].rearrange("(s p) f -> p s f", p=128), (), (stage,), stage)
        else:
            kb.dma(stage[0:T, 0:D], src.h[t0:t0 + T, :], (), (stage,), stage)
        for k in range(KC):
            ps = ps_next()
            for s in range(nsub):
                kb.tr(ps[:, s * 128:s * 128 + rows], st[0:rows, s, k * 128:(k + 1) * 128], ident[0:rows, 0:rows],
                      (stage, ident), (ps,))
            kb.copy(xT[:, k, :T], ps[:, :T], (ps,), (xT,), eng=("act" if k % 2 else "dve"))

    def store_tokmajor(dst_rows_ap, src_chunk_ap, src_bufs, nchunks, T):
        stage = stage_box[0]
        nsub = (T + 127) // 128
        rows = min(T, 128)
        W = nchunks * 128
        st = stage[:, 0:nsub * W].rearrange("p (s f) -> p s f", s=nsub)
        for s in range(nsub):
            for c0 in range(0, nchunks, 4):
                ps = ps_next()
                nn = min(4, nchunks - c0)
                for c in range(nn):
                    kb.tr(ps[0:rows, c * 128:(c + 1) * 128], src_chunk_ap(c0 + c)[:, s * 128:s * 128 + rows], ident[:, :],
                          src_bufs + (ident,), (ps,))
                kb.copy(st[0:rows, s, c0 * 128:(c0 + nn) * 128], ps[0:rows, 0:nn * 128], (ps,), (stage,),
                        eng=("act" if (c0 // 4) % 2 else "dve"))
        if dst_rows_ap is None:
            return
        if T >= 128:
            kb.dma(dst_rows_ap.rearrange("(s p) f -> p s f", p=128), st, (stage,), (), stage)
        else:
            kb.dma(dst_rows_ap, stage[0:T, 0:W], (stage,), (), stage)

    qT = P.sbuf("qT", [128, 8, TMAX], BF16)
    yrwT = P.sbuf("yrwT", [128, 8, TMAX], BF16)
    ynsaT = P.sbuf("ynsaT", [128, 8, TMAX], BF16)
    carry = [P.sbuf("carry0", [128, 26], F32), P.sbuf("carry1", [128, 26], F32)]
    S0T = P.sbuf("S0T", [64, 16, 64], F32)
    kb.memset(carry[0][:, :], 0.0, (carry[0],))
    kb.memset(S0T[:, :, :], 0.0, (S0T,))
    kb.memset(ynsaT[:, :, :], 0.0, (ynsaT,))
    rwmask4 = P.sbuf("rwmask4_sb", [4, 12], F32)
    kb.dma(rwmask4[:], mask4_d.h[:, :], (), (rwmask4,), rwmask4)

    def win_proj_a(T, kvT):
        for (c0, w) in WIN_BLOCKS:
            s_ = ws.take()
            v = s_[:, 0:16 * w].rearrange("p (k n) -> p k n", k=16)
            for mi in range(w // 128):
                ps = ps_next()
                for k in range(KC):
                    kb.mm(ps[:, :T], v[:, k, mi * 128:(mi + 1) * 128], hT[:, k, :T], k == 0, k == KC - 1, (s_, hT), (ps,))
                ci = c0 // 128 + mi
                eng = "act" if ci % 2 else "dve"
                if ci < 26:
                    kb.copy(pT[:, ci, :T], ps[:, :T], (ps,), (pT,), eng=eng)
                elif ci < 34:
                    kb.copy(qT[:, ci - 26, :T], ps[:, :T], (ps,), (qT,), eng=eng)
                else:
                    kb.copy(kvT[:, ci - 34, :T], ps[:, :T], (ps,), (kvT,), eng=eng)

    def bc3(ap2, n):
        return ap2.unsqueeze(2).to_broadcast([ap2.shape[0], ap2.shape[1], n])

    def bcmid(ap2, n):
        return ap2.unsqueeze(1).to_broadcast([ap2.shape[0], n, ap2.shape[1]])

    def ts1(out_ap, in_ap, scalar, op, reads, writes, eng="dve"):
        return P.op(eng, lambda e: e.tensor_single_scalar(out_ap, in_ap, scalar, op), reads, writes)

    def rwkv(kind, t0, T, tile_idx):
        A = arena
        C = 64 if kind == "p" else 4
        nch = T // C
        nd = 5 if C == 64 else 1
        mk = rwmask if C == 64 else rwmask4
        mkU = mk[0:C, 0:2 * C]
        mkL = mk[0:C, 2 * C:3 * C]

        def fm(nm):
            return A.alloc(nm, 512, view=lambda a: a.rearrange("p (k c) -> p k c", k=8))

        logw = fm("logw"); a_ = fm("a_"); kk_ = fm("kk_")
        Eg = fm("Eg"); Einv = fm("Einv"); BtT = fm("BtT"); KtT = fm("KtT"); tmp = fm("tmp"); tmp2 = fm("tmp2")
        La, Lb = tmp, tmp2
        gate_, bonus_ = kk_, a_
        AR = A.alloc("AR", 1024, view=lambda a: a.rearrange("p (k two c) -> p k two c", k=8, two=2))
        ARH = A.alloc("ARH", 1024, view=lambda a: a.rearrange("p (k two c) -> p k two c", k=8, two=2))
        BtH = fm("BtH"); KtH = fm("KtH")
        gh = A.alloc("gh", 8)
        tw = A.alloc("tw", 64); sgd = A.alloc("sgd", 64)

        def tm(nm, w=64):
            return A.alloc(nm, 8 * w, view=lambda a: a.rearrange("p (h c) -> p h c", h=8))

        Vtok = tm("Vtok"); Bttok = tm("Bttok"); Kttok = tm("Kttok"); XT = tm("XT"); UT = tm("UT")
        Nm = tm("Nm"); NT = tm("NT"); Pa = tm("Pa"); PTa = tm("PTa"); Rm = tm("Rm")
        Yt, Yc = XT, Pa
        mb_off = A.off
        MBm = tm("MBm", 128); MKm = tm("MKm", 128)
        nat16 = _alias(MBm, Buf("nat16v", A.t[:, mb_off:mb_off + 1024].rearrange("p (h c) -> p h c", h=16)))
        st8 = A.alloc("st8", 32, view=lambda a: a.rearrange("p (h c) -> p h c", h=8))
        lvl = cfg.get("rw_stop", 99)

        def fmop(X, XH, h):
            return (X if h % 2 == 0 else XH), h // 2

        cur, nxt = carry[tile_idx % 2], carry[(tile_idx + 1) % 2]
        dtmp = tmpA
        if kind == "p":
            kb.copy(nxt[:, :], pT[:, :, T - 1], (pT,), (nxt,))
            for c in range(26):
                kb.tt(dtmp[:, 1:T], pT[:, c, 0:T - 1], pT[:, c, 1:T], ALU.subtract, (pT,), (dtmp,))
                kb.tt(dtmp[:, 0:1], cur[:, c:c + 1], pT[:, c, 0:1], ALU.subtract, (pT, cur), (dtmp,))
                kb.stt(pT[:, c, :T], dtmp[:, :T], v_mu[:, c:c + 1], pT[:, c, :T], ALU.mult, ALU.add, (dtmp, v_mu, pT), (pT,))
            if t0 + T == SEQ_:
                store_col(shp.h, nxt[:, :], nxt, 26)
        else:
            sh0 = A.alloc("sh0", 26 * 4, view=lambda a: a.rearrange("p (k s) -> p k s", k=26))
            sho = A.alloc("sho", 26 * 4, view=lambda a: a.rearrange("p (s k) -> p s k", s=4))
            for sq_i in range(4):
                load_col(sh0[:, :, sq_i], sh0, st_sh.h[sq_i, :], 26)
            p4 = pT[:, :, 0:16].rearrange("p k (s t) -> p k s t", t=4)
            for sq_i in range(4):
                kb.copy(sho[:, sq_i, :], p4[:, :, sq_i, 3], (pT,), (sho,))
                store_col(shs.h[sq_i, :], sho[:, sq_i, :], sho, 26)
            d4 = dtmp[:, 0:16].rearrange("p (s t) -> p s t", t=4)
            for c in range(26):
                kb.tt(d4[:, :, 1:4], p4[:, c, :, 0:3], p4[:, c, :, 1:4], ALU.subtract, (pT,), (dtmp,))
                kb.tt(d4[:, :, 0], sh0[:, c, :], p4[:, c, :, 0], ALU.subtract, (pT, sh0), (dtmp,))
                kb.stt(pT[:, c, :T], dtmp[:, :T], v_mu[:, c:c + 1], pT[:, c, :T], ALU.mult, ALU.add, (dtmp, v_mu, pT), (pT,))
        if lvl <= 1:
            return

        def heads_T(src, dst, src_bufs, dst_bufs):
            for g in range(2):
                ps = ps_next()
                for q in range(8):
                    kb.tr(ps[0:64, q * 64:(q + 1) * 64], src[0:64, g * 8 + q, :], ident[0:64, 0:64], src_bufs + (ident,), (ps,))
                kb.copy(dst[0:64, g * 8:g * 8 + 8, :], ps[0:64, :].rearrange("p (h c) -> p h c", h=8), (ps,), dst_bufs,
                        eng=("act" if g else "dve"))

        for ci in range(nch):
            cs = slice(ci * C, (ci + 1) * C)
            if kind == "s":
                kb.dma(nat16[0:64, :, :], st_rw.h[ci].rearrange("h i j -> i h j"), (), (nat16,), nat16)
                heads_T(nat16, S0T, (nat16,), (S0T,))
            r_ = pT[:, 0:8, cs]; k_ = pT[:, 8:16, cs]; v_ = pT[:, 16:24, cs]
            f3 = lambda b: b[:, :, 0:C]
            kb.act(tw[:, 0:C], pT[:, 24, cs], AF.Tanh, (pT,), (tw,))
            kb.act(sgd[:, 0:C], pT[:, 25, cs], AF.Sigmoid, (pT,), (sgd,))
            for m in range(8):
                ps = ps_next()
                msl = slice(m * 128, (m + 1) * 128)
                kb.mm(ps[:, 0:C], w2z[:, msl], tw[:, 0:C], True, True, (w2z, tw), (ps,))
                kb.act(logw[:, m, 0:C], ps[:, 0:C], AF.Sigmoid, (ps, v_w0), (logw,), bias=v_w0[:, m:m + 1])
                ps = ps_next()
                kb.mm(ps[:, 0:C], a2z[:, msl], pT[:, 24, cs], True, True, (a2z, pT), (ps,))
                kb.act(a_[:, m, 0:C], ps[:, 0:C], AF.Sigmoid, (ps, v_a0), (a_,), bias=v_a0[:, m:m + 1])
            if lvl <= 2:
                return
            kb.tt(f3(kk_), k_, bc3(v_kk[:, 0:8], C), ALU.mult, (pT, v_kk), (kk_,))
            kb.tt(f3(tmp), f3(kk_), f3(kk_), ALU.mult, (kk_,), (tmp,))
            ps = ps_next()
            for m in range(8):
                kb.mm(ps[:, m * 64:m * 64 + C], bones[:, :], tmp[:, m, 0:C], True, True, (bones, tmp), (ps,))
            psv = ps[:, :].rearrange("p (k c) -> p k c", k=8)[:, :, 0:C]
            kb.act(f3(tmp), psv, AF.Sqrt, (ps,), (tmp,))
            ts1(f3(tmp), f3(tmp), 1e-12, ALU.max, (tmp,), (tmp,))
            P.op("dve", lambda e: e.reciprocal(f3(tmp), f3(tmp)), (tmp,), (tmp,))
            kb.tt(f3(kk_), f3(kk_), f3(tmp), ALU.mult, (kk_, tmp), (kk_,))
            ts1(f3(tmp), f3(a_), 1.0, ALU.subtract, (a_,), (tmp,))
            kb.tt(f3(tmp), f3(tmp), bc3(v_ka[:, 0:8], C), ALU.mult, (tmp, v_ka), (tmp,))
            kb.stt(k_, f3(tmp), 1.0, k_, ALU.add, ALU.mult, (tmp, pT), (pT,))
            kb.tt(f3(a_), f3(kk_), f3(a_), ALU.mult, (kk_, a_), (a_,))
            ts1(f3(logw), f3(logw), -float(np.exp(-0.5)), ALU.mult, (logw,), (logw,))
            src, dst = logw, La
            sh = 1
            while sh < C:
                kb.copy(dst[:, :, 0:sh], src[:, :, 0:sh], (src,), (dst,), eng="act")
                kb.tt(dst[:, :, sh:C], src[:, :, sh:C], src[:, :, 0:C - sh], ALU.add, (src,), (dst,))
                src, dst = dst, (Lb if dst is La else La)
                sh *= 2
            L = src
            assert L is tmp2
            kb.act(f3(Eg), f3(L), AF.Exp, (L,), (Eg,))
            kb.act(f3(Einv), f3(L), AF.Exp, (L,), (Einv,), scale=-1.0)
            kb.tt(f3(tmp), f3(L), f3(logw), ALU.subtract, (L, logw), (tmp,))
            kb.act(f3(tmp), f3(tmp), AF.Exp, (tmp,), (tmp,))
            kb.tt(AR[:, :, 0, 0:C], f3(kk_), f3(tmp), ALU.mult, (kk_, tmp), (AR,))
            kb.tt(AR[:, :, 1, 0:C], r_, f3(Eg), ALU.mult, (pT, Eg), (AR,))
            kb.tt(f3(BtT), f3(a_), f3(Einv), ALU.mult, (a_, Einv), (BtT,))
            kb.tt(f3(KtT), k_, f3(Einv), ALU.mult, (pT, Einv), (KtT,))
            ps = ps_next()
            for m in range(8):
                kb.mm(ps[:, m * 64:m * 64 + C], g2sb[:, m * 128:(m + 1) * 128], sgd[:, 0:C], True, True, (g2sb, sgd), (ps,))
            kb.copy(f3(gate_), ps[:, :].rearrange("p (k c) -> p k c", k=8)[:, :, 0:C], (ps,), (gate_,), eng="act")
            kb.tt(f3(tmp), r_, k_, ALU.mult, (pT,), (tmp,))
            kb.tt(f3(tmp), f3(tmp), bc3(v_rk[:, 0:8], C), ALU.mult, (tmp, v_rk), (tmp,))
            ps = ps_next()
            for m in range(8):
                kb.mm(ps[:, m * 64:m * 64 + C], bones[:, :], tmp[:, m, 0:C], True, True, (bones, tmp), (ps,))
            kb.tt(f3(bonus_), ps[:, :].rearrange("p (k c) -> p k c", k=8)[:, :, 0:C], v_, ALU.mult, (ps, pT), (bonus_,))
            for (srcv, sb, dstv, db) in ((AR[:, :, 0, 0:C], AR, ARH[0:64, :, 0, 0:C], ARH), (AR[:, :, 1, 0:C], AR, ARH[0:64, :, 1, 0:C], ARH),
                                         (f3(BtT), BtT, BtH[0:64, :, 0:C], BtH), (f3(KtT), KtT, KtH[0:64, :, 0:C], KtH)):
                ps = ps_next()
                kb.mm(ps[0:64, 0:8 * C], selhi[:, 0:64], srcv, True, True, (selhi, sb), (ps,))
                kb.copy(dstv, ps[0:64, 0:8 * C].rearrange("p (k c) -> p k c", k=8), (ps,), (db,), eng="act")
            ps = ps_next()
            kb.mm(ps[0:64, 0:8], selhi[:, 0:64], Eg[:, :, C - 1], True, True, (selhi, Eg), (ps,))
            kb.copy(gh[0:64, 0:8], ps[0:64, 0:8], (ps,), (gh,))
            if lvl <= 3:
                return

            for hh in range(2):
                prs = [4 * hh + q for q in range(4)]
                heads = [8 * hh + q for q in range(8)]
                for (srcap, sb, dstb) in ((lambda pr: pT[:, 16 + pr, cs], pT, Vtok), (lambda pr: BtT[:, pr, 0:C], BtT, Bttok),
                                          (lambda pr: KtT[:, pr, 0:C], KtT, Kttok)):
                    ps = ps_next()
                    for q, pr in enumerate(prs):
                        kb.tr(ps[0:C, q * 128:(q + 1) * 128], srcap(pr), ident[:, :], (sb, ident), (ps,))
                    kb.copy(dstb[0:C, :, :], ps[0:C, :].rearrange("p (h c) -> p h c", h=8), (ps,), (dstb,), eng="act")

                def v4(ps_, width, w2):
                    return ps_[0:C, 0:4 * width].rearrange("p (h c) -> p h c", h=4)[:, :, 0:w2]

                for (X_, XH_, Mm) in ((BtT, BtH, MBm), (KtT, KtH, MKm)):
                    for hb in range(2):
                        ps_ = ps_next()
                        for hi in range(4):
                            hl = hb * 4 + hi
                            h = heads[hl]
                            Xs, pr = fmop(X_, XH_, h)
                            As, _ = fmop(AR, ARH, h)
                            kb.mm(ps_[0:C, hi * 128:hi * 128 + 2 * C], Xs[0:64, pr, 0:C], As[0:64, pr, :, 0:C], True, True, (Xs, As), (ps_,))
                        kb.tt(Mm[0:C, hb * 4:hb * 4 + 4, 0:2 * C], v4(ps_, 128, 2 * C), bcmid(mkU, 4), ALU.mult, (ps_, mk), (Mm,))
                if lvl <= 4:
                    return
                for hb in range(2):
                    ps_ = ps_next()
                    for hi in range(4):
                        hl = hb * 4 + hi
                        h = heads[hl]
                        As, pr = fmop(AR, ARH, h)
                        Bs, _ = fmop(BtT, BtH, h)
                        kb.mm(ps_[0:C, hi * 64:hi * 64 + C], As[0:64, pr, 0, 0:C], Bs[0:64, pr, 0:C], True, True, (As, Bs), (ps_,))
                    kb.stt(NT[0:C, hb * 4:hb * 4 + 4, 0:C], v4(ps_, 64, C), -1.0, bcmid(mkL, 4), ALU.mult, ALU.mult, (ps_, mk), (NT,))
                ts1(Nm[0:C, :, 0:C], MBm[0:C, :, 0:C], -1.0, ALU.mult, (MBm,), (Nm,))
                kb.tt(Rm[0:C, :, 0:C], Nm[0:C, :, 0:C], bcmid(ident[0:C, 0:C], 8), ALU.add, (Nm, ident), (Rm,))
                Pc, PTc, Pn, PTn = Nm, NT, Pa, PTa
                for step in range(nd):
                    last = step == nd - 1
                    for hb in range(2):
                        ps_ = ps_next()
                        for hi in range(4):
                            hl = hb * 4 + hi
                            kb.mm(ps_[0:C, hi * 64:hi * 64 + C], Pc[0:C, hl, 0:C], PTc[0:C, hl, 0:C], True, True, (Pc, PTc), (ps_,))
                        kb.copy(PTn[0:C, hb * 4:hb * 4 + 4, 0:C], v4(ps_, 64, C), (ps_,), (PTn,), eng="act")
                    if not last:
                        for hb in range(2):
                            ps_ = ps_next()
                            for hi in range(4):
                                hl = hb * 4 + hi
                                kb.mm(ps_[0:C, hi * 64:hi * 64 + C], PTc[0:C, hl, 0:C], Pc[0:C, hl, 0:C], True, True, (Pc, PTc), (ps_,))
                            kb.copy(Pn[0:C, hb * 4:hb * 4 + 4, 0:C], v4(ps_, 64, C), (ps_,), (Pn,), eng="dve")
                    for hb in range(2):
                        ps_ = ps_next()
                        for hi in range(4):
                            hl = hb * 4 + hi
                            kb.mm(ps_[0:C, hi * 64:hi * 64 + C], PTn[0:C, hl, 0:C], Rm[0:C, hl, 0:C], True, True, (PTn, Rm), (ps_,))
                        kb.tt(Rm[0:C, hb * 4:hb * 4 + 4, 0:C], Rm[0:C, hb * 4:hb * 4 + 4, 0:C], v4(ps_, 64, C), ALU.add, (Rm, ps_), (Rm,))
                    Pc, PTc, Pn, PTn = Pn, PTn, Pc, PTc
                if lvl <= 5:
                    return
                for hb in range(2):
                    ps_ = ps_next()
                    for hi in range(4):
                        hl = hb * 4 + hi
                        h = heads[hl]
                        As, pr = fmop(AR, ARH, h)
                        o = ps_[0:C, hi * 64:hi * 64 + 64]
                        kb.mm(o, As[0:64, pr, 0, 0:C], S0T[0:64, h, :], True, False, (As, S0T), (ps_,))
                        kb.mm(o, MKm[0:C, hl, 0:C], Vtok[0:C, hl, :], False, True, (MKm, Vtok), (ps_,))
                    kb.copy(XT[0:C, hb * 4:hb * 4 + 4, :], v4(ps_, 64, 64), (ps_,), (XT,), eng="act")
                for hb in range(2):
                    ps_ = ps_next()
                    for hi in range(4):
                        hl = hb * 4 + hi
                        kb.mm(ps_[0:C, hi * 64:hi * 64 + 64], Rm[0:C, hl, 0:C], XT[0:C, hl, :], True, True, (Rm, XT), (ps_,))
                    ts1(UT[0:C, hb * 4:hb * 4 + 4, :], v4(ps_, 64, 64), -1.0, ALU.mult, (ps_,), (UT,))
                for hb in range(2):
                    ps_ = ps_next()
                    for hi in range(4):
                        hl = hb * 4 + hi
                        h = heads[hl]
                        As, pr = fmop(AR, ARH, h)
                        o = ps_[0:C, hi * 64:hi * 64 + 64]
                        kb.mm(o, As[0:64, pr, 1, 0:C], S0T[0:64, h, :], True, False, (As, S0T), (ps_,))
                        kb.mm(o, MBm[0:C, hl, C:2 * C], UT[0:C, hl, :], False, False, (MBm, UT), (ps_,))
                        kb.mm(o, MKm[0:C, hl, C:2 * C], Vtok[0:C, hl, :], False, True, (MKm, Vtok), (ps_,))
                    kb.copy(Yt[0:C, hb * 4:hb * 4 + 4, :], v4(ps_, 64, 64), (ps_,), (Yt,), eng="act")
                for hb in range(2):
                    ps_ = ps_next()
                    for hi in range(4):
                        hl = hb * 4 + hi
                        o = ps_[0:64, hi * 64:hi * 64 + 64]
                        kb.mm(o, Bttok[0:C, hl, :], UT[0:C, hl, :], True, False, (Bttok, UT), (ps_,))
                        kb.mm(o, Kttok[0:C, hl, :], Vtok[0:C, hl, :], False, True, (Kttok, Vtok), (ps_,))
                    for hi in range(4):
                        hl = hb * 4 + hi
                        h = heads[hl]; pr = h // 2
                        gC = Eg[0:64, pr, C - 1:C] if h % 2 == 0 else gh[0:64, pr:pr + 1]
                        gb = Eg if h % 2 == 0 else gh
                        ts1(NT[0:64, hl, :], S0T[0:64, h, :], gC, ALU.mult, (S0T, gb), (NT,))
                        kb.stt(S0T[0:64, h, :], ps_[0:64, hi * 64:hi * 64 + 64], gC, NT[0:64, hl, :], ALU.mult, ALU.add, (ps_, gb, NT), (S0T,))
                if lvl <= 6:
                    return
                P.op("dve", lambda e: e.tensor_reduce(st8[0:C, :, 0], Yt[0:C, :, :], AX.X, ALU.add), (Yt,), (st8,))
                ts1(st8[0:C, :, 0], st8[0:C, :, 0], 1.0 / 64, ALU.mult, (st8,), (st8,))
                kb.tt(Yc[0:C, :, :], Yt[0:C, :, :], bc3(st8[0:C, :, 0], 64), ALU.subtract, (Yt, st8), (Yc,))
                kb.tt(Yt[0:C, :, :], Yc[0:C, :, :], Yc[0:C, :, :], ALU.mult, (Yc,), (Yt,))
                P.op("dve", lambda e: e.tensor_reduce(st8[0:C, :, 1], Yt[0:C, :, :], AX.X, ALU.add), (Yt,), (st8,))
                kb.ts(st8[0:C, :, 1], st8[0:C, :, 1], 1.0 / 64, 64e-5, ALU.mult, ALU.add, (st8,), (st8,))
                kb.act(st8[0:C, :, 1], st8[0:C, :, 1], AF.Sqrt, (st8,), (st8,))
                P.op("dve", lambda e: e.reciprocal(st8[0:C, :, 2], st8[0:C, :, 1]), (st8,), (st8,))
                kb.tt(Yc[0:C, :, :], Yc[0:C, :, :], bc3(st8[0:C, :, 2], 64), ALU.mult, (Yc, st8), (Yc,))
                ps_ = ps_next()
                for q, pr in enumerate(prs):
                    kb.tr(ps_[:, q * 64:q * 64 + C], Yc[0:C, q * 2:q * 2 + 2, :], ident[0:C, 0:C], (Yc, ident), (ps_,))
                for q, pr in enumerate(prs):
                    kb.act(tmp2[:, pr, 0:C], ps_[:, q * 64:q * 64 + C], AF.Identity, (ps_, v_lb, v_lw), (tmp2,),
                           bias=v_lb[:, pr:pr + 1], scale=v_lw[:, pr:pr + 1])
            if lvl <= 7:
                return
            kb.tt(f3(tmp2), f3(tmp2), f3(bonus_), ALU.add, (tmp2, bonus_), (tmp2,))
            kb.tt(yrwT[:, :, cs], f3(tmp2), f3(gate_), ALU.mult, (tmp2, gate_), (yrwT,))
            if kind == "s" or (t0 + T == SEQ_ and ci == nch - 1):
                heads_T(S0T, nat16, (S0T,), (nat16,))
                dst = rws.h[ci] if kind == "s" else rwp.h
                kb.dma(dst.rearrange("h i j -> i h j"), nat16[0:64, :, :], (nat16,), (), nat16)

    NEGM = -30000.0
    SCALE = 128 ** -0.5
    kselT = P.sbuf("kselT", [128, 2, SEQ], BF16)
    vsel = P.sbuf("vsel", [128, SEQ // 128, 2, 128], BF16)
    kwinT = P.sbuf("kwinT", [128, 2, 6, 128], BF16)
    vwin = P.sbuf("vwin", [128, 6, 2, 128], BF16)
    kcT = P.sbuf("kcT", [128, 2, 64], BF16)
    vc_all = P.sbuf("vc_all", [64, 2, 128], BF16)
    wgate = P.sbuf("wgate", [128, KC, 24], BF16)
    kb.dma(wgate[:, :, :], w_in.h.rearrange("(k p) n -> p k n", p=128)[:, :, 5888:5912], (), (wgate,), wgate, eng="pool")
    cw2 = P.sbuf("cw2", [128, 2, 2, 128], BF16)
    for kv_ in range(2):
        kb.dma(cw2[:, kv_, :, :], cmp_w2.h[kv_].rearrange("(k p) d -> p k d", p=128), (), (cw2,), cw2, eng="pool")
    peT = P.sbuf("peT", [128, 64], F32)
    mc_sb = P.sbuf("mc_sb", [128, 67], F32)
    kb.dma(mc_sb[:], mc_d.h[:, :], (), (mc_sb,), mc_sb)
    bc_sb = P.sbuf("bc_sb", [128, 16, 32], F32)
    kb.dma(bc_sb[:], bc_d.h[:, :, :], (), (bc_sb,), bc_sb)
    bs_sb = P.sbuf("bs_sb", [4, 136], F32)
    kb.dma(bs_sb[:], bs_d.h[:, :], (), (bs_sb,), bs_sb)
    iotap = P.sbuf("iotap_sb", [128, 1], F32)
    kb.dma(iotap[:], iotap_d.h[:, :], (), (iotap,), iotap)
    relb = P.sbuf("relb33", [33, 8], F32)
    kb.memset(relb[:, :], 1.0, (relb,))
    kb.dma(relb[0:32, :], rel_bias.h[:, :], (relb,), (relb,), relb)
    Big = P.dram("big_sel", [8, 128, 2048], F32, "Internal")
    BigW = P.dram("big_win", [8, 128, 640], F32, "Internal")
    BigC = P.dram("big_cmp", [8, 128 * 67], F32, "Internal")

    def nsa_setup():
        A = arena
        A.reset()
        pes = A.alloc("pes", 128)
        kb.dma(pes[0:64, 0:128], cmp_pe.h.rearrange("r k d -> (r k) d"), (), (pes,), pes)
        ps = ps_next()
        kb.tr(ps[:, 0:64], pes[0:64, 0:128], ident[0:64, 0:64], (pes, ident), (ps,))
        kb.copy(peT[:, :], ps[:, 0:64], (ps,), (peT,))
        for (oh_d, ncol, bigd, wrow) in ((ohs_d, 2176, Big, 2048), (ohw_d, 768, BigW, 640)):
            oh = A.alloc("oh", ncol)
            tr_ = A.alloc("trev", ncol)
            kb.dma(oh[0:33, 0:ncol], oh_d.h[:, :], (), (oh,), oh)
            for c0 in range(0, ncol, 512):
                w_ = min(512, ncol - c0)
                ps = ps_next()
                kb.mm(ps[0:8, 0:w_], relb[0:33, 0:8], oh[0:33, c0:c0 + w_], True, True, (relb, oh), (ps,))
                kb.copy(tr_[0:8, c0:c0 + w_], ps[0:8, 0:w_], (ps,), (tr_,))
            for q in range(128):
                kb.dma(bigd.h[:, q, :], tr_[0:8, 127 - q:127 - q + wrow], (tr_,), (bigd,), tr_)
        ncol = 128 * 67
        for c0 in range(0, ncol, 512):
            w_ = min(512, ncol - c0)
            oh = A.alloc("ohc", 512)
            tc_ = A.alloc("trc", 512)
            kb.dma(oh[0:33, 0:w_], ohc_d.h[:, c0:c0 + w_], (), (oh,), oh)
            ps = ps_next()
            kb.mm(ps[0:8, 0:w_], relb[0:33, 0:8], oh[0:33, 0:w_], True, True, (relb, oh), (ps,))
            kb.copy(tc_[0:8, 0:w_], ps[0:8, 0:w_], (ps,), (tc_,))
            kb.dma(BigC.h[:, c0:c0 + w_], tc_[0:8, 0:w_], (tc_,), (BigC,), tc_)
            if A.off + 1024 > A.words:
                A.reset()

    def psbf(ps):
        return ps.h.bitcast(BF16)

    def nsa_cache_update(t0, T, ti, kvT):
        A = arena
        nsub = T // 128
        for g in range(2):
            kb.copy(kselT[:, g, t0:t0 + T], kvT[:, 4 + g, :T], (kvT,), (kselT,), eng="act")
            for s_ in range(nsub):
                kt = t0 // 128 + s_
                kb.copy(kwinT[:, g, kt % 6, :], kvT[:, 8 + g, s_ * 128:(s_ + 1) * 128], (kvT,), (kwinT,), eng="act")
        for s_ in range(nsub):
            kt = t0 // 128 + s_
            ps = ps_next()
            for j, ch in enumerate((6, 7, 10, 11)):
                kb.tr(ps[:, j * 128:(j + 1) * 128], kvT[:, ch, s_ * 128:(s_ + 1) * 128], ident[:, :], (kvT, ident), (ps,))
            kb.copy(vsel[:, kt, :, :], ps[:, 0:256].rearrange("p (g d) -> p g d", g=2), (ps,), (vsel,))
            kb.copy(vwin[:, kt % 6, :, :], ps[:, 256:512].rearrange("p (g d) -> p g d", g=2), (ps,), (vwin,))
        nb = T // 32
        Xb = A.alloc("Xb", 4 * T // 2, view=lambda a: a.bitcast(BF16).rearrange("p (c n r) -> p c n r", c=4, r=32))
        hid = A.alloc("hid", 64, view=lambda a: a.rearrange("p (c n) -> p c n", c=2))
        hx = A.alloc("hx", 64, view=lambda a: a.rearrange("p (c n) -> p c n", c=2))
        hb = A.alloc("hb", 32, view=lambda a: a.bitcast(BF16).rearrange("p (c n) -> p c n", c=2))
        vct = A.alloc("vct", 128, view=lambda a: a.bitcast(BF16))
        pe3 = peT[:, :].rearrange("p (r k) -> p k r", k=2)
        for c in range(4):
            kv_ = c // 2
            kb.tt(Xb[:, c, :, :], kvT[:, c, :T].rearrange("p (n r) -> p n r", r=32), bcmid(pe3[:, kv_, :], nb), ALU.add,
                  (kvT, peT), (Xb,))
        for kv_ in range(2):
            slots = [ws.take(), ws.take()]
            wv = [s_[:, 0:16 * 256].rearrange("p (r c) -> p r c", r=16) for s_ in slots]
            for g in range(2):
                c = kv_ * 2 + g
                for cc in range(2):
                    ps = ps_next()
                    for r in range(32):
                        kb.mm(ps[:, 0:nb], wv[r // 16][:, r % 16, cc * 128:(cc + 1) * 128], Xb[:, c, :, r], r == 0, r == 31,
                              (slots[r // 16], Xb), (ps,))
                    kb.copy(hid[:, cc, 0:nb], ps[:, 0:nb], (ps,), (hid,))
                kb.tt(hx[:, :, 0:nb], hid[:, :, 0:nb], hid[:, :, 0:nb], ALU.mult, (hid,), (hx,))
                kb.ts(hx[:, :, 0:nb], hx[:, :, 0:nb], 0.044715, 1.0, ALU.mult, ALU.add, (hx,), (hx,))
                kb.tt(hx[:, :, 0:nb], hx[:, :, 0:nb], hid[:, :, 0:nb], ALU.mult, (hx, hid), (hx,))
                kb.act(hx[:, :, 0:nb], hx[:, :, 0:nb], AF.Sigmoid, (hx,), (hx,), scale=1.5957691216057308)
                kb.tt(hb[:, :, 0:nb], hx[:, :, 0:nb], hid[:, :, 0:nb], ALU.mult, (hx, hid), (hb,))
                ps = ps_next()
                if kv_ == 0:
                    for cc in range(2):
                        kb.mm(ps[:, 0:nb], cw2[:, 0, cc, :], hb[:, cc, 0:nb], cc == 0, cc == 1, (cw2, hb), (ps,))
                    kb.copy(kcT[:, g, ti * nb:(ti + 1) * nb], ps[:, 0:nb], (ps,), (kcT,))
                else:
                    for cc in range(2):
                        kb.mm(ps[0:nb, 0:128], hb[:, cc, 0:nb], cw2[:, 1, cc, :], cc == 0, cc == 1, (cw2, hb), (ps,))
                    kb.copy(vct[0:nb, g * 128:(g + 1) * 128], ps[0:nb, 0:128], (ps,), (vct,))
        kb.dma(vc_all[ti * nb:(ti + 1) * nb, :, :], vct[0:nb, 0:256].rearrange("p (g d) -> p g d", g=2), (vct,), (vc_all,), vct)

    def nsa_prompt(t0, T, ti):
        A = arena
        S = A.alloc("S", 2048)
        Bt = [A.alloc("bias0", 2048), A.alloc("bias1", 2048)]
        E = A.alloc("E", 1024, view=lambda a: a.bitcast(BF16))
        PTb = A.alloc("PTb", 1024, view=lambda a: a.bitcast(BF16).rearrange("p (k q) -> p k q", k=16))
        acc = A.alloc("acc", 1024)
        ob = A.alloc("ob", 128)
        pc = A.alloc("pc", 64); pcb = A.alloc("pcb", 32, view=lambda a: a.bitcast(BF16))
        imp = A.alloc("imp", 64); imp2 = A.alloc("imp2", 32); impw = A.alloc("impw", 32); selm = A.alloc("selm", 32)
        mx8 = A.alloc("mx8", 16)
        st = A.alloc("st", 8)
        gsig = A.alloc("gsig", 24)
        bi = [0]

        def softmax_pv(h, g, W, ktiles, kT_ap_fn, v_ap_fn, bias_src_ap, gate_col, first, mask_blocks):
            bt = Bt[bi[0] % 2]
            bi[0] += 1
            kb.dma(bt[:, 0:W], bias_src_ap, (Big, BigW), (bt,), bt)
            qap = qT[:, h, qs]
            c0 = 0
            while c0 < W:
                w_ = min(512, W - c0)
                ps = ps_next()
                for (kc0, kw, rhs_ap, rb) in kT_ap_fn(c0, w_):
                    kb.mm(ps[:, kc0 - c0:kc0 - c0 + kw], qap, rhs_ap, True, True, (qT, rb), (ps,))
                kb.stt(S[:, c0:c0 + w_], ps[:, 0:w_], SCALE, bt[:, c0:c0 + w_], ALU.mult, ALU.add, (ps, bt), (S,))
                c0 += w_
            kb.reduce(st[:, 0:1], S[:, 0:W], ALU.max, (S,), (st,))
            ts1(st[:, 1:2], st[:, 0:1], -1.0, ALU.mult, (st,), (st,))
            if mask_blocks is None:
                kb.act(E[:, 0:W], S[:, 0:W], AF.Exp, (S, st), (E,), bias=st[:, 1:2])
            else:
                kb.act(S[:, 0:W], S[:, 0:W], AF.Exp, (S, st), (S,), bias=st[:, 1:2])
                nb_ = W // 64
                kb.tt(E[:, 0:W].rearrange("p (b c) -> p b c", c=64), S[:, 0:W].rearrange("p (b c) -> p b c", c=64),
                      bc3(mask_blocks[:, 0:nb_], 64), ALU.mult, (S, mask_blocks), (E,))
            kb.reduce(st[:, 2:3], E[:, 0:W], ALU.add, (E,), (st,))
            kb.recip(st[:, 3:4], st[:, 2:3], (st,), (st,))
            kb.tt(st[:, 3:4], st[:, 3:4], gate_col, ALU.mult, (st, gsig), (st,))
            nk = len(ktiles)
            for k0 in range(0, nk, 8):
                ps = ps_next()
                pv = psbf(ps)
                kk_ = min(8, nk - k0)
                for j in range(kk_):
                    (cc0, cw, _) = ktiles[k0 + j]
                    kb.tr(pv[0:cw, j * 128:(j + 1) * 128], E[:, cc0:cc0 + cw], identb[:, :], (E, identb), (ps,))
                kb.copy(PTb[:, k0:k0 + kk_, :], pv[:, 0:kk_ * 128].rearrange("p (k q) -> p k q", k=kk_), (ps,), (PTb,),
                        eng=("act" if (k0 // 8) % 2 else "dve"))
            ps = ps_next()
            for j, (cc0, cw, kt) in enumerate(ktiles):
                vap, vb = v_ap_fn(kt, cw)
                kb.mm(ps[:, 0:128], PTb[0:cw, j, :], vap, j == 0, j == nk - 1, (PTb, vb), (ps,))
            hsl = slice(h * 128, (h + 1) * 128)
            if first:
                ts1(acc[:, hsl], ps[:, 0:128], st[:, 3:4], ALU.mult, (ps, st), (acc,))
            else:
                kb.stt(acc[:, hsl], ps[:, 0:128], st[:, 3:4], acc[:, hsl], ALU.mult, ALU.add, (ps, st, acc), (acc,))

        for qb in range(T // 128):
            i = t0 // 128 + qb
            q0 = i * 128
            qs = slice(qb * 128, (qb + 1) * 128)
            ps = ps_next()
            for k in range(KC):
                kb.mm(ps[:, 0:24], hT[:, k, qs], wgate[:, k, :], k == 0, k == KC - 1, (hT, wgate), (ps,))
            kb.act(gsig[:, 0:24], ps[:, 0:24], AF.Sigmoid, (ps,), (gsig,))
            Wc = 4 * i + 4
            for g in range(2):
                for hh in range(4):
                    h = g * 4 + hh
                    bt = Bt[bi[0] % 2]
                    bi[0] += 1
                    kb.dma(bt[:, 0:Wc], BigC.h[h].rearrange("(q j) -> q j", j=67)[:, 63 - 4 * i:67], (BigC,), (bt,), bt)
                    ps = ps_next()
                    kb.mm(ps[:, 0:Wc], qT[:, h, qs], kcT[:, g, 0:Wc], True, True, (qT, kcT), (ps,))
                    kb.stt(S[:, 0:Wc], ps[:, 0:Wc], SCALE, bt[:, 0:Wc], ALU.mult, ALU.add, (ps, bt), (S,))
                    kb.reduce(st[:, 0:1], S[:, 0:Wc], ALU.max, (S,), (st,))
                    ts1(st[:, 1:2], st[:, 0:1], -1.0, ALU.mult, (st,), (st,))
                    kb.act(pc[:, 0:Wc], S[:, 0:Wc], AF.Exp, (S, st), (pc,), bias=st[:, 1:2])
                    kb.tt(pc[:, 0:Wc], pc[:, 0:Wc], mc_sb[:, 63 - 4 * i:67], ALU.mult, (pc, mc_sb), (pc,))
                    kb.reduce(st[:, 2:3], pc[:, 0:Wc], ALU.add, (pc,), (st,))
                    ts1(st[:, 2:3], st[:, 2:3], 1e-30, ALU.max, (st,), (st,))
                    kb.recip(st[:, 3:4], st[:, 2:3], (st,), (st,))
                    ts1(pc[:, 0:Wc], pc[:, 0:Wc], st[:, 3:4], ALU.mult, (pc, st), (pc,))
                    if hh == 0:
                        kb.copy(imp[:, 0:Wc], pc[:, 0:Wc], (pc,), (imp,))
                    else:
                        kb.tt(imp[:, 0:Wc], imp[:, 0:Wc], pc[:, 0:Wc], ALU.add, (imp, pc), (imp,))
                    kb.copy(pcb[:, 0:Wc], pc[:, 0:Wc], (pc,), (pcb,))
                    ps = ps_next()
                    pv = psbf(ps)
                    kb.tr(pv[0:Wc, 0:128], pcb[:, 0:Wc], identb[:, :], (pcb, identb), (ps,))
                    kb.copy(PTb[0:Wc, 0, :], pv[0:Wc, 0:128], (ps,), (PTb,))
                    ps = ps_next()
                    kb.mm(ps[:, 0:128], PTb[0:Wc, 0, :], vc_all[0:Wc, g, :], True, True, (PTb, vc_all), (ps,))
                    ts1(acc[:, h * 128:(h + 1) * 128], ps[:, 0:128], gsig[:, h:h + 1], ALU.mult, (ps, gsig), (acc,))
                nsb = 2 * i + 2
                mask_blocks = None
                if nsb > 16:
                    kb.reduce(imp2[:, 0:nsb], imp[:, 0:Wc].rearrange("p (b two) -> p b two", two=2), ALU.add, (imp,), (imp2,))
                    if nsb < 32:
                        kb.memset(imp2[:, nsb:32], -1e30, (imp2,))
                    kb.tt(imp2[:, 0:nsb], imp2[:, 0:nsb], bc_sb[:, i, 0:nsb], ALU.add, (imp2, bc_sb), (imp2,))
                    P.op("dve", lambda e: e.max(mx8[:, 0:8], imp2[:, 0:32]), (imp2,), (mx8,))
                    P.op("dve", lambda e: e.match_replace(impw[:, 0:32], mx8[:, 0:8], imp2[:, 0:32], -3e38), (mx8, imp2), (impw,))
                    P.op("dve", lambda e: e.max(mx8[:, 8:16], impw[:, 0:32]), (impw,), (mx8,))
                    kb.reduce(st[:, 4:5], mx8[:, 8:16], ALU.min, (mx8,), (st,))
                    ts1(selm[:, 0:32], imp2[:, 0:32], st[:, 4:5], ALU.is_ge, (imp2, st), (selm,))
                    mask_blocks = selm
                for hh in range(4):
                    h = g * 4 + hh
                    Ws = q0 + 128
                    ktl = [(kt * 128, 128, kt) for kt in range(i + 1)]
                    softmax_pv(h, g, Ws, ktl,
                               lambda c0, w_: [(c0, w_, kselT[:, g, c0:c0 + w_], kselT)],
                               lambda kt, cw: (vsel[0:cw, kt, g, :], vsel),
                               Big.h[h, :, 1920 - q0:1920 - q0 + Ws], gsig[:, 8 + h:9 + h], False, mask_blocks)
                    kt0 = max(0, i - 4)
                    ktw = list(range(kt0, i + 1))
                    Ww = 128 * len(ktw)
                    ktlw = [(j * 128, 128, kt) for j, kt in enumerate(ktw)]
                    softmax_pv(h, g, Ww, ktlw,
                               lambda c0, w_: [(c0 + jj * 128, 128, kwinT[:, g, ktw[(c0 // 128) + jj] % 6, :], kwinT)
                                               for jj in range(w_ // 128)],
                               lambda kt, cw: (vwin[0:cw, kt % 6, g, :], vwin),
                               BigW.h[h, :, 640 - Ww:640], gsig[:, 16 + h:17 + h], False, None)
            for h0 in range(0, 8, 4):
                ps = ps_next()
                for j in range(4):
                    kb.tr(ps[:, j * 128:(j + 1) * 128], acc[:, (h0 + j) * 128:(h0 + j + 1) * 128], ident[:, :], (acc, ident), (ps,))
                kb.copy(ynsaT[:, h0:h0 + 4, qs], ps[:, :].rearrange("p (h q) -> p h q", h=4), (ps,), (ynsaT,), eng=("act" if h0 else "dve"))

    PAST = 8192
    newK = P.sbuf("newK", [128, 2, 2, 16], BF16)
    newVT = P.sbuf("newVT", [128, 2, 2, 16], BF16)
    smp = {}
    TSs = P.dram("tab_s_sel", [8, 8704], F32, "Internal")
    TWs = P.dram("tab_s_win", [8, 1024], F32, "Internal")
    TCs = P.dram("tab_s_cmp", [8, 1024], F32, "Internal")
    ckv_rows = ckv.h.rearrange("n p c -> (n p) c")

    def nsa_setup_sample():
        A = arena
        A.reset()
        for (oh_d, ncol, dst) in ((ohss_d, 8704, TSs), (ohsw_d, 1024, TWs), (ohsc_d, 1024, TCs)):
            for c0 in range(0, ncol, 512):
                oh = A.alloc("ohc", 512)
                tc_ = A.alloc("trc", 512)
                kb.dma(oh[0:33, 0:512], oh_d.h[:, c0:c0 + 512], (), (oh,), oh)
                ps = ps_next()
                kb.mm(ps[0:8, 0:512], relb[0:33, 0:8], oh[0:33, 0:512], True, True, (relb, oh), (ps,))
                kb.copy(tc_[0:8, 0:512], ps[0:8, 0:512], (ps,), (tc_,))
                kb.dma(dst.h[:, c0:c0 + 512], tc_[0:8, 0:512], (tc_,), (dst,), tc_)
                if A.off + 1024 > A.words:
                    A.reset()

    def nsa_sample_newrows(kvT):
        for w_, (kc, vc_) in enumerate(((4, 6), (8, 10))):
            for g in range(2):
                kb.copy(newK[:, w_, g, :], kvT[:, kc + g, 0:16], (kvT,), (newK,))
                kb.copy(newVT[:, w_, g, :], kvT[:, vc_ + g, 0:16], (kvT,), (newVT,))

    def page_index_bufs(A):
        return (A.alloc("pti", 64, view=lambda a: a.bitcast(I32)), A.alloc("ptf", 64),
                A.alloc("idx", 64, view=lambda a: a.bitcast(I32)))

    def page_index(A, s_, bufs):
        pti, ptf, idx = bufs
        kb.dma(pti[:, :], ptab.h[s_:s_ + 1, :].to_broadcast([128, 64]), (), (pti,), pti)
        kb.copy(ptf[:, :], pti[:, :], (pti,), (ptf,))
        kb.ts(ptf[:, :], ptf[:, :], 128.0, iotap[:, 0:1], ALU.mult, ALU.add, (ptf, iotap), (ptf,))
        kb.copy(idx[:, :], ptf[:, :], (ptf,), (idx,))
        return idx

    def gather(dst_buf, dst_ap, src_ap, idx, j, eoff=0):
        def fn(e):
            return e.indirect_dma_start(out=dst_ap, out_offset=None, in_=src_ap,
                                        in_offset=bass.IndirectOffsetOnAxis(ap=idx[:, j:j + 1], axis=0), element_offset=eoff)
        return P.op("pool", fn, (idx,), (dst_buf,), dma=True, sem_buf=dst_buf)

    def nsa_sample_compress():
        A = arena
        kcTs = A.alloc("kcTs", 1024, view=lambda a: a.bitcast(BF16).rearrange("p (s g n) -> p s g n", s=4, g=2))
        vcs = A.alloc("vcs", 1024, view=lambda a: a.bitcast(BF16).rearrange("p (s c g d) -> p s c g d", s=4, c=2, g=2))
        smp["kcTs"], smp["vcs"], smp["mark"] = kcTs, vcs, A.off
        W1b = A.alloc("W1b", 8192, view=lambda a: a.bitcast(BF16).rearrange("p (k r c) -> p k r c", k=2, r=32))
        for kv_ in range(2):
            for half in range(2):
                kb.dma(W1b[:, kv_, half * 16:(half + 1) * 16, :],
                       cmp_w1.h[kv_, half * 2048:(half + 1) * 2048, :].rearrange("(r p) c -> p r c", p=128), (), (W1b,), W1b, eng="pool")
        pgs = [A.alloc("pgA", 1024), A.alloc("pgB", 1024)]
        Xb = A.alloc("Xb8", 2048, view=lambda a: a.bitcast(BF16).rearrange("p (c n r) -> p c n r", c=4, r=32))
        hid = A.alloc("hid", 64, view=lambda a: a.rearrange("p (c n) -> p c n", c=2))
        hx = A.alloc("hx", 64, view=lambda a: a.rearrange("p (c n) -> p c n", c=2))
        hb = A.alloc("hb", 32, view=lambda a: a.bitcast(BF16).rearrange("p (c n) -> p c n", c=2))
        vct = A.alloc("vct", 128, view=lambda a: a.bitcast(BF16))
        pe3 = peT[:, :].rearrange("p (r k) -> p k r", k=2)
        nb = 32
        pib = page_index_bufs(A)
        for s_ in range(4):
            idx = page_index(A, s_, pib)
            for grp in range(8):
                for pj in range(8):
                    j = grp * 8 + pj
                    pg = pgs[j % 2]
                    gather(pg, pg[:, 0:1024], ckv_rows[:, :], idx, j)
                    ps = ps_next()
                    for c in range(4):
                        kb.tr(ps[:, c * 128:(c + 1) * 128], pg[:, c * 128:(c + 1) * 128], ident[:, :], (pg, ident), (ps,))
                    for c in range(4):
                        kb.tt(Xb[:, c, pj * 4:(pj + 1) * 4, :], ps[:, c * 128:(c + 1) * 128].rearrange("p (n r) -> p n r", r=32),
                              bcmid(pe3[:, c // 2, :], 4), ALU.add, (ps, peT), (Xb,), eng=("dve"))
                for kv_ in range(2):
                    for g in range(2):
                        c = kv_ * 2 + g
                        for cc in range(2):
                            ps = ps_next()
                            for r in range(32):
                                kb.mm(ps[:, 0:nb], W1b[:, kv_, r, cc * 128:(cc + 1) * 128], Xb[:, c, :, r], r == 0, r == 31, (W1b, Xb), (ps,))
                            kb.copy(hid[:, cc, 0:nb], ps[:, 0:nb], (ps,), (hid,), eng="act")
                        kb.tt(hx[:, :, 0:nb], hid[:, :, 0:nb], hid[:, :, 0:nb], ALU.mult, (hid,), (hx,))
                        kb.ts(hx[:, :, 0:nb], hx[:, :, 0:nb], 0.044715, 1.0, ALU.mult, ALU.add, (hx,), (hx,))
                        kb.tt(hx[:, :, 0:nb], hx[:, :, 0:nb], hid[:, :, 0:nb], ALU.mult, (hx, hid), (hx,))
                        kb.act(hx[:, :, 0:nb], hx[:, :, 0:nb], AF.Sigmoid, (hx,), (hx,), scale=1.5957691216057308)
                        kb.tt(hb[:, :, 0:nb], hx[:, :, 0:nb], hid[:, :, 0:nb], ALU.mult, (hx, hid), (hb,))
                        ps = ps_next()
                        if kv_ == 0:
                            for cc in range(2):
                                kb.mm(ps[:, 0:nb], cw2[:, 0, cc, :], hb[:, cc, 0:nb], cc == 0, cc == 1, (cw2, hb), (ps,))
                            kb.copy(kcTs[:, s_, g, grp * nb:(grp + 1) * nb], ps[:, 0:nb], (ps,), (kcTs,))
                        else:
                            for cc in range(2):
                                kb.mm(ps[0:nb, 0:128], hb[:, cc, 0:nb], cw2[:, 1, cc, :], cc == 0, cc == 1, (cw2, hb), (ps,))
                            kb.copy(vct[0:nb, g * 128:(g + 1) * 128], ps[0:nb, 0:128], (ps,), (vct,))
                po = (grp % 4) * nb
                kb.dma(vcs[po:po + nb, s_, grp // 4, :, :], vct[0:nb, 0:256].rearrange("p (g d) -> p g d", g=2), (vct,), (vcs,), vct)

    def nsa_sample_attend():
        A = arena
        kcTs, vcs = smp["kcTs"], smp["vcs"]
        KTs = A.alloc("KTs", 4104, view=lambda a: a.bitcast(BF16))
        Vs = A.alloc("Vs", 65 * 64, view=lambda a: a.bitcast(BF16).rearrange("p (k d) -> p k d", d=128))
        KW = A.alloc("KW", 264, view=lambda a: a.bitcast(BF16))
        VW = A.alloc("VW", 5 * 64, view=lambda a: a.bitcast(BF16).rearrange("p (k d) -> p k d", d=128))
        pg1 = A.alloc("pgA", 1024)
        pg2 = A.alloc("pgB", 1024)
        pgs = [pg1, pg2]
        wst = pg1
        S = A.alloc("S4", 1024)
        bt = A.alloc("bias4", 520)
        E = A.alloc("E4", 512, view=lambda a: a.bitcast(BF16))
        PTb = A.alloc("PTb4", 32, view=lambda a: a.bitcast(BF16).rearrange("p (k q) -> p k q", q=4))
        acc = A.alloc("acc4", 1024)
        pc = A.alloc("pc4", 256); pcb = A.alloc("pcb4", 128, view=lambda a: a.bitcast(BF16))
        imp = A.alloc("imp4", 256); imp2 = A.alloc("imp24", 136); impw = A.alloc("impw4", 136); selm = A.alloc("selm4", 136)
        mx8 = A.alloc("mx84", 16)
        st = A.alloc("st4", 16)
        gsig = A.alloc("gsig4", 24)
        ps_mod[0] = 7
        pacc = psb[7]

        def attend(h, q_ap, groups, gate_col, first, hsl):
            multi = len(groups) > 1

            def load_bias(gi):
                (W, kT_ap, kT_buf, bias_fn, mask_ap, vt) = groups[gi]
                for t in range(4):
                    kb.dma(bt[t:t + 1, 0:W], bias_fn(t), (TSs, TWs, TCs), (bt,), bt)

            def scores(gi):
                (W, kT_ap, kT_buf, bias_fn, mask_ap, vt) = groups[gi]
                if bias_fn is not None:
                    load_bias(gi)
                c0 = 0
                while c0 < W:
                    w_ = min(512, W - c0)
                    ps = ps_next()
                    kb.mm(ps[0:4, 0:w_], q_ap, kT_ap[:, c0:c0 + w_], True, True, (qT, kT_buf), (ps,))
                    if bias_fn is not None:
                        kb.stt(S[0:4, c0:c0 + w_], ps[0:4, 0:w_], SCALE, bt[0:4, c0:c0 + w_], ALU.mult, ALU.add, (ps, bt), (S,))
                    else:
                        kb.ts(S[0:4, c0:c0 + w_], ps[0:4, 0:w_], SCALE, crow[0:4, h:h + 1], ALU.mult, ALU.add, (ps, crow), (S,))
                    c0 += w_
                return W

            if not multi:
                W = scores(0)
                kb.reduce(st[0:4, 0:1], S[0:4, 0:W], ALU.max, (S,), (st,))
            else:
                kb.copy(st[0:4, 7:8], crow[0:4, h:h + 1], (crow,), (st,))
                for gi in range(len(groups)):
                    (W, kT_ap, kT_buf, bias_fn, mask_ap, vt) = groups[gi]
                    if bias_fn is not None:
                        load_bias(gi)
                        kb.reduce(st[0:4, 5:6], bt[0:4, 0:W], ALU.max, (bt,), (st,))
                        kb.tt(st[0:4, 7:8], st[0:4, 7:8], st[0:4, 5:6], ALU.max, (st,), (st,))
                    c0 = 0
                    while c0 < W:
                        w_ = min(512, W - c0)
                        ps = ps_next()
                        kb.mm(ps[0:4, 0:w_], q_ap, kT_ap[:, c0:c0 + w_], True, True, (qT, kT_buf), (ps,))
                        if gi == 0 and c0 == 0:
                            kb.reduce(st[0:4, 0:1], ps[0:4, 0:w_], ALU.max, (ps,), (st,))
                        else:
                            kb.reduce(st[0:4, 5:6], ps[0:4, 0:w_], ALU.max, (ps,), (st,))
                            kb.tt(st[0:4, 0:1], st[0:4, 0:1], st[0:4, 5:6], ALU.max, (st,), (st,))
                        c0 += w_
                kb.stt(st[0:4, 0:1], st[0:4, 0:1], SCALE, st[0:4, 7:8], ALU.mult, ALU.add, (st,), (st,))
            ts1(st[0:4, 1:2], st[0:4, 0:1], -1.0, ALU.mult, (st,), (st,))
            nmm = sum(len(gp[5]) for gp in groups)
            imm = 0
            for gi in range(len(groups)):
                (W, kT_ap, kT_buf, bias_fn, mask_ap, vt) = groups[gi]
                if multi:
                    scores(gi)
                if mask_ap is None:
                    kb.act(E[0:4, 0:W], S[0:4, 0:W], AF.Exp, (S, st), (E,), bias=st[0:4, 1:2])
                else:
                    kb.act(S[0:4, 0:W], S[0:4, 0:W], AF.Exp, (S, st), (S,), bias=st[0:4, 1:2])
                    if W >= 128:
                        kb.tt(E[0:4, 0:W].rearrange("p (b c) -> p b c", c=64), S[0:4, 0:W].rearrange("p (b c) -> p b c", c=64),
                              mask_ap, ALU.mult, (S, selm), (E,))
                    else:
                        kb.tt(E[0:4, 0:W], S[0:4, 0:W], mask_ap, ALU.mult, (S, selm), (E,))
                if gi == 0:
                    kb.reduce(st[0:4, 2:3], E[0:4, 0:W], ALU.add, (E,), (st,))
                else:
                    kb.reduce(st[0:4, 5:6], E[0:4, 0:W], ALU.add, (E,), (st,))
                    kb.tt(st[0:4, 2:3], st[0:4, 2:3], st[0:4, 5:6], ALU.add, (st,), (st,))
                for k0 in range(0, len(vt), 16):
                    ps = ps_next()
                    pv = psbf(ps)
                    kk_ = min(16, len(vt) - k0)
                    for j in range(kk_):
                        (cc0, cw, _, _) = vt[k0 + j]
                        kb.tr(pv[0:cw, j * 4:(j + 1) * 4], E[0:4, cc0:cc0 + cw], identb[0:4, 0:4], (E, identb), (ps,))
                    kb.copy(PTb[:, 0:kk_, :], pv[:, 0:kk_ * 4].rearrange("p (k q) -> p k q", q=4), (ps,), (PTb,), eng="act")
                    for j in range(kk_):
                        (cc0, cw, v_ap, v_buf) = vt[k0 + j]
                        kb.mm(pacc[0:4, 0:128], PTb[0:cw, j, :], v_ap, imm == 0, imm == nmm - 1, (PTb, v_buf), (pacc,))
                        imm += 1
            ts1(st[0:4, 2:3], st[0:4, 2:3], 1e-30, ALU.max, (st,), (st,))
            kb.recip(st[0:4, 3:4], st[0:4, 2:3], (st,), (st,))
            kb.tt(st[0:4, 4:5], st[0:4, 3:4], gate_col, ALU.mult, (st, gsig), (st,))
            if first:
                ts1(acc[0:4, hsl], pacc[0:4, 0:128], st[0:4, 4:5], ALU.mult, (pacc, st), (acc,))
            else:
                kb.stt(acc[0:4, hsl], pacc[0:4, 0:128], st[0:4, 4:5], acc[0:4, hsl], ALU.mult, ALU.add, (pacc, st, acc), (acc,))

        crow = A.alloc("crow4", 8)
        kb.dma(crow[0:4, 0:8], TSs.h[:, 0:1].rearrange("h o -> o h").to_broadcast([4, 8]), (TSs,), (crow,), crow, nc_ok=True)

        pib = page_index_bufs(A)
        for s_ in range(4):
            cols = slice(s_ * 4, s_ * 4 + 4)
            idx = page_index(A, s_, pib)
            ps = ps_next()
            for k in range(KC):
                kb.mm(ps[0:4, 0:24], hT[:, k, cols], wgate[:, k, :], k == 0, k == KC - 1, (hT, wgate), (ps,))
            kb.act(gsig[0:4, 0:24], ps[0:4, 0:24], AF.Sigmoid, (ps,), (gsig,))
            for g in range(2):
                for j in range(64):
                    pg = pgs[j % 2]
                    gather(pg, pg[:, 0:1024], ckv_rows[:, :], idx, j)
                    if j % 4 == 0:
                        psk = ps_next()
                    kb.tr(psk[:, (j % 4) * 128:(j % 4 + 1) * 128], pg[:, 512 + g * 128:640 + g * 128], ident[:, :], (pg, ident), (psk,))
                    kb.copy(Vs[:, j, :], pg[:, 768 + g * 128:896 + g * 128], (pg,), (Vs,), eng="dve")
                    if j % 4 == 3:
                        kb.copy(KTs[:, (j - 3) * 128:(j + 1) * 128], psk[:, :], (psk,), (KTs,), eng="act")
                kb.copy(KTs[:, PAST:PAST + 4], newK[:, 0, g, cols], (newK,), (KTs,))
                ps = ps_next()
                pvn = psbf(ps)
                kb.tr(pvn[0:4, 0:128], newVT[:, 0, g, cols], identb[:, :], (newVT, identb), (ps,))
                kb.tr(pvn[0:4, 128:256], newVT[:, 1, g, cols], identb[:, :], (newVT, identb), (ps,))
                kb.copy(Vs[0:4, 64, :], pvn[0:4, 0:128], (ps,), (Vs,))
                for j in range(4):
                    kb.dma(wst[:, 0:256].rearrange("p (two d) -> p two d", two=2),
                           cwin.h[s_, j * 128:(j + 1) * 128, :].rearrange("p (two g d) -> p two g d", two=2, g=2)[:, :, g, :], (), (wst,), wst, eng="pool")
                    ps = ps_next()
                    kb.tr(ps[:, 0:128], wst[:, 0:128], ident[:, :], (wst, ident), (ps,))
                    kb.copy(KW[:, j * 128:(j + 1) * 128], ps[:, 0:128], (ps,), (KW,), eng="act")
                    kb.copy(VW[:, j, :], wst[:, 128:256], (wst,), (VW,))
                kb.copy(KW[:, 512:516], newK[:, 1, g, cols], (newK,), (KW,))
                kb.copy(VW[0:4, 4, :], pvn[0:4, 128:256], (ps,), (VW,))
                for hh in range(4):
                    h = g * 4 + hh
                    hsl = slice(h * 128, (h + 1) * 128)
                    kb.dma(bt[0:4, 0:256], TCs.h[h].rearrange("(t n) -> t n", n=256), (TCs,), (bt,), bt)
                    ps = ps_next()
                    kb.mm(ps[0:4, 0:256], qT[:, h, cols], kcTs[:, s_, g, :], True, True, (qT, kcTs), (ps,))
                    kb.stt(S[0:4, 0:256], ps[0:4, 0:256], SCALE, bt[0:4, 0:256], ALU.mult, ALU.add, (ps, bt), (S,))
                    kb.reduce(st[0:4, 0:1], S[0:4, 0:256], ALU.max, (S,), (st,))
                    ts1(st[0:4, 1:2], st[0:4, 0:1], -1.0, ALU.mult, (st,), (st,))
                    kb.act(pc[0:4, 0:256], S[0:4, 0:256], AF.Exp, (S, st), (pc,), bias=st[0:4, 1:2])
                    kb.reduce(st[0:4, 2:3], pc[0:4, 0:256], ALU.add, (pc,), (st,))
                    kb.recip(st[0:4, 3:4], st[0:4, 2:3], (st,), (st,))
                    ts1(pc[0:4, 0:256], pc[0:4, 0:256], st[0:4, 3:4], ALU.mult, (pc, st), (pc,))
                    if hh == 0:
                        kb.copy(imp[0:4, 0:256], pc[0:4, 0:256], (pc,), (imp,))
                    else:
                        kb.tt(imp[0:4, 0:256], imp[0:4, 0:256], pc[0:4, 0:256], ALU.add, (imp, pc), (imp,))
                    kb.copy(pcb[0:4, 0:256], pc[0:4, 0:256], (pc,), (pcb,))
                    ps = ps_next()
                    pv = psbf(ps)
                    for j in range(2):
                        kb.tr(pv[:, j * 4:(j + 1) * 4], pcb[0:4, j * 128:(j + 1) * 128], identb[0:4, 0:4], (pcb, identb), (ps,))
                    kb.copy(PTb[:, 0:2, :], pv[:, 0:8].rearrange("p (k q) -> p k q", q=4), (ps,), (PTb,))
                    for j in range(2):
                        kb.mm(pacc[0:4, 0:128], PTb[:, j, :], vcs[:, s_, j, g, :], j == 0, j == 1, (PTb, vcs), (pacc,))
                    ts1(acc[0:4, hsl], pacc[0:4, 0:128], gsig[0:4, h:h + 1], ALU.mult, (pacc, gsig), (acc,))
                kb.copy(imp2[0:4, 0:136], bs_sb[0:4, 0:136], (bs_sb,), (imp2,))
                kb.reduce(impw[0:4, 0:128], imp[0:4, 0:256].rearrange("p (b two) -> p b two", two=2), ALU.add, (imp,), (impw,))
                kb.tt(imp2[0:4, 0:128], imp2[0:4, 0:128], impw[0:4, 0:128], ALU.add, (imp2, impw), (imp2,))
                P.op("dve", lambda e: e.max(mx8[0:4, 0:8], imp2[0:4, 0:136]), (imp2,), (mx8,))
                P.op("dve", lambda e: e.match_replace(impw[0:4, 0:136], mx8[0:4, 0:8], imp2[0:4, 0:136], -3e38), (mx8, imp2), (impw,))
                P.op("dve", lambda e: e.max(mx8[0:4, 8:16], impw[0:4, 0:136]), (impw,), (mx8,))
                kb.reduce(st[0:4, 6:7], mx8[0:4, 8:16], ALU.min, (mx8,), (st,))
                ts1(selm[0:4, 0:136], imp2[0:4, 0:136], st[0:4, 6:7], ALU.is_ge, (imp2, st), (selm,))
                for hh in range(4):
                    h = g * 4 + hh
                    hsl = slice(h * 128, (h + 1) * 128)
                    q_ap = qT[:, h, cols]
                    groups = []
                    for gi in range(7):
                        k0 = gi * 1024
                        groups.append((1024, KTs[:, k0:k0 + 1024], KTs, None,
                                       selm[0:4, gi * 16:(gi + 1) * 16].unsqueeze(2).to_broadcast([4, 16, 64]),
                                       [(kt * 128, 128, Vs[:, gi * 8 + kt, :], Vs) for kt in range(8)]))
                    for k0 in (7168, 7680):
                        groups.append((512, KTs[:, k0:k0 + 512], KTs,
                                       (lambda t, k0=k0, h=h: TSs.h[h:h + 1, k0 - t + 3:k0 - t + 3 + 512]),
                                       selm[0:4, k0 // 64:k0 // 64 + 8].unsqueeze(2).to_broadcast([4, 8, 64]),
                                       [(kt * 128, 128, Vs[:, k0 // 128 + kt, :], Vs) for kt in range(4)]))
                    groups.append((4, KTs[:, PAST:PAST + 4], KTs,
                                   (lambda t, h=h: TSs.h[h:h + 1, PAST - t + 3:PAST - t + 3 + 4]),
                                   selm[0:4, 128:129].to_broadcast([4, 4]),
                                   [(0, 4, Vs[0:4, 64, :], Vs)]))
                    attend(h, q_ap, groups, gsig[0:4, 8 + h:9 + h], False, hsl)
                    gw = [(516, KW[:, 0:516], KW, (lambda t, h=h: TWs.h[h:h + 1, 3 - t:3 - t + 516]), None,
                           [(j * 128, 128, VW[:, j, :], VW) for j in range(4)] + [(512, 4, VW[0:4, 4, :], VW)])]
                    attend(h, q_ap, gw, gsig[0:4, 16 + h:17 + h], False, hsl)
            for h0 in range(0, 8, 4):
                ps = ps_next()
                for j in range(4):
                    kb.tr(ps[:, j * 4:(j + 1) * 4], acc[0:4, (h0 + j) * 128:(h0 + j + 1) * 128], ident[0:4, 0:4], (acc, ident), (ps,))
                kb.copy(ynsaT[:, h0:h0 + 4, cols], ps[:, 0:16].rearrange("p (h q) -> p h q", h=4), (ps,), (ynsaT,))
        ps_mod[0] = 8

    def merge_and_out(T):
        A = arena
        fT = A.alloc("fTm", KC * TMAX)
        t1 = A.alloc("mt1", TMAX); t2 = A.alloc("mt2", TMAX)
        for mb in range(D // 256):
            for bi, (yT, ) in enumerate(((yrwT,), (ynsaT,))):
                sg = ws.take()
                sb_ = ws.take()
                vg = sg[:, 0:16 * 256].rearrange("p (k n) -> p k n", k=16)
                vb = sb_[:, 0:8 * 256].rearrange("p (k n) -> p k n", k=8)
                for mi in range(2):
                    pg = ps_next(); pb = ps_next()
                    for k in range(KC):
                        kb.mm(pg[:, :T], vg[:, k, mi * 128:(mi + 1) * 128], hT[:, k, :T], k == 0, k == KC - 1, (sg, hT), (pg,))
                    for k in range(8):
                        kb.mm(pb[:, :T], vb[:, k, mi * 128:(mi + 1) * 128], yT[:, k, :T], k == 0, k == 7, (sb_, yT), (pb,))
                    m = mb * 2 + mi
                    tt_ = t1 if mi == 0 else t2
                    kb.act(tmpA[:, :T], pg[:, :T], AF.Sigmoid, (pg,), (tmpA,))
                    if bi == 0:
                        kb.tt(tt_[:, :T], tmpA[:, :T], pb[:, :T], ALU.mult, (tmpA, pb), (tt_,))
                    else:
                        kb.tt(tmpA[:, :T], tmpA[:, :T], pb[:, :T], ALU.mult, (tmpA, pb), (tmpA,))
                        kb.tt(mergedT[:, m, :T], tmpA[:, :T], tt_[:, :T], ALU.add, (tmpA, tt_), (mergedT,))
        for mb in range(D // 256):
            so = ws.take()
            vo = so[:, 0:16 * 256].rearrange("p (k n) -> p k n", k=16)
            for mi in range(2):
                po = ps_next()
                for k in range(KC):
                    kb.mm(po[:, :T], vo[:, k, mi * 128:(mi + 1) * 128], mergedT[:, k, :T], k == 0, k == KC - 1, (so, mergedT), (po,))
                m = mb * 2 + mi
                kb.copy(fT[:, m * TMAX:m * TMAX + T], po[:, :T], (po,), (fT,), eng="act")
        post_residual(fT, g_mixpost, 1.0, T)

    if not cfg.get("skip_nsa"):
        nsa_setup()
        nsa_setup_sample()
    for ti, (kind, t0, T) in enumerate(tiles):
        arena.reset()
        stage_box[0] = arena.alloc("stage", KC * TMAX)
        load_x(kind, t0, T)
        ffn(g_f1pre, g_f1post, T, tag=(ti, 1))
        pre_norm(g_mixpre, T)
        arena.reset()
        stage_box[0] = arena.alloc("stage", 8 * TMAX)
        kvT = arena.alloc("kvT", 12 * TMAX, view=lambda a: a.rearrange("p (k t) -> p k t", k=12))
        win_proj_a(T, kvT)
        if kind == "p":
            if not cfg.get("skip_nsa"):
                nsa_cache_update(t0, T, ti, kvT)
            store_tokmajor(kvp.h[t0:t0 + T, :], lambda c: kvT[:, c, :], (kvT,), 8, T)
            if t0 + T > SEQ_ - 512:
                w0 = t0 - (SEQ_ - 512)
                store_tokmajor(winp.h[w0:w0 + T, :], lambda c: kvT[:, 8 + c, :], (kvT,), 4, T)
        else:
            if not cfg.get("skip_nsa"):
                nsa_sample_newrows(kvT)
            store_tokmajor(kvs.h[0:T, :], lambda c: kvT[:, c, :], (kvT,), 8, T)
            store_tokmajor(None, lambda c: kvT[:, 8 + c, :], (kvT,), 4, T)
            stage = stage_box[0]
            for sq_i in range(4):
                kb.dma(wins.h[sq_i, 508:512, :], stage[sq_i * 4:sq_i * 4 + 4, 0:512], (stage,), (), stage)
                kb.dma(wins.h[sq_i, 0:508, :], cwin.h[sq_i, 4:512, :], (), (), stage)
        arena.reset()
        if cfg.get("skip_rwkv"):
            kb.memset(yrwT[:, :, :], 0.0, (yrwT,))
        else:
            rwkv(kind, t0, T, ti)
        arena.reset()
        if kind == "p" and not cfg.get("skip_nsa"):
            nsa_prompt(t0, T, ti)
        if kind == "s" and not cfg.get("skip_nsa"):
            nsa_sample_compress()
            arena.reset_from(smp["mark"])
            nsa_sample_attend()
        arena.reset()
        merge_and_out(T)
        arena.reset()
        stage_box[0] = arena.alloc("stage", KC * TMAX)
        ffn(g_f2pre, g_f2post, T, second=True, tag=(ti, 2))
        if kind == "p":
            store_tokmajor(yp.h[t0:t0 + T, :], lambda c: xT[:, c, :], (xT,), KC, T)
        else:
            store_tokmajor(ys.h[0:T, :], lambda c: xT[:, c, :], (xT,), KC, T)


_NC_CACHE = {}


def _rwmask(C):
    r = np.arange(C)[:, None]
    c = np.arange(C)[None, :]
    return np.concatenate([(r < c), (r <= c), (r > c)], axis=1).astype(np.float32)


def _bucket(d):
    import math
    n = np.maximum(d, 0)
    nf = np.maximum(n, 1).astype(np.float32)
    large = 16 + (np.log(nf / np.float32(16)) / np.float32(math.log(64)) * np.float32(16)).astype(np.int32)
    large = np.minimum(large, 31)
    return np.where(n < 16, n, large)


def _onehot33(d, valid, masked):
    oh = np.zeros((33,) + d.shape, np.float32)
    b = _bucket(d)
    for k in range(32):
        oh[k] = ((b == k) & valid).astype(np.float32)
    oh[32] = np.where(masked, -30000.0, 0.0)
    return oh


def _nsa_consts():
    import ml_dtypes
    y = np.arange(2176); d = 2047 - y
    oh_sel = _onehot33(d, d >= 0, d < 0)
    y = np.arange(768); d = 639 - y
    oh_win = _onehot33(d, (d >= 0) & (d < 512), (d < 0) | (d >= 512))
    q = np.arange(128)[:, None]; jj = np.arange(67)[None, :]
    d = q - 32 * (jj - 63) - 31
    oh_cmp = _onehot33(d, d >= 0, np.zeros_like(d, bool)).reshape(33, 128 * 67)
    mc = (d >= 0).astype(np.float32)
    bc = np.zeros((128, 16, 32), np.float32)
    for i in range(16):
        pos = 128 * i + np.arange(128)[:, None]
        cur = pos // 64
        sb = np.arange(32)[None, :]
        forced = (sb == 0) | (sb == cur) | (sb == cur - 1)
        vis = sb * 64 <= pos
        bc[:, i, :] = np.where(vis, np.where(forced, 1e4, 0.0), -1e30)
    x = np.arange(8704); d = 8195 - x
    oh_s_sel = _onehot33(d, d >= 0, d < 0)
    x = np.arange(1024); d = 515 - x
    oh_s_win = _onehot33(d, (d >= 0) & (d < 512), (d < 0) | (d >= 512))
    t = np.arange(4)[:, None]; n = np.arange(256)[None, :]
    d = 8192 + t - (32 * n + 31)
    oh_s_cmp = _onehot33(d, d >= 0, np.zeros_like(d, bool)).reshape(33, 1024)
    bs = np.zeros((4, 136), np.float32)
    bs[:, [0, 127, 128]] = 1e4
    bs[:, 129:] = -1e30
    return {"oh_s_sel": oh_s_sel, "oh_s_win": oh_s_win, "oh_s_cmp": oh_s_cmp, "bs": bs,
            "iotap": np.arange(128, dtype=np.float32).reshape(128, 1),
            "oh_sel": oh_sel, "oh_win": oh_win, "oh_cmp": oh_cmp, "mc": mc, "bc": bc,
            "identb": np.eye(128, dtype=np.float32).astype(ml_dtypes.bfloat16)}


def _consts():
    bones = np.zeros((128, 128), np.float32)
    bones[:64, :64] = 1.0
    bones[64:, 64:] = 1.0
    return {"ident": np.eye(128, dtype=np.float32), "ones": np.ones((128, 128), dtype=np.float32),
            "rwmask": _rwmask(64), "rwmask4": _rwmask(4), "blockones": bones,
            "selhi": np.concatenate([np.zeros((64, 64), np.float32), np.eye(64, dtype=np.float32)], 0),
            **_nsa_consts()}


def kernel(**inputs):
    TP = 256
    tiles = [("p", t0, TP) for t0 in range(0, SEQ, TP)] + [("s", 0, 16)]
    cfg = {"tiles": tiles}
    nc = build(cfg)
    f32 = np.float32

    def w(name):
        return np.ascontiguousarray(np.asarray(inputs[name], dtype=f32)[0])

    shared = {k: w(k) for k in
              ("ffn1_pre_g", "ffn1_post_g", "ffn1_w1", "ffn1_w3", "ffn1_w2", "mix_pre_g", "mix_post_g", "w_in",
               "rw_mu", "rw_w0", "rw_w2", "rw_a0", "rw_a2", "rw_g2", "rw_k_k", "rw_k_a", "rw_lnx_w", "rw_lnx_b",
               "w_br_rw", "w_br_nsa", "w_out", "ffn2_pre_g", "ffn2_post_g", "ffn2_w1", "ffn2_w3", "ffn2_w2")}
    shared["rw_r_k"] = w("rw_r_k").reshape(RW_DIM)
    for k in ("cmp_pe", "cmp_w1", "cmp_w2"):
        shared[k] = w(k)
    shared["rel_bias"] = np.ascontiguousarray(np.asarray(inputs["rel_bias"], dtype=f32))
    shared["cache_kv"] = np.ascontiguousarray(np.asarray(inputs["cache_kv"], dtype=f32)[0]).reshape(2560, 128, 1024)
    page_table = np.asarray(inputs["page_table"], dtype=np.int32)
    shared.update(_consts())
    x_prompt = np.asarray(inputs["x_prompt"], dtype=f32)
    x_sample = np.asarray(inputs["x_sample"], dtype=f32)
    cache_win = np.asarray(inputs["cache_win"], dtype=f32)
    state_rwkv = np.asarray(inputs["state_rwkv"], dtype=f32)
    state_shift = np.asarray(inputs["state_shift"], dtype=f32)
    in_maps = []
    for c in range(NCORE):
        m = dict(shared)
        m["xp"] = np.ascontiguousarray(x_prompt[c])
        m["xs"] = np.ascontiguousarray(x_sample[4 * c:4 * c + 4].reshape(16, D))
        m["cache_win"] = np.ascontiguousarray(cache_win[0, 4 * c:4 * c + 4].reshape(4, 512, 512))
        m["state_rwkv"] = np.ascontiguousarray(state_rwkv[0, 4 * c:4 * c + 4])
        m["state_shift"] = np.ascontiguousarray(state_shift[0, 4 * c:4 * c + 4])
        m["page_table"] = np.ascontiguousarray(page_table[4 * c:4 * c + 4])
        in_maps.append(m)
    res = run_bass_kernel_spmd(nc, in_maps, core_ids=list(range(NCORE)))
    R = res.results
    y_prompt = np.stack([R[c]["y_prompt"] for c in range(NCORE)], 0)
    y_sample = np.concatenate([R[c]["y_sample"].reshape(4, 4, D) for c in range(NCORE)], 0)
    kv_prompt = np.stack([R[c]["kv_prompt"].reshape(SEQ, 4, 2, 128) for c in range(NCORE)], 0)[None]
    kv_sample = np.concatenate([R[c]["kv_sample"].reshape(4, 4, 4, 2, 128) for c in range(NCORE)], 0)[None]
    win_prompt = np.stack([R[c]["win_prompt"].reshape(512, 2, 2, 128) for c in range(NCORE)], 0)[None]
    win_sample = np.concatenate([R[c]["win_sample"].reshape(4, 512, 2, 2, 128) for c in range(NCORE)], 0)[None]
    rwkv_prompt = np.stack([R[c]["rwkv_prompt"] for c in range(NCORE)], 0)[None]
    rwkv_sample = np.concatenate([R[c]["rwkv_sample"] for c in range(NCORE)], 0)[None]
    shift_prompt = np.stack([R[c]["shift_prompt"] for c in range(NCORE)], 0)[None]
    shift_sample = np.concatenate([R[c]["shift_sample"] for c in range(NCORE)], 0)[None]
    return (y_prompt, y_sample, kv_prompt, kv_sample, win_prompt, win_sample,
            rwkv_prompt, rwkv_sample, shift_prompt, shift_sample)
```

```python
import numpy as np
from contextlib import ExitStack
import concourse.bass as bass
import concourse.mybir as mybir
from concourse.bass_utils import run_bass_kernel_spmd

F32 = mybir.dt.float32
BF16 = mybir.dt.bfloat16
I32 = mybir.dt.int32
AF = mybir.ActivationFunctionType
ALU = mybir.AluOpType
AX = mybir.AxisListType

D = 2048
DFF = 5632
SEQ = 2048
NCORE = 8
RW_DIM = 1024
RW_PROJ = 3328
IN_COLS = 10008
KC = D // 128
EPS = 1e-6

ENGS = ("pe", "act", "dve", "pool", "sp")
SAME_ENGINE_SYNC = True


class Buf:
    def __init__(self, name, h):
        self.name = name
        self.h = h
        self.last_ws = []
        self.readers = []
        self.group_deps = []
        self.sem = None
        self.sem_count = 0
        self.semholder = self

    def __getitem__(self, idx):
        return self.h[idx]


class SemHolder:
    def __init__(self, name):
        self.name = name
        self.sem = None
        self.sem_count = 0


class _AliasBuf:
    def __init__(self, base, view):
        object.__setattr__(self, "_base", base)
        object.__setattr__(self, "h", view.h)

    def __getattr__(self, k):
        return getattr(object.__getattribute__(self, "_base"), k)

    def __setattr__(self, k, v):
        setattr(object.__getattribute__(self, "_base"), k, v)

    def __getitem__(self, idx):
        return object.__getattribute__(self, "h")[idx]


def _alias(base, view):
    return _AliasBuf(base, view)


class Op:
    __slots__ = ("eng", "fn", "deps", "is_dma", "sem_buf", "count", "signaled", "value", "idx")


class Prog:
    def __init__(self, nc, stack):
        self.nc = nc
        self.stack = stack
        self.eng_ops = {e: [] for e in ENGS}
        self.bufs = []
        self.nops = 0

    def sbuf(self, name, shape, dtype):
        h = self.stack.enter_context(self.nc.sbuf_tensor(name, list(shape), dtype))
        b = Buf(name, h)
        self.bufs.append(b)
        return b

    def psum(self, name, shape, dtype):
        h = self.stack.enter_context(self.nc.psum_tensor(name, list(shape), dtype))
        b = Buf(name, h)
        self.bufs.append(b)
        return b

    def dram(self, name, shape, dtype, kind):
        h = self.nc.dram_tensor(name, list(shape), dtype, kind=kind)
        b = Buf(name, h.ap())
        b.t = h
        self.bufs.append(b)
        return b

    def op(self, eng, fn, reads=(), writes=(), dma=False, sem_buf=None):
        o = Op()
        o.eng, o.fn, o.is_dma, o.signaled, o.value = eng, fn, dma, False, None
        o.idx = self.nops
        self.nops += 1
        deps = []
        for b in reads:
            deps.extend(b.last_ws)
        for b in writes:
            concurrent = (dma and b.last_ws and not b.readers and all(w.is_dma for w in b.last_ws))
            if concurrent:
                deps.extend(b.group_deps)
            else:
                deps.extend(b.last_ws)
                deps.extend(b.readers)
        seen = set()
        o.deps = []
        for d in deps:
            if id(d) in seen:
                continue
            seen.add(id(d))
            if (not d.is_dma) and d.eng == eng:
                if eng == "pe" or eng == "sp" or not SAME_ENGINE_SYNC:
                    continue
            o.deps.append(d)
        for b in reads:
            b.readers.append(o)
        for b in writes:
            concurrent = (dma and b.last_ws and not b.readers and all(w.is_dma for w in b.last_ws))
            if concurrent:
                b.last_ws.append(o)
            else:
                b.group_deps = list(b.last_ws) + list(b.readers)
                b.last_ws = [o]
                b.readers = []
        if dma:
            if sem_buf is None:
                raise ValueError("dma needs sem_buf")
            hold = sem_buf.semholder
            o.sem_buf = hold
            hold.sem_count += 16
            o.count = hold.sem_count
        self.eng_ops[eng].append(o)
        return o

    def emit(self):
        nc = self.nc
        for e in ENGS:
            for o in self.eng_ops[e]:
                for d in o.deps:
                    if not d.is_dma:
                        d.signaled = True
        for e in ENGS:
            n = 0
            for o in self.eng_ops[e]:
                if o.is_dma:
                    continue
                if o.signaled:
                    n += 1
                    o.value = n
        esem = {e: self.stack.enter_context(nc.semaphore("es_" + e)) for e in ENGS}
        holders, seen = [], set()
        for b in self.bufs:
            h = b.semholder
            if id(h) not in seen and h.sem_count > 0:
                seen.add(id(h))
                holders.append(h)
        for h in holders:
            h.sem = self.stack.enter_context(nc.semaphore("ds_" + h.name))
        dma_bufs = holders
        prog = self

        def run_engine(e, eng):
            waited = {}
            for o in prog.eng_ops[e]:
                need = {}
                for d in o.deps:
                    if d.is_dma:
                        s, v = d.sem_buf.sem, d.count
                    else:
                        s, v = esem[d.eng], d.value
                    key = id(s)
                    if key not in need or need[key][1] < v:
                        need[key] = (s, v)
                for key, (s, v) in need.items():
                    if waited.get(key, 0) >= v:
                        continue
                    waited[key] = v
                    eng.wait_ge(s, v)
                ins = o.fn(eng)
                if o.is_dma:
                    ins.then_inc(o.sem_buf.sem, 16)
                elif o.signaled:
                    ins.then_inc(esem[e], 1)
            if e == "sp":
                for b in dma_bufs:
                    eng.wait_ge(b.sem, b.sem_count)

        with nc.Block() as block:
            @block.tensor
            def _(eng):
                run_engine("pe", eng)

            @block.scalar
            def _(eng):
                run_engine("act", eng)

            @block.vector
            def _(eng):
                run_engine("dve", eng)

            @block.gpsimd
            def _(eng):
                run_engine("pool", eng)

            @block.sync
            def _(eng):
                run_engine("sp", eng)


class K:
    def __init__(self, P):
        self.P = P

    def dma(self, dst_ap, src_ap, reads, writes, sem_buf, eng="sp", nc_ok=False):
        def fn(e):
            if nc_ok:
                return e.dma_start(out=dst_ap, in_=src_ap, allow_slow_non_contiguous=True)
            return e.dma_start(out=dst_ap, in_=src_ap)
        return self.P.op(eng, fn, reads, writes, dma=True, sem_buf=sem_buf)

    def mm(self, out_ap, lhsT_ap, rhs_ap, start, stop, reads, writes):
        return self.P.op("pe", lambda e: e.matmul(out_ap, lhsT_ap, rhs_ap, start=start, stop=stop), reads, writes)

    def tr(self, out_ap, in_ap, ident_ap, reads, writes):
        return self.P.op("pe", lambda e: e.transpose(out_ap, in_ap, ident_ap), reads, writes)

    def act(self, out_ap, in_ap, func, reads, writes, bias=None, scale=None, accum=None):
        def fn(e):
            kw = {}
            if bias is not None:
                kw["bias"] = bias
            if scale is not None:
                kw["scale"] = scale
            if accum is not None:
                kw["accum_out"] = accum
            return e.activation(out_ap, in_ap, func, **kw)
        return self.P.op("act", fn, reads, writes)

    def tt(self, out_ap, a_ap, b_ap, op, reads, writes, eng="dve"):
        return self.P.op(eng, lambda e: e.tensor_tensor(out_ap, a_ap, b_ap, op), reads, writes)

    def ts(self, out_ap, a_ap, s1, s2, op0, op1, reads, writes, eng="dve"):
        if op1 is None:
            return self.P.op(eng, lambda e: e.tensor_scalar(out_ap, a_ap, s1, None, op0), reads, writes)
        return self.P.op(eng, lambda e: e.tensor_scalar(out_ap, a_ap, s1, s2, op0, op1), reads, writes)

    def stt(self, out_ap, a_ap, s, b_ap, op0, op1, reads, writes, eng="dve"):
        return self.P.op(eng, lambda e: e.scalar_tensor_tensor(out_ap, a_ap, s, b_ap, op0, op1), reads, writes)

    def copy(self, out_ap, in_ap, reads, writes, eng="dve"):
        if eng == "act":
            return self.P.op("act", lambda e: e.copy(out_ap, in_ap), reads, writes)
        return self.P.op(eng, lambda e: e.tensor_copy(out_ap, in_ap), reads, writes)

    def reduce(self, out_ap, in_ap, op, reads, writes):
        return self.P.op("dve", lambda e: e.tensor_reduce(out_ap, in_ap, AX.X, op), reads, writes)

    def recip(self, out_ap, in_ap, reads, writes):
        return self.P.op("dve", lambda e: e.reciprocal(out_ap, in_ap), reads, writes)

    def memset(self, ap, val, writes, eng="dve"):
        return self.P.op(eng, lambda e: e.memset(ap, val), (), writes)


class Arena:
    def __init__(self, P, name, words):
        self.P = P
        self.t = P.stack.enter_context(P.nc.sbuf_tensor(name, [128, words], F32))
        self.words = words
        self.off = 0
        self.live = []
        self.pending = []
        self.n = 0
        self.holders = {}

    def reset(self):
        ops = list(self.pending)
        for b in self.live:
            ops += b.last_ws + b.readers
        best, dmas, seen = {}, [], set()
        for o in ops:
            if o.is_dma:
                if id(o) not in seen:
                    seen.add(id(o))
                    dmas.append(o)
            elif o.eng not in best or o.idx > best[o.eng].idx:
                best[o.eng] = o
        bd = {}
        for o in dmas:
            k = id(o.sem_buf)
            if k not in bd or o.count > bd[k].count:
                bd[k] = o
        self.pending = list(best.values()) + list(bd.values())
        self.live = []
        self.off = 0

    def reset_from(self, off):
        keep = [b for b in self.live if b._off < off]
        drop = [b for b in self.live if b._off >= off]
        self.live = drop
        save_pending = self.pending
        self.reset()
        self.live = keep
        self.off = off

    def alloc(self, name, words, view=None):
        assert self.off + words <= self.words, (name, self.off, words, self.words)
        ap = self.t[:, self.off:self.off + words]
        off0 = self.off
        self.off += words
        if view is not None:
            ap = view(ap)
        self.n += 1
        b = Buf("%s_%d" % (name, self.n), ap)
        if name not in self.holders:
            self.holders[name] = SemHolder("ar_" + name)
        b.semholder = self.holders[name]
        b._off = off0
        b.last_ws = list(self.pending)
        self.P.bufs.append(b)
        self.live.append(b)
        return b


class WeightStream:
    def __init__(self, kb, nslots, slot_elems):
        self.kb = kb
        self.base = [kb.P.sbuf("wslot%d" % i, [128, slot_elems], BF16) for i in range(nslots)]
        self.extra = []
        self.extra_tag = None
        self.plan = []
        self.where = {}
        self.held = {}
        self.issued = 0
        self.taken = 0

    def add(self, fn, tag=None):
        self.plan.append((fn, tag))

    def set_extra(self, slots, tag):
        self.extra = list(slots)
        self.extra_tag = tag

    def clear_extra(self):
        for s_ in self.extra:
            b = self.held.get(id(s_))
            assert b is None or b < self.taken, "extra slot still holds an untaken block"
            self.held.pop(id(s_), None)
        self.extra = []
        self.extra_tag = None

    def prefetch(self):
        while self.issued < len(self.plan):
            fn, tag = self.plan[self.issued]
            cands = self.base + (self.extra if (tag is not None and tag == self.extra_tag) else [])
            slot = None
            for c in cands:
                b = self.held.get(id(c))
                if b is None or b < self.taken - 1:
                    slot = c
                    break
            if slot is None:
                break
            for dst_ap, src_ap in fn(slot):
                self.kb.dma(dst_ap, src_ap, (), (slot,), slot, eng="pool")
            self.held[id(slot)] = self.issued
            self.where[self.issued] = slot
            self.issued += 1

    def take(self):
        self.prefetch()
        assert self.taken in self.where, "weight block not issued"
        slot = self.where.pop(self.taken)
        self.taken += 1
        return slot


def build(cfg):
    nc = bass.Bass("TRN2", target_bir_lowering=False)
    with ExitStack() as stack:
        P = Prog(nc, stack)
        kb = K(P)
        _build_body(nc, P, kb, cfg)
        P.emit()
    return nc


def _build_body(nc, P, kb, cfg):
    tiles = cfg["tiles"]
    SEQ_ = cfg.get("seq", SEQ)
    TMAX = max([t[2] for t in tiles] + [256])
    def din(name, shape, dt=F32):
        return P.dram(name, shape, dt, "ExternalInput")

    def dout(name, shape, dt=F32):
        return P.dram(name, shape, dt, "ExternalOutput")

    xp = din("xp", [SEQ, D])
    xs = din("xs", [16, D])
    ident_d = din("ident", [128, 128])
    ones_d = din("ones", [128, 128])
    f1_pre = din("ffn1_pre_g", [D]); f1_post = din("ffn1_post_g", [D])
    f1_w1 = din("ffn1_w1", [D, DFF]); f1_w3 = din("ffn1_w3", [D, DFF]); f1_w2 = din("ffn1_w2", [DFF, D])
    mix_pre = din("mix_pre_g", [D]); mix_post = din("mix_post_g", [D])
    w_in = din("w_in", [D, IN_COLS])

    yp = dout("y_prompt", [SEQ, D])
    ys = dout("y_sample", [16, D])
    kvp = dout("kv_prompt", [SEQ, 1024])
    kvs = dout("kv_sample", [16, 1024])
    winp = dout("win_prompt", [512, 512])
    shp = dout("shift_prompt", [RW_PROJ])
    shs = dout("shift_sample", [4, RW_PROJ])
    wins = dout("win_sample", [4, 512, 512])
    cwin = din("cache_win", [4, 512, 512])
    rwp = dout("rwkv_prompt", [16, 64, 64])
    rws = dout("rwkv_sample", [4, 16, 64, 64])
    st_rw = din("state_rwkv", [4, 16, 64, 64])
    st_sh = din("state_shift", [4, RW_PROJ])
    mask_d = din("rwmask", [64, 192])
    mask4_d = din("rwmask4", [4, 12])
    selhi_d = din("selhi", [128, 64])
    identb_d = din("identb", [128, 128], BF16)
    cmp_pe = din("cmp_pe", [32, 2, 128]); cmp_w1 = din("cmp_w1", [2, 4096, 256]); cmp_w2 = din("cmp_w2", [2, 256, 128])
    rel_bias = din("rel_bias", [32, 8])
    ohs_d = din("oh_sel", [33, 2176]); ohw_d = din("oh_win", [33, 768]); ohc_d = din("oh_cmp", [33, 128 * 67])
    mc_d = din("mc", [128, 67]); bc_d = din("bc", [128, 16, 32])
    NPOOL = cfg.get("npool", 2560)
    ckv = din("cache_kv", [NPOOL, 128, 1024])
    ptab = din("page_table", [4, 64], I32)
    ohss_d = din("oh_s_sel", [33, 8704]); ohsw_d = din("oh_s_win", [33, 1024]); ohsc_d = din("oh_s_cmp", [33, 1024])
    bs_d = din("bs", [4, 136]); iotap_d = din("iotap", [128, 1])
    bones_d = din("blockones", [128, 128])
    rw_mu = din("rw_mu", [RW_PROJ]); rw_w0 = din("rw_w0", [RW_DIM]); rw_w2 = din("rw_w2", [64, RW_DIM])
    rw_a0 = din("rw_a0", [RW_DIM]); rw_a2 = din("rw_a2", [64, RW_DIM]); rw_g2 = din("rw_g2", [128, RW_DIM])
    rw_k_k = din("rw_k_k", [RW_DIM]); rw_k_a = din("rw_k_a", [RW_DIM]); rw_r_k = din("rw_r_k", [RW_DIM])
    rw_lnx_w = din("rw_lnx_w", [RW_DIM]); rw_lnx_b = din("rw_lnx_b", [RW_DIM])
    w_br_rw = din("w_br_rw", [RW_DIM, D]); w_br_nsa = din("w_br_nsa", [1024, D]); w_out = din("w_out", [D, D])
    f2_pre = din("ffn2_pre_g", [D]); f2_post = din("ffn2_post_g", [D])
    f2_w1 = din("ffn2_w1", [D, DFF]); f2_w3 = din("ffn2_w3", [D, DFF]); f2_w2 = din("ffn2_w2", [DFF, D])

    psb = [P.psum("ps%d" % i, [128, 512], F32) for i in range(8)]
    ps_i = [0]

    ps_mod = [8]

    def ps_next():
        b = psb[ps_i[0] % ps_mod[0]]
        ps_i[0] += 1
        return b

    ident = P.sbuf("ident_sb", [128, 128], F32)
    ones = P.sbuf("ones_sb", [128, 128], F32)
    kb.dma(ident[:], ident_d[:], (), (ident,), ident)
    kb.dma(ones[:], ones_d[:], (), (ones,), ones)

    vstg = P.sbuf("vstg", [32, 128], F32)
    vstg2 = P.sbuf("vstg2", [32, 128], F32)

    def load_col(dst_ap, dst_buf, src_row_ap, k):
        kb.dma(vstg[0:k, :], src_row_ap.rearrange("(k p) -> k p", p=128), (), (vstg,), vstg)
        ps = ps_next()
        kb.tr(ps[:, 0:k], vstg[0:k, :], ident[0:k, 0:k], (vstg, ident), (ps,))
        kb.copy(dst_ap, ps[:, 0:k], (ps,), (dst_buf,))

    def store_col(dst_row_ap, src_ap, src_buf, k):
        ps = ps_next()
        kb.tr(ps[0:k, 0:128], src_ap, ident[:, :], (src_buf, ident), (ps,))
        kb.copy(vstg2[0:k, :], ps[0:k, 0:128], (ps,), (vstg2,))
        kb.dma(dst_row_ap.rearrange("(k p) -> k p", p=128), vstg2[0:k, :], (vstg2,), (), vstg2)

    def load_vec(name, d_buf, n):
        t = P.sbuf(name, [128, n // 128], F32)
        load_col(t[:, :], t, d_buf.h, n // 128)
        return t

    g_f1pre = load_vec("g_f1pre", f1_pre, D)
    g_f1post = load_vec("g_f1post", f1_post, D)
    g_mixpre = load_vec("g_mixpre", mix_pre, D)
    g_mixpost = load_vec("g_mixpost", mix_post, D)
    g_f2pre = load_vec("g_f2pre", f2_pre, D)
    g_f2post = load_vec("g_f2post", f2_post, D)
    v_mu = load_vec("v_mu", rw_mu, RW_PROJ)
    v_w0 = load_vec("v_w0", rw_w0, RW_DIM); v_a0 = load_vec("v_a0", rw_a0, RW_DIM)
    v_kk = load_vec("v_kk", rw_k_k, RW_DIM); v_ka = load_vec("v_ka", rw_k_a, RW_DIM)
    v_rk = load_vec("v_rk", rw_r_k, RW_DIM)
    v_lw = load_vec("v_lw", rw_lnx_w, RW_DIM); v_lb = load_vec("v_lb", rw_lnx_b, RW_DIM)
    w2z = P.sbuf("w2z", [128, RW_DIM], F32)
    a2z = P.sbuf("a2z", [128, RW_DIM], F32)
    kb.memset(w2z[:, :], 0.0, (w2z,))
    kb.memset(a2z[:, :], 0.0, (a2z,))
    kb.dma(w2z[0:64, :], rw_w2.h[:, :], (w2z,), (w2z,), w2z)
    kb.dma(a2z[64:128, :], rw_a2.h[:, :], (a2z,), (a2z,), a2z)
    selhi = P.sbuf("selhi_sb", [128, 64], F32)
    kb.dma(selhi[:], selhi_d.h[:, :], (), (selhi,), selhi)
    g2sb = P.sbuf("g2sb", [128, RW_DIM], F32)
    kb.dma(g2sb[:], rw_g2.h[:, :], (), (g2sb,), g2sb)
    rwmask = P.sbuf("rwmask_sb", [64, 192], F32)
    kb.dma(rwmask[:], mask_d.h[:, :], (), (rwmask,), rwmask)
    identb = P.sbuf("identb_sb", [128, 128], BF16)
    kb.dma(identb[:], identb_d.h[:, :], (), (identb,), identb)
    bones = P.sbuf("bones_sb", [128, 128], F32)
    kb.dma(bones[:], bones_d.h[:, :], (), (bones,), bones)

    xT = P.sbuf("xT", [128, KC, TMAX], F32)
    hT = P.sbuf("hT", [128, KC, TMAX], BF16)
    big1 = P.sbuf("big1", [128, 26 * TMAX], F32)
    gT = Buf("gTv", big1.h[:, 0:(DFF // 128) * TMAX // 2].bitcast(BF16).rearrange("p (k t) -> p k t", k=DFF // 128))
    pT = Buf("pTv", big1.h[:, :].rearrange("p (k t) -> p k t", k=26))
    gT = _alias(big1, gT); pT = _alias(big1, pT)
    mergedT = Buf("mTv", big1.h[:, 0:KC * TMAX // 2].bitcast(BF16).rearrange("p (k t) -> p k t", k=KC))
    mergedT = _alias(big1, mergedT)
    arena = Arena(P, "arena", cfg.get("arena_words", 17408))
    stage_box = [None]
    rstd = P.sbuf("rstd", [128, TMAX], F32)
    sq = P.sbuf("sq", [128, TMAX], F32)
    tmpA = P.sbuf("tmpA", [128, TMAX], F32)
    SLOT = 16 * 256
    ws = WeightStream(kb, 3, SLOT)

    def plan_ffn(w1, w3, w2, tag=None):
        w1v = w1.h.rearrange("(k p) n -> p k n", p=128)
        w3v = w3.h.rearrange("(k p) n -> p k n", p=128)
        w2v = w2.h.rearrange("(k p) n -> p k n", p=128)
        for j in range(DFF // 256):
            for wv in (w1v, w3v):
                def fn(slot, wv=wv, j=j):
                    dst = slot[:, 0:16 * 256].rearrange("p (k n) -> p k n", k=16)
                    return [(dst, wv[:, :, j * 256:(j + 1) * 256])]
                ws.add(fn, tag)
        for mb in range(D // 256):
            for half in range(4):
                def fn(slot, mb=mb, half=half):
                    dst = slot[:, 0:11 * 256].rearrange("p (k n) -> p k n", k=11)
                    return [(dst, w2v[:, half * 11:(half + 1) * 11, mb * 256:(mb + 1) * 256])]
                ws.add(fn, tag)

    WIN_BLOCKS = [(c0, min(256, 5888 - c0)) for c0 in range(0, 5888, 256)]

    def plan_win_a():
        wv = w_in.h.rearrange("(k p) n -> p k n", p=128)
        for (c0, w) in WIN_BLOCKS:
            def fn(slot, c0=c0, w=w):
                dst = slot[:, 0:16 * w].rearrange("p (k n) -> p k n", k=16)
                return [(dst, wv[:, :, c0:c0 + w])]
            ws.add(fn)

    def plan_merge(tag=None):
        wv = w_in.h.rearrange("(k p) n -> p k n", p=128)
        brv = [w_br_rw.h.rearrange("(k p) n -> p k n", p=128), w_br_nsa.h.rearrange("(k p) n -> p k n", p=128)]
        wov = w_out.h.rearrange("(k p) n -> p k n", p=128)
        for mb in range(D // 256):
            for bi in range(2):
                c0 = (5912 if bi == 0 else 7960) + mb * 256

                def fn(slot, c0=c0):
                    dst = slot[:, 0:16 * 256].rearrange("p (k n) -> p k n", k=16)
                    return [(dst, wv[:, :, c0:c0 + 256])]
                ws.add(fn, tag)

                def fn2(slot, bi=bi, mb=mb):
                    dst = slot[:, 0:8 * 256].rearrange("p (k n) -> p k n", k=8)
                    return [(dst, brv[bi][:, :, mb * 256:(mb + 1) * 256])]
                ws.add(fn2, tag)
        for mb in range(D // 256):
            def fn3(slot, mb=mb):
                dst = slot[:, 0:16 * 256].rearrange("p (k n) -> p k n", k=16)
                return [(dst, wov[:, :, mb * 256:(mb + 1) * 256])]
            ws.add(fn3, tag)

    def plan_cmp():
        for kv_ in range(2):
            for half in range(2):
                def fn(slot, kv_=kv_, half=half):
                    dst = slot[:, 0:16 * 256].rearrange("p (r c) -> p r c", r=16)
                    src = cmp_w1.h[kv_, half * 2048:(half + 1) * 2048, :].rearrange("(r p) c -> p r c", p=128)
                    return [(dst, src)]
                ws.add(fn)

    for ti_, (kind_, _, _) in enumerate(tiles):
        plan_ffn(f1_w1, f1_w3, f1_w2, (ti_, 1))
        plan_win_a()
        if kind_ == "p" and not cfg.get("skip_nsa"):
            plan_cmp()
        plan_merge((ti_, 3))
        plan_ffn(f2_w1, f2_w3, f2_w2, (ti_, 2))

    def rms_stats(src_chunk_ap, src_bufs, T):
        ps = ps_next()
        for k in range(KC):
            kb.act(sq[:, :T], src_chunk_ap(k), AF.Square, src_bufs, (sq,))
            kb.mm(ps[:, :T], ones[:, :], sq[:, :T], k == 0, k == KC - 1, (ones, sq), (ps,))
        kb.ts(sq[:, :T], ps[:, :T], 1.0 / D, EPS, ALU.mult, ALU.add, (ps,), (sq,))
        kb.act(sq[:, :T], sq[:, :T], AF.Sqrt, (sq,), (sq,))
        P.op("dve", lambda e: e.reciprocal(rstd[:, :T], sq[:, :T]), (sq,), (rstd,))

    def pre_norm(gvec, T):
        rms_stats(lambda k: xT[:, k, :T], (xT,), T)
        for k in range(KC):
            kb.stt(hT[:, k, :T], xT[:, k, :T], gvec[:, k:k + 1], rstd[:, :T], ALU.mult, ALU.mult,
                   (xT, gvec, rstd), (hT,))

    def post_residual(fT, gpost, coef, T):
        rms_stats(lambda k: fT[:, k * TMAX:k * TMAX + T], (fT,), T)
        for k in range(KC):
            kb.stt(tmpA[:, :T], fT[:, k * TMAX:k * TMAX + T], gpost[:, k:k + 1], rstd[:, :T], ALU.mult, ALU.mult,
                   (fT, gpost, rstd), (tmpA,))
            kb.stt(xT[:, k, :T], tmpA[:, :T], coef, xT[:, k, :T], ALU.mult, ALU.add, (tmpA, xT), (xT,))

    def ffn(gpre, gpost, T, second=False, tag=None):
        fT = stage_box[0]
        nx = (arena.words - arena.off) // (SLOT // 2)
        if tag is not None and nx > 0 and not cfg.get("no_extra_slots"):
            ws.set_extra([arena.alloc("wsx%d" % i, SLOT // 2, view=lambda a: a.bitcast(BF16)) for i in range(nx)], tag)
        pre_norm(gpre, T)
        for j in range(DFF // 256):
            s1 = ws.take()
            s3 = ws.take()
            v1 = s1[:, 0:16 * 256].rearrange("p (k n) -> p k n", k=16)
            v3 = s3[:, 0:16 * 256].rearrange("p (k n) -> p k n", k=16)
            for mi in range(2):
                p1 = ps_next()
                p3 = ps_next()
                for k in range(KC):
                    kb.mm(p1[:, :T], v1[:, k, mi * 128:(mi + 1) * 128], hT[:, k, :T], k == 0, k == KC - 1, (s1, hT), (p1,))
                for k in range(KC):
                    kb.mm(p3[:, :T], v3[:, k, mi * 128:(mi + 1) * 128], hT[:, k, :T], k == 0, k == KC - 1, (s3, hT), (p3,))
                kb.act(tmpA[:, :T], p1[:, :T], AF.Silu, (p1,), (tmpA,))
                kb.tt(gT[:, j * 2 + mi, :T], tmpA[:, :T], p3[:, :T], ALU.mult, (tmpA, p3), (gT,))
        for mb in range(D // 256):
            pa = ps_next()
            pb = ps_next()
            for half in range(4):
                s2 = ws.take()
                v2 = s2[:, 0:11 * 256].rearrange("p (k n) -> p k n", k=11)
                for mi, pp in ((0, pa), (1, pb)):
                    for k in range(11):
                        kk = half * 11 + k
                        kb.mm(pp[:, :T], v2[:, k, mi * 128:(mi + 1) * 128], gT[:, kk, :T], kk == 0, kk == 43, (s2, gT), (pp,))
            for mi, pp in ((0, pa), (1, pb)):
                m = mb * 2 + mi
                kb.copy(fT[:, m * TMAX:m * TMAX + T], pp[:, :T], (pp,), (fT,), eng="act")
        ws.clear_extra()
        post_residual(fT, gpost, 0.5, T)

    def load_x(kind, t0, T):
        src = xp if kind == "p" else xs
        stage = stage_box[0]
        nsub = (T + 127) // 128
        st = stage[:, 0:nsub * D].rearrange("p (s f) -> p s f", s=nsub)
        rows = min(T, 128)
        if T >= 128:
            kb.dma(st, src.h[t0:t0 + T, :].rearrange("(s p) f -> p s f", p=128), (), (stage,), stage)
        else:
            kb.dma(stage[0:T, 0:D], src.h[t0:t0 + T, :], (), (stage,), stage)
        for k in range(KC):
            ps = ps_next()
            for s in range(nsub):
                kb.tr(ps[:, s * 128:s * 128 + rows], st[0:rows, s, k * 128:(k + 1) * 128], ident[0:rows, 0:rows],
                      (stage, ident), (ps,))
            kb.copy(xT[:, k, :T], ps[:, :T], (ps,), (xT,), eng=("act" if k % 2 else "dve"))

    def store_tokmajor(dst_rows_ap, src_chunk_ap, src_bufs, nchunks, T):
        stage = stage_box[0]
        nsub = (T + 127) // 128
        rows = min(T, 128)
        W = nchunks * 128
        st = stage[:, 0:nsub * W].rearrange("p (s f) -> p s f", s=nsub)
        for s in range(nsub):
            for c0 in range(0, nchunks, 4):
                ps = ps_next()
                nn = min(4, nchunks - c0)
                for c in range(nn):
                    kb.tr(ps[0:rows, c * 128:(c + 1) * 128], src_chunk_ap(c0 + c)[:, s * 128:s * 128 + rows], ident[:, :],
                          src_bufs + (ident,), (ps,))
                kb.copy(st[0:rows, s, c0 * 128:(c0 + nn) * 128], ps[0:rows, 0:nn * 128], (ps,), (stage,),
                        eng=("act" if (c0 // 4) % 2 else "dve"))
        if dst_rows_ap is None:
            return
        if T >= 128:
            kb.dma(dst_rows_ap.rearrange("(s p) f -> p s f", p=128), st, (stage,), (), stage)
        else:
            kb.dma(dst_rows_ap, stage[0:T, 0:W], (stage,), (), stage)

    qT = P.sbuf("qT", [128, 8, TMAX], BF16)
    yrwT = P.sbuf("yrwT", [128, 8, TMAX], BF16)
    ynsaT = P.sbuf("ynsaT", [128, 8, TMAX], BF16)
    carry = [P.sbuf("carry0", [128, 26], F32), P.sbuf("carry1", [128, 26], F32)]
    S0T = P.sbuf("S0T", [64, 16, 64], F32)
    kb.memset(carry[0][:, :], 0.0, (carry[0],))
    kb.memset(S0T[:, :, :], 0.0, (S0T,))
    kb.memset(ynsaT[:, :, :], 0.0, (ynsaT,))
    rwmask4 = P.sbuf("rwmask4_sb", [4, 12], F32)
    kb.dma(rwmask4[:], mask4_d.h[:, :], (), (rwmask4,), rwmask4)

    def win_proj_a(T, kvT):
        for (c0, w) in WIN_BLOCKS:
            s_ = ws.take()
            v = s_[:, 0:16 * w].rearrange("p (k n) -> p k n", k=16)
            for mi in range(w // 128):
                ps = ps_next()
                for k in range(KC):
                    kb.mm(ps[:, :T], v[:, k, mi * 128:(mi + 1) * 128], hT[:, k, :T], k == 0, k == KC - 1, (s_, hT), (ps,))
                ci = c0 // 128 + mi
                eng = "act" if ci % 2 else "dve"
                if ci < 26:
                    kb.copy(pT[:, ci, :T], ps[:, :T], (ps,), (pT,), eng=eng)
                elif ci < 34:
                    kb.copy(qT[:, ci - 26, :T], ps[:, :T], (ps,), (qT,), eng=eng)
                else:
                    kb.copy(kvT[:, ci - 34, :T], ps[:, :T], (ps,), (kvT,), eng=eng)

    def bc3(ap2, n):
        return ap2.unsqueeze(2).to_broadcast([ap2.shape[0], ap2.shape[1], n])

    def bcmid(ap2, n):
        return ap2.unsqueeze(1).to_broadcast([ap2.shape[0], n, ap2.shape[1]])

    def ts1(out_ap, in_ap, scalar, op, reads, writes, eng="dve"):
        return P.op(eng, lambda e: e.tensor_single_scalar(out_ap, in_ap, scalar, op), reads, writes)

    def rwkv(kind, t0, T, tile_idx):
        A = arena
        C = 64 if kind == "p" else 4
        nch = T // C
        nd = 5 if C == 64 else 1
        mk = rwmask if C == 64 else rwmask4
        mkU = mk[0:C, 0:2 * C]
        mkL = mk[0:C, 2 * C:3 * C]

        def fm(nm):
            return A.alloc(nm, 512, view=lambda a: a.rearrange("p (k c) -> p k c", k=8))

        logw = fm("logw"); a_ = fm("a_"); kk_ = fm("kk_")
        Eg = fm("Eg"); Einv = fm("Einv"); BtT = fm("BtT"); KtT = fm("KtT"); tmp = fm("tmp"); tmp2 = fm("tmp2")
        La, Lb = tmp, tmp2
        gate_, bonus_ = kk_, a_
        AR = A.alloc("AR", 1024, view=lambda a: a.rearrange("p (k two c) -> p k two c", k=8, two=2))
        ARH = A.alloc("ARH", 1024, view=lambda a: a.rearrange("p (k two c) -> p k two c", k=8, two=2))
        BtH = fm("BtH"); KtH = fm("KtH")
        gh = A.alloc("gh", 8)
        tw = A.alloc("tw", 64); sgd = A.alloc("sgd", 64)

        def tm(nm, w=64):
            return A.alloc(nm, 8 * w, view=lambda a: a.rearrange("p (h c) -> p h c", h=8))

        Vtok = tm("Vtok"); Bttok = tm("Bttok"); Kttok = tm("Kttok"); XT = tm("XT"); UT = tm("UT")
        Nm = tm("Nm"); NT = tm("NT"); Pa = tm("Pa"); PTa = tm("PTa"); Rm = tm("Rm")
        Yt, Yc = XT, Pa
        mb_off = A.off
        MBm = tm("MBm", 128); MKm = tm("MKm", 128)
        nat16 = _alias(MBm, Buf("nat16v", A.t[:, mb_off:mb_off + 1024].rearrange("p (h c) -> p h c", h=16)))
        st8 = A.alloc("st8", 32, view=lambda a: a.rearrange("p (h c) -> p h c", h=8))
        lvl = cfg.get("rw_stop", 99)

        def fmop(X, XH, h):
            return (X if h % 2 == 0 else XH), h // 2

        cur, nxt = carry[tile_idx % 2], carry[(tile_idx + 1) % 2]
        dtmp = tmpA
        if kind == "p":
            kb.copy(nxt[:, :], pT[:, :, T - 1], (pT,), (nxt,))
            for c in range(26):
                kb.tt(dtmp[:, 1:T], pT[:, c, 0:T - 1], pT[:, c, 1:T], ALU.subtract, (pT,), (dtmp,))
                kb.tt(dtmp[:, 0:1], cur[:, c:c + 1], pT[:, c, 0:1], ALU.subtract, (pT, cur), (dtmp,))
                kb.stt(pT[:, c, :T], dtmp[:, :T], v_mu[:, c:c + 1], pT[:, c, :T], ALU.mult, ALU.add, (dtmp, v_mu, pT), (pT,))
            if t0 + T == SEQ_:
                store_col(shp.h, nxt[:, :], nxt, 26)
        else:
            sh0 = A.alloc("sh0", 26 * 4, view=lambda a: a.rearrange("p (k s) -> p k s", k=26))
            sho = A.alloc("sho", 26 * 4, view=lambda a: a.rearrange("p (s k) -> p s k", s=4))
            for sq_i in range(4):
                load_col(sh0[:, :, sq_i], sh0, st_sh.h[sq_i, :], 26)
            p4 = pT[:, :, 0:16].rearrange("p k (s t) -> p k s t", t=4)
            for sq_i in range(4):
                kb.copy(sho[:, sq_i, :], p4[:, :, sq_i, 3], (pT,), (sho,))
                store_col(shs.h[sq_i, :], sho[:, sq_i, :], sho, 26)
            d4 = dtmp[:, 0:16].rearrange("p (s t) -> p s t", t=4)
            for c in range(26):
                kb.tt(d4[:, :, 1:4], p4[:, c, :, 0:3], p4[:, c, :, 1:4], ALU.subtract, (pT,), (dtmp,))
                kb.tt(d4[:, :, 0], sh0[:, c, :], p4[:, c, :, 0], ALU.subtract, (pT, sh0), (dtmp,))
                kb.stt(pT[:, c, :T], dtmp[:, :T], v_mu[:, c:c + 1], pT[:, c, :T], ALU.mult, ALU.add, (dtmp, v_mu, pT), (pT,))
        if lvl <= 1:
            return

        def heads_T(src, dst, src_bufs, dst_bufs):
            for g in range(2):
                ps = ps_next()
                for q in range(8):
                    kb.tr(ps[0:64, q * 64:(q + 1) * 64], src[0:64, g * 8 + q, :], ident[0:64, 0:64], src_bufs + (ident,), (ps,))
                kb.copy(dst[0:64, g * 8:g * 8 + 8, :], ps[0:64, :].rearrange("p (h c) -> p h c", h=8), (ps,), dst_bufs,
                        eng=("act" if g else "dve"))

        for ci in range(nch):
            cs = slice(ci * C, (ci + 1) * C)
            if kind == "s":
                kb.dma(nat16[0:64, :, :], st_rw.h[ci].rearrange("h i j -> i h j"), (), (nat16,), nat16)
                heads_T(nat16, S0T, (nat16,), (S0T,))
            r_ = pT[:, 0:8, cs]; k_ = pT[:, 8:16, cs]; v_ = pT[:, 16:24, cs]
            f3 = lambda b: b[:, :, 0:C]
            kb.act(tw[:, 0:C], pT[:, 24, cs], AF.Tanh, (pT,), (tw,))
            kb.act(sgd[:, 0:C], pT[:, 25, cs], AF.Sigmoid, (pT,), (sgd,))
            for m in range(8):
                ps = ps_next()
                msl = slice(m * 128, (m + 1) * 128)
                kb.mm(ps[:, 0:C], w2z[:, msl], tw[:, 0:C], True, True, (w2z, tw), (ps,))
                kb.act(logw[:, m, 0:C], ps[:, 0:C], AF.Sigmoid, (ps, v_w0), (logw,), bias=v_w0[:, m:m + 1])
                ps = ps_next()
                kb.mm(ps[:, 0:C], a2z[:, msl], pT[:, 24, cs], True, True, (a2z, pT), (ps,))
                kb.act(a_[:, m, 0:C], ps[:, 0:C], AF.Sigmoid, (ps, v_a0), (a_,), bias=v_a0[:, m:m + 1])
            if lvl <= 2:
                return
            kb.tt(f3(kk_), k_, bc3(v_kk[:, 0:8], C), ALU.mult, (pT, v_kk), (kk_,))
            kb.tt(f3(tmp), f3(kk_), f3(kk_), ALU.mult, (kk_,), (tmp,))
            ps = ps_next()
            for m in range(8):
                kb.mm(ps[:, m * 64:m * 64 + C], bones[:, :], tmp[:, m, 0:C], True, True, (bones, tmp), (ps,))
            psv = ps[:, :].rearrange("p (k c) -> p k c", k=8)[:, :, 0:C]
            kb.act(f3(tmp), psv, AF.Sqrt, (ps,), (tmp,))
            ts1(f3(tmp), f3(tmp), 1e-12, ALU.max, (tmp,), (tmp,))
            P.op("dve", lambda e: e.reciprocal(f3(tmp), f3(tmp)), (tmp,), (tmp,))
            kb.tt(f3(kk_), f3(kk_), f3(tmp), ALU.mult, (kk_, tmp), (kk_,))
            ts1(f3(tmp), f3(a_), 1.0, ALU.subtract, (a_,), (tmp,))
            kb.tt(f3(tmp), f3(tmp), bc3(v_ka[:, 0:8], C), ALU.mult, (tmp, v_ka), (tmp,))
            kb.stt(k_, f3(tmp), 1.0, k_, ALU.add, ALU.mult, (tmp, pT), (pT,))
            kb.tt(f3(a_), f3(kk_), f3(a_), ALU.mult, (kk_, a_), (a_,))
            ts1(f3(logw), f3(logw), -float(np.exp(-0.5)), ALU.mult, (logw,), (logw,))
            src, dst = logw, La
            sh = 1
            while sh < C:
                kb.copy(dst[:, :, 0:sh], src[:, :, 0:sh], (src,), (dst,), eng="act")
                kb.tt(dst[:, :, sh:C], src[:, :, sh:C], src[:, :, 0:C - sh], ALU.add, (src,), (dst,))
                src, dst = dst, (Lb if dst is La else La)
                sh *= 2
            L = src
            assert L is tmp2
            kb.act(f3(Eg), f3(L), AF.Exp, (L,), (Eg,))
            kb.act(f3(Einv), f3(L), AF.Exp, (L,), (Einv,), scale=-1.0)
            kb.tt(f3(tmp), f3(L), f3(logw), ALU.subtract, (L, logw), (tmp,))
            kb.act(f3(tmp), f3(tmp), AF.Exp, (tmp,), (tmp,))
            kb.tt(AR[:, :, 0, 0:C], f3(kk_), f3(tmp), ALU.mult, (kk_, tmp), (AR,))
            kb.tt(AR[:, :, 1, 0:C], r_, f3(Eg), ALU.mult, (pT, Eg), (AR,))
            kb.tt(f3(BtT), f3(a_), f3(Einv), ALU.mult, (a_, Einv), (BtT,))
            kb.tt(f3(KtT), k_, f3(Einv), ALU.mult, (pT, Einv), (KtT,))
            ps = ps_next()
            for m in range(8):
                kb.mm(ps[:, m * 64:m * 64 + C], g2sb[:, m * 128:(m + 1) * 128], sgd[:, 0:C], True, True, (g2sb, sgd), (ps,))
            kb.copy(f3(gate_), ps[:, :].rearrange("p (k c) -> p k c", k=8)[:, :, 0:C], (ps,), (gate_,), eng="act")
            kb.tt(f3(tmp), r_, k_, ALU.mult, (pT,), (tmp,))
            kb.tt(f3(tmp), f3(tmp), bc3(v_rk[:, 0:8], C), ALU.mult, (tmp, v_rk), (tmp,))
            ps = ps_next()
            for m in range(8):
                kb.mm(ps[:, m * 64:m * 64 + C], bones[:, :], tmp[:, m, 0:C], True, True, (bones, tmp), (ps,))
            kb.tt(f3(bonus_), ps[:, :].rearrange("p (k c) -> p k c", k=8)[:, :, 0:C], v_, ALU.mult, (ps, pT), (bonus_,))
            for (srcv, sb, dstv, db) in ((AR[:, :, 0, 0:C], AR, ARH[0:64, :, 0, 0:C], ARH), (AR[:, :, 1, 0:C], AR, ARH[0:64, :, 1, 0:C], ARH),
                                         (f3(BtT), BtT, BtH[0:64, :, 0:C], BtH), (f3(KtT), KtT, KtH[0:64, :, 0:C], KtH)):
                ps = ps_next()
                kb.mm(ps[0:64, 0:8 * C], selhi[:, 0:64], srcv, True, True, (selhi, sb), (ps,))
                kb.copy(dstv, ps[0:64, 0:8 * C].rearrange("p (k c) -> p k c", k=8), (ps,), (db,), eng="act")
            ps = ps_next()
            kb.mm(ps[0:64, 0:8], selhi[:, 0:64], Eg[:, :, C - 1], True, True, (selhi, Eg), (ps,))
            kb.copy(gh[0:64, 0:8], ps[0:64, 0:8], (ps,), (gh,))
            if lvl <= 3:
                return

            for hh in range(2):
                prs = [4 * hh + q for q in range(4)]
                heads = [8 * hh + q for q in range(8)]
                for (srcap, sb, dstb) in ((lambda pr: pT[:, 16 + pr, cs], pT, Vtok), (lambda pr: BtT[:, pr, 0:C], BtT, Bttok),
                                          (lambda pr: KtT[:, pr, 0:C], KtT, Kttok)):
                    ps = ps_next()
                    for q, pr in enumerate(prs):
                        kb.tr(ps[0:C, q * 128:(q + 1) * 128], srcap(pr), ident[:, :], (sb, ident), (ps,))
                    kb.copy(dstb[0:C, :, :], ps[0:C, :].rearrange("p (h c) -> p h c", h=8), (ps,), (dstb,), eng="act")

                def v4(ps_, width, w2):
                    return ps_[0:C, 0:4 * width].rearrange("p (h c) -> p h c", h=4)[:, :, 0:w2]

                for (X_, XH_, Mm) in ((BtT, BtH, MBm), (KtT, KtH, MKm)):
                    for hb in range(2):
                        ps_ = ps_next()
                        for hi in range(4):
                            hl = hb * 4 + hi
                            h = heads[hl]
                            Xs, pr = fmop(X_, XH_, h)
                            As, _ = fmop(AR, ARH, h)
                            kb.mm(ps_[0:C, hi * 128:hi * 128 + 2 * C], Xs[0:64, pr, 0:C], As[0:64, pr, :, 0:C], True, True, (Xs, As), (ps_,))
                        kb.tt(Mm[0:C, hb * 4:hb * 4 + 4, 0:2 * C], v4(ps_, 128, 2 * C), bcmid(mkU, 4), ALU.mult, (ps_, mk), (Mm,))
                if lvl <= 4:
                    return
                for hb in range(2):
                    ps_ = ps_next()
                    for hi in range(4):
                        hl = hb * 4 + hi
                        h = heads[hl]
                        As, pr = fmop(AR, ARH, h)
                        Bs, _ = fmop(BtT, BtH, h)
                        kb.mm(ps_[0:C, hi * 64:hi * 64 + C], As[0:64, pr, 0, 0:C], Bs[0:64, pr, 0:C], True, True, (As, Bs), (ps_,))
                    kb.stt(NT[0:C, hb * 4:hb * 4 + 4, 0:C], v4(ps_, 64, C), -1.0, bcmid(mkL, 4), ALU.mult, ALU.mult, (ps_, mk), (NT,))
                ts1(Nm[0:C, :, 0:C], MBm[0:C, :, 0:C], -1.0, ALU.mult, (MBm,), (Nm,))
                kb.tt(Rm[0:C, :, 0:C], Nm[0:C, :, 0:C], bcmid(ident[0:C, 0:C], 8), ALU.add, (Nm, ident), (Rm,))
                Pc, PTc, Pn, PTn = Nm, NT, Pa, PTa
                for step in range(nd):
                    last = step == nd - 1
                    for hb in range(2):
                        ps_ = ps_next()
                        for hi in range(4):
                            hl = hb * 4 + hi
                            kb.mm(ps_[0:C, hi * 64:hi * 64 + C], Pc[0:C, hl, 0:C], PTc[0:C, hl, 0:C], True, True, (Pc, PTc), (ps_,))
                        kb.copy(PTn[0:C, hb * 4:hb * 4 + 4, 0:C], v4(ps_, 64, C), (ps_,), (PTn,), eng="act")
                    if not last:
                        for hb in range(2):
                            ps_ = ps_next()
                            for hi in range(4):
                                hl = hb * 4 + hi
                                kb.mm(ps_[0:C, hi * 64:hi * 64 + C], PTc[0:C, hl, 0:C], Pc[0:C, hl, 0:C], True, True, (Pc, PTc), (ps_,))
                            kb.copy(Pn[0:C, hb * 4:hb * 4 + 4, 0:C], v4(ps_, 64, C), (ps_,), (Pn,), eng="dve")
                    for hb in range(2):
                        ps_ = ps_next()
                        for hi in range(4):
                            hl = hb * 4 + hi
                            kb.mm(ps_[0:C, hi * 64:hi * 64 + C], PTn[0:C, hl, 0:C], Rm[0:C, hl, 0:C], True, True, (PTn, Rm), (ps_,))
                        kb.tt(Rm[0:C, hb * 4:hb * 4 + 4, 0:C], Rm[0:C, hb * 4:hb * 4 + 4, 0:C], v4(ps_, 64, C), ALU.add, (Rm, ps_), (Rm,))
                    Pc, PTc, Pn, PTn = Pn, PTn, Pc, PTc
                if lvl <= 5:
                    return
                for hb in range(2):
                    ps_ = ps_next()
                    for hi in range(4):
                        hl = hb * 4 + hi
                        h = heads[hl]
                        As, pr = fmop(AR, ARH, h)
                        o = ps_[0:C, hi * 64:hi * 64 + 64]
                        kb.mm(o, As[0:64, pr, 0, 0:C], S0T[0:64, h, :], True, False, (As, S0T), (ps_,))
                        kb.mm(o, MKm[0:C, hl, 0:C], Vtok[0:C, hl, :], False, True, (MKm, Vtok), (ps_,))
                    kb.copy(XT[0:C, hb * 4:hb * 4 + 4, :], v4(ps_, 64, 64), (ps_,), (XT,), eng="act")
                for hb in range(2):
                    ps_ = ps_next()
                    for hi in range(4):
                        hl = hb * 4 + hi
                        kb.mm(ps_[0:C, hi * 64:hi * 64 + 64], Rm[0:C, hl, 0:C], XT[0:C, hl, :], True, True, (Rm, XT), (ps_,))
                    ts1(UT[0:C, hb * 4:hb * 4 + 4, :], v4(ps_, 64, 64), -1.0, ALU.mult, (ps_,), (UT,))
                for hb in range(2):
                    ps_ = ps_next()
                    for hi in range(4):
                        hl = hb * 4 + hi
                        h = heads[hl]
                        As, pr = fmop(AR, ARH, h)
                        o = ps_[0:C, hi * 64:hi * 64 + 64]
                        kb.mm(o, As[0:64, pr, 1, 0:C], S0T[0:64, h, :], True, False, (As, S0T), (ps_,))
                        kb.mm(o, MBm[0:C, hl, C:2 * C], UT[0:C, hl, :], False, False, (MBm, UT), (ps_,))
                        kb.mm(o, MKm[0:C, hl, C:2 * C], Vtok[0:C, hl, :], False, True, (MKm, Vtok), (ps_,))
                    kb.copy(Yt[0:C, hb * 4:hb * 4 + 4, :], v4(ps_, 64, 64), (ps_,), (Yt,), eng="act")
                for hb in range(2):
                    ps_ = ps_next()
                    for hi in range(4):
                        hl = hb * 4 + hi
                        o = ps_[0:64, hi * 64:hi * 64 + 64]
                        kb.mm(o, Bttok[0:C, hl, :], UT[0:C, hl, :], True, False, (Bttok, UT), (ps_,))
                        kb.mm(o, Kttok[0:C, hl, :], Vtok[0:C, hl, :], False, True, (Kttok, Vtok), (ps_,))
                    for hi in range(4):
                        hl = hb * 4 + hi
                        h = heads[hl]; pr = h // 2
                        gC = Eg[0:64, pr, C - 1:C] if h % 2 == 0 else gh[0:64, pr:pr + 1]
                        gb = Eg if h % 2 == 0 else gh
                        ts1(NT[0:64, hl, :], S0T[0:64, h, :], gC, ALU.mult, (S0T, gb), (NT,))
                        kb.stt(S0T[0:64, h, :], ps_[0:64, hi * 64:hi * 64 + 64], gC, NT[0:64, hl, :], ALU.mult, ALU.add, (ps_, gb, NT), (S0T,))
                if lvl <= 6:
                    return
                P.op("dve", lambda e: e.tensor_reduce(st8[0:C, :, 0], Yt[0:C, :, :], AX.X, ALU.add), (Yt,), (st8,))
                ts1(st8[0:C, :, 0], st8[0:C, :, 0], 1.0 / 64, ALU.mult, (st8,), (st8,))
                kb.tt(Yc[0:C, :, :], Yt[0:C, :, :], bc3(st8[0:C, :, 0], 64), ALU.subtract, (Yt, st8), (Yc,))
                kb.tt(Yt[0:C, :, :], Yc[0:C, :, :], Yc[0:C, :, :], ALU.mult, (Yc,), (Yt,))
                P.op("dve", lambda e: e.tensor_reduce(st8[0:C, :, 1], Yt[0:C, :, :], AX.X, ALU.add), (Yt,), (st8,))
                kb.ts(st8[0:C, :, 1], st8[0:C, :, 1], 1.0 / 64, 64e-5, ALU.mult, ALU.add, (st8,), (st8,))
                kb.act(st8[0:C, :, 1], st8[0:C, :, 1], AF.Sqrt, (st8,), (st8,))
                P.op("dve", lambda e: e.reciprocal(st8[0:C, :, 2], st8[0:C, :, 1]), (st8,), (st8,))
                kb.tt(Yc[0:C, :, :], Yc[0:C, :, :], bc3(st8[0:C, :, 2], 64), ALU.mult, (Yc, st8), (Yc,))
                ps_ = ps_next()
                for q, pr in enumerate(prs):
                    kb.tr(ps_[:, q * 64:q * 64 + C], Yc[0:C, q * 2:q * 2 + 2, :], ident[0:C, 0:C], (Yc, ident), (ps_,))
                for q, pr in enumerate(prs):
                    kb.act(tmp2[:, pr, 0:C], ps_[:, q * 64:q * 64 + C], AF.Identity, (ps_, v_lb, v_lw), (tmp2,),
                           bias=v_lb[:, pr:pr + 1], scale=v_lw[:, pr:pr + 1])
            if lvl <= 7:
                return
            kb.tt(f3(tmp2), f3(tmp2), f3(bonus_), ALU.add, (tmp2, bonus_), (tmp2,))
            kb.tt(yrwT[:, :, cs], f3(tmp2), f3(gate_), ALU.mult, (tmp2, gate_), (yrwT,))
            if kind == "s" or (t0 + T == SEQ_ and ci == nch - 1):
                heads_T(S0T, nat16, (S0T,), (nat16,))
                dst = rws.h[ci] if kind == "s" else rwp.h
                kb.dma(dst.rearrange("h i j -> i h j"), nat16[0:64, :, :], (nat16,), (), nat16)

    NEGM = -30000.0
    SCALE = 128 ** -0.5
    kselT = P.sbuf("kselT", [128, 2, SEQ], BF16)
    vsel = P.sbuf("vsel", [128, SEQ // 128, 2, 128], BF16)
    kwinT = P.sbuf("kwinT", [128, 2, 6, 128], BF16)
    vwin = P.sbuf("vwin", [128, 6, 2, 128], BF16)
    kcT = P.sbuf("kcT", [128, 2, 64], BF16)
    vc_all = P.sbuf("vc_all", [64, 2, 128], BF16)
    wgate = P.sbuf("wgate", [128, KC, 24], BF16)
    kb.dma(wgate[:, :, :], w_in.h.rearrange("(k p) n -> p k n", p=128)[:, :, 5888:5912], (), (wgate,), wgate, eng="pool")
    cw2 = P.sbuf("cw2", [128, 2, 2, 128], BF16)
    for kv_ in range(2):
        kb.dma(cw2[:, kv_, :, :], cmp_w2.h[kv_].rearrange("(k p) d -> p k d", p=128), (), (cw2,), cw2, eng="pool")
    peT = P.sbuf("peT", [128, 64], F32)
    mc_sb = P.sbuf("mc_sb", [128, 67], F32)
    kb.dma(mc_sb[:], mc_d.h[:, :], (), (mc_sb,), mc_sb)
    bc_sb = P.sbuf("bc_sb", [128, 16, 32], F32)
    kb.dma(bc_sb[:], bc_d.h[:, :, :], (), (bc_sb,), bc_sb)
    bs_sb = P.sbuf("bs_sb", [4, 136], F32)
    kb.dma(bs_sb[:], bs_d.h[:, :], (), (bs_sb,), bs_sb)
    iotap = P.sbuf("iotap_sb", [128, 1], F32)
    kb.dma(iotap[:], iotap_d.h[:, :], (), (iotap,), iotap)
    relb = P.sbuf("relb33", [33, 8], F32)
    kb.memset(relb[:, :], 1.0, (relb,))
    kb.dma(relb[0:32, :], rel_bias.h[:, :], (relb,), (relb,), relb)
    Big = P.dram("big_sel", [8, 128, 2048], F32, "Internal")
    BigW = P.dram("big_win", [8, 128, 640], F32, "Internal")
    BigC = P.dram("big_cmp", [8, 128 * 67], F32, "Internal")

    def nsa_setup():
        A = arena
        A.reset()
        pes = A.alloc("pes", 128)
        kb.dma(pes[0:64, 0:128], cmp_pe.h.rearrange("r k d -> (r k) d"), (), (pes,), pes)
        ps = ps_next()
        kb.tr(ps[:, 0:64], pes[0:64, 0:128], ident[0:64, 0:64], (pes, ident), (ps,))
        kb.copy(peT[:, :], ps[:, 0:64], (ps,), (peT,))
        for (oh_d, ncol, bigd, wrow) in ((ohs_d, 2176, Big, 2048), (ohw_d, 768, BigW, 640)):
            oh = A.alloc("oh", ncol)
            tr_ = A.alloc("trev", ncol)
            kb.dma(oh[0:33, 0:ncol], oh_d.h[:, :], (), (oh,), oh)
            for c0 in range(0, ncol, 512):
                w_ = min(512, ncol - c0)
                ps = ps_next()
                kb.mm(ps[0:8, 0:w_], relb[0:33, 0:8], oh[0:33, c0:c0 + w_], True, True, (relb, oh), (ps,))
                kb.copy(tr_[0:8, c0:c0 + w_], ps[0:8, 0:w_], (ps,), (tr_,))
            for q in range(128):
                kb.dma(bigd.h[:, q, :], tr_[0:8, 127 - q:127 - q + wrow], (tr_,), (bigd,), tr_)
        ncol = 128 * 67
        for c0 in range(0, ncol, 512):
            w_ = min(512, ncol - c0)
            oh = A.alloc("ohc", 512)
            tc_ = A.alloc("trc", 512)
            kb.dma(oh[0:33, 0:w_], ohc_d.h[:, c0:c0 + w_], (), (oh,), oh)
            ps = ps_next()
            kb.mm(ps[0:8, 0:w_], relb[0:33, 0:8], oh[0:33, 0:w_], True, True, (relb, oh), (ps,))
            kb.copy(tc_[0:8, 0:w_], ps[0:8, 0:w_], (ps,), (tc_,))
            kb.dma(BigC.h[:, c0:c0 + w_], tc_[0:8, 0:w_], (tc_,), (BigC,), tc_)
            if A.off + 1024 > A.words:
                A.reset()

    def psbf(ps):
        return ps.h.bitcast(BF16)

    def nsa_cache_update(t0, T, ti, kvT):
        A = arena
        nsub = T // 128
        for g in range(2):
            kb.copy(kselT[:, g, t0:t0 + T], kvT[:, 4 + g, :T], (kvT,), (kselT,), eng="act")
            for s_ in range(nsub):
                kt = t0 // 128 + s_
                kb.copy(kwinT[:, g, kt % 6, :], kvT[:, 8 + g, s_ * 128:(s_ + 1) * 128], (kvT,), (kwinT,), eng="act")
        for s_ in range(nsub):
            kt = t0 // 128 + s_
            ps = ps_next()
            for j, ch in enumerate((6, 7, 10, 11)):
                kb.tr(ps[:, j * 128:(j + 1) * 128], kvT[:, ch, s_ * 128:(s_ + 1) * 128], ident[:, :], (kvT, ident), (ps,))
            kb.copy(vsel[:, kt, :, :], ps[:, 0:256].rearrange("p (g d) -> p g d", g=2), (ps,), (vsel,))
            kb.copy(vwin[:, kt % 6, :, :], ps[:, 256:512].rearrange("p (g d) -> p g d", g=2), (ps,), (vwin,))
        nb = T // 32
        Xb = A.alloc("Xb", 4 * T // 2, view=lambda a: a.bitcast(BF16).rearrange("p (c n r) -> p c n r", c=4, r=32))
        hid = A.alloc("hid", 64, view=lambda a: a.rearrange("p (c n) -> p c n", c=2))
        hx = A.alloc("hx", 64, view=lambda a: a.rearrange("p (c n) -> p c n", c=2))
        hb = A.alloc("hb", 32, view=lambda a: a.bitcast(BF16).rearrange("p (c n) -> p c n", c=2))
        vct = A.alloc("vct", 128, view=lambda a: a.bitcast(BF16))
        pe3 = peT[:, :].rearrange("p (r k) -> p k r", k=2)
        for c in range(4):
            kv_ = c // 2
            kb.tt(Xb[:, c, :, :], kvT[:, c, :T].rearrange("p (n r) -> p n r", r=32), bcmid(pe3[:, kv_, :], nb), ALU.add,
                  (kvT, peT), (Xb,))
        for kv_ in range(2):
            slots = [ws.take(), ws.take()]
            wv = [s_[:, 0:16 * 256].rearrange("p (r c) -> p r c", r=16) for s_ in slots]
            for g in range(2):
                c = kv_ * 2 + g
                for cc in range(2):
                    ps = ps_next()
                    for r in range(32):
                        kb.mm(ps[:, 0:nb], wv[r // 16][:, r % 16, cc * 128:(cc + 1) * 128], Xb[:, c, :, r], r == 0, r == 31,
                              (slots[r // 16], Xb), (ps,))
                    kb.copy(hid[:, cc, 0:nb], ps[:, 0:nb], (ps,), (hid,))
                kb.tt(hx[:, :, 0:nb], hid[:, :, 0:nb], hid[:, :, 0:nb], ALU.mult, (hid,), (hx,))
                kb.ts(hx[:, :, 0:nb], hx[:, :, 0:nb], 0.044715, 1.0, ALU.mult, ALU.add, (hx,), (hx,))
                kb.tt(hx[:, :, 0:nb], hx[:, :, 0:nb], hid[:, :, 0:nb], ALU.mult, (hx, hid), (hx,))
                kb.act(hx[:, :, 0:nb], hx[:, :, 0:nb], AF.Sigmoid, (hx,), (hx,), scale=1.5957691216057308)
                kb.tt(hb[:, :, 0:nb], hx[:, :, 0:nb], hid[:, :, 0:nb], ALU.mult, (hx, hid), (hb,))
                ps = ps_next()
                if kv_ == 0:
                    for cc in range(2):
                        kb.mm(ps[:, 0:nb], cw2[:, 0, cc, :], hb[:, cc, 0:nb], cc == 0, cc == 1, (cw2, hb), (ps,))
                    kb.copy(kcT[:, g, ti * nb:(ti + 1) * nb], ps[:, 0:nb], (ps,), (kcT,))
                else:
                    for cc in range(2):
                        kb.mm(ps[0:nb, 0:128], hb[:, cc, 0:nb], cw2[:, 1, cc, :], cc == 0, cc == 1, (cw2, hb), (ps,))
                    kb.copy(vct[0:nb, g * 128:(g + 1) * 128], ps[0:nb, 0:128], (ps,), (vct,))
        kb.dma(vc_all[ti * nb:(ti + 1) * nb, :, :], vct[0:nb, 0:256].rearrange("p (g d) -> p g d", g=2), (vct,), (vc_all,), vct)

    def nsa_prompt(t0, T, ti):
        A = arena
        S = A.alloc("S", 2048)
        Bt = [A.alloc("bias0", 2048), A.alloc("bias1", 2048)]
        E = A.alloc("E", 1024, view=lambda a: a.bitcast(BF16))
        PTb = A.alloc("PTb", 1024, view=lambda a: a.bitcast(BF16).rearrange("p (k q) -> p k q", k=16))
        acc = A.alloc("acc", 1024)
        ob = A.alloc("ob", 128)
        pc = A.alloc("pc", 64); pcb = A.alloc("pcb", 32, view=lambda a: a.bitcast(BF16))
        imp = A.alloc("imp", 64); imp2 = A.alloc("imp2", 32); impw = A.alloc("impw", 32); selm = A.alloc("selm", 32)
        mx8 = A.alloc("mx8", 16)
        st = A.alloc("st", 8)
        gsig = A.alloc("gsig", 24)
        bi = [0]

        def softmax_pv(h, g, W, ktiles, kT_ap_fn, v_ap_fn, bias_src_ap, gate_col, first, mask_blocks):
            bt = Bt[bi[0] % 2]
            bi[0] += 1
            kb.dma(bt[:, 0:W], bias_src_ap, (Big, BigW), (bt,), bt)
            qap = qT[:, h, qs]
            c0 = 0
            while c0 < W:
                w_ = min(512, W - c0)
                ps = ps_next()
                for (kc0, kw, rhs_ap, rb) in kT_ap_fn(c0, w_):
                    kb.mm(ps[:, kc0 - c0:kc0 - c0 + kw], qap, rhs_ap, True, True, (qT, rb), (ps,))
                kb.stt(S[:, c0:c0 + w_], ps[:, 0:w_], SCALE, bt[:, c0:c0 + w_], ALU.mult, ALU.add, (ps, bt), (S,))
                c0 += w_
            kb.reduce(st[:, 0:1], S[:, 0:W], ALU.max, (S,), (st,))
            ts1(st[:, 1:2], st[:, 0:1], -1.0, ALU.mult, (st,), (st,))
            if mask_blocks is None:
                kb.act(E[:, 0:W], S[:, 0:W], AF.Exp, (S, st), (E,), bias=st[:, 1:2])
            else:
                kb.act(S[:, 0:W], S[:, 0:W], AF.Exp, (S, st), (S,), bias=st[:, 1:2])
                nb_ = W // 64
                kb.tt(E[:, 0:W].rearrange("p (b c) -> p b c", c=64), S[:, 0:W].rearrange("p (b c) -> p b c", c=64),
                      bc3(mask_blocks[:, 0:nb_], 64), ALU.mult, (S, mask_blocks), (E,))
            kb.reduce(st[:, 2:3], E[:, 0:W], ALU.add, (E,), (st,))
            kb.recip(st[:, 3:4], st[:, 2:3], (st,), (st,))
            kb.tt(st[:, 3:4], st[:, 3:4], gate_col, ALU.mult, (st, gsig), (st,))
            nk = len(ktiles)
            for k0 in range(0, nk, 8):
                ps = ps_next()
                pv = psbf(ps)
                kk_ = min(8, nk - k0)
                for j in range(kk_):
                    (cc0, cw, _) = ktiles[k0 + j]
                    kb.tr(pv[0:cw, j * 128:(j + 1) * 128], E[:, cc0:cc0 + cw], identb[:, :], (E, identb), (ps,))
                kb.copy(PTb[:, k0:k0 + kk_, :], pv[:, 0:kk_ * 128].rearrange("p (k q) -> p k q", k=kk_), (ps,), (PTb,),
                        eng=("act" if (k0 // 8) % 2 else "dve"))
            ps = ps_next()
            for j, (cc0, cw, kt) in enumerate(ktiles):
                vap, vb = v_ap_fn(kt, cw)
                kb.mm(ps[:, 0:128], PTb[0:cw, j, :], vap, j == 0, j == nk - 1, (PTb, vb), (ps,))
            hsl = slice(h * 128, (h + 1) * 128)
            if first:
                ts1(acc[:, hsl], ps[:, 0:128], st[:, 3:4], ALU.mult, (ps, st), (acc,))
            else:
                kb.stt(acc[:, hsl], ps[:, 0:128], st[:, 3:4], acc[:, hsl], ALU.mult, ALU.add, (ps, st, acc), (acc,))

        for qb in range(T // 128):
            i = t0 // 128 + qb
            q0 = i * 128
            qs = slice(qb * 128, (qb + 1) * 128)
            ps = ps_next()
            for k in range(KC):
                kb.mm(ps[:, 0:24], hT[:, k, qs], wgate[:, k, :], k == 0, k == KC - 1, (hT, wgate), (ps,))
            kb.act(gsig[:, 0:24], ps[:, 0:24], AF.Sigmoid, (ps,), (gsig,))
            Wc = 4 * i + 4
            for g in range(2):
                for hh in range(4):
                    h = g * 4 + hh
                    bt = Bt[bi[0] % 2]
                    bi[0] += 1
                    kb.dma(bt[:, 0:Wc], BigC.h[h].rearrange("(q j) -> q j", j=67)[:, 63 - 4 * i:67], (BigC,), (bt,), bt)
                    ps = ps_next()
                    kb.mm(ps[:, 0:Wc], qT[:, h, qs], kcT[:, g, 0:Wc], True, True, (qT, kcT), (ps,))
                    kb.stt(S[:, 0:Wc], ps[:, 0:Wc], SCALE, bt[:, 0:Wc], ALU.mult, ALU.add, (ps, bt), (S,))
                    kb.reduce(st[:, 0:1], S[:, 0:Wc], ALU.max, (S,), (st,))
                    ts1(st[:, 1:2], st[:, 0:1], -1.0, ALU.mult, (st,), (st,))
                    kb.act(pc[:, 0:Wc], S[:, 0:Wc], AF.Exp, (S, st), (pc,), bias=st[:, 1:2])
                    kb.tt(pc[:, 0:Wc], pc[:, 0:Wc], mc_sb[:, 63 - 4 * i:67], ALU.mult, (pc, mc_sb), (pc,))
                    kb.reduce(st[:, 2:3], pc[:, 0:Wc], ALU.add, (pc,), (st,))
                    ts1(st[:, 2:3], st[:, 2:3], 1e-30, ALU.max, (st,), (st,))
                    kb.recip(st[:, 3:4], st[:, 2:3], (st,), (st,))
                    ts1(pc[:, 0:Wc], pc[:, 0:Wc], st[:, 3:4], ALU.mult, (pc, st), (pc,))
                    if hh == 0:
                        kb.copy(imp[:, 0:Wc], pc[:, 0:Wc], (pc,), (imp,))
                    else:
                        kb.tt(imp[:, 0:Wc], imp[:, 0:Wc], pc[:, 0:Wc], ALU.add, (imp, pc), (imp,))
                    kb.copy(pcb[:, 0:Wc], pc[:, 0:Wc], (pc,), (pcb,))
                    ps = ps_next()
                    pv = psbf(ps)
                    kb.tr(pv[0:Wc, 0:128], pcb[:, 0:Wc], identb[:, :], (pcb, identb), (ps,))
                    kb.copy(PTb[0:Wc, 0, :], pv[0:Wc, 0:128], (ps,), (PTb,))
                    ps = ps_next()
                    kb.mm(ps[:, 0:128], PTb[0:Wc, 0, :], vc_all[0:Wc, g, :], True, True, (PTb, vc_all), (ps,))
                    ts1(acc[:, h * 128:(h + 1) * 128], ps[:, 0:128], gsig[:, h:h + 1], ALU.mult, (ps, gsig), (acc,))
                nsb = 2 * i + 2
                mask_blocks = None
                if nsb > 16:
                    kb.reduce(imp2[:, 0:nsb], imp[:, 0:Wc].rearrange("p (b two) -> p b two", two=2), ALU.add, (imp,), (imp2,))
                    if nsb < 32:
                        kb.memset(imp2[:, nsb:32], -1e30, (imp2,))
                    kb.tt(imp2[:, 0:nsb], imp2[:, 0:nsb], bc_sb[:, i, 0:nsb], ALU.add, (imp2, bc_sb), (imp2,))
                    P.op("dve", lambda e: e.max(mx8[:, 0:8], imp2[:, 0:32]), (imp2,), (mx8,))
                    P.op("dve", lambda e: e.match_replace(impw[:, 0:32], mx8[:, 0:8], imp2[:, 0:32], -3e38), (mx8, imp2), (impw,))
                    P.op("dve", lambda e: e.max(mx8[:, 8:16], impw[:, 0:32]), (impw,), (mx8,))
                    kb.reduce(st[:, 4:5], mx8[:, 8:16], ALU.min, (mx8,), (st,))
                    ts1(selm[:, 0:32], imp2[:, 0:32], st[:, 4:5], ALU.is_ge, (imp2, st), (selm,))
                    mask_blocks = selm
                for hh in range(4):
                    h = g * 4 + hh
                    Ws = q0 + 128
                    ktl = [(kt * 128, 128, kt) for kt in range(i + 1)]
                    softmax_pv(h, g, Ws, ktl,
                               lambda c0, w_: [(c0, w_, kselT[:, g, c0:c0 + w_], kselT)],
                               lambda kt, cw: (vsel[0:cw, kt, g, :], vsel),
                               Big.h[h, :, 1920 - q0:1920 - q0 + Ws], gsig[:, 8 + h:9 + h], False, mask_blocks)
                    kt0 = max(0, i - 4)
                    ktw = list(range(kt0, i + 1))
                    Ww = 128 * len(ktw)
                    ktlw = [(j * 128, 128, kt) for j, kt in enumerate(ktw)]
                    softmax_pv(h, g, Ww, ktlw,
                               lambda c0, w_: [(c0 + jj * 128, 128, kwinT[:, g, ktw[(c0 // 128) + jj] % 6, :], kwinT)
                                               for jj in range(w_ // 128)],
                               lambda kt, cw: (vwin[0:cw, kt % 6, g, :], vwin),
                               BigW.h[h, :, 640 - Ww:640], gsig[:, 16 + h:17 + h], False, None)
            for h0 in range(0, 8, 4):
                ps = ps_next()
                for j in range(4):
                    kb.tr(ps[:, j * 128:(j + 1) * 128], acc[:, (h0 + j) * 128:(h0 + j + 1) * 128], ident[:, :], (acc, ident), (ps,))
                kb.copy(ynsaT[:, h0:h0 + 4, qs], ps[:, :].rearrange("p (h q) -> p h q", h=4), (ps,), (ynsaT,), eng=("act" if h0 else "dve"))

    PAST = 8192
    newK = P.sbuf("newK", [128, 2, 2, 16], BF16)
    newVT = P.sbuf("newVT", [128, 2, 2, 16], BF16)
    smp = {}
    TSs = P.dram("tab_s_sel", [8, 8704], F32, "Internal")
    TWs = P.dram("tab_s_win", [8, 1024], F32, "Internal")
    TCs = P.dram("tab_s_cmp", [8, 1024], F32, "Internal")
    ckv_rows = ckv.h.rearrange("n p c -> (n p) c")

    def nsa_setup_sample():
        A = arena
        A.reset()
        for (oh_d, ncol, dst) in ((ohss_d, 8704, TSs), (ohsw_d, 1024, TWs), (ohsc_d, 1024, TCs)):
            for c0 in range(0, ncol, 512):
                oh = A.alloc("ohc", 512)
                tc_ = A.alloc("trc", 512)
                kb.dma(oh[0:33, 0:512], oh_d.h[:, c0:c0 + 512], (), (oh,), oh)
                ps = ps_next()
                kb.mm(ps[0:8, 0:512], relb[0:33, 0:8], oh[0:33, 0:512], True, True, (relb, oh), (ps,))
                kb.copy(tc_[0:8, 0:512], ps[0:8, 0:512], (ps,), (tc_,))
                kb.dma(dst.h[:, c0:c0 + 512], tc_[0:8, 0:512], (tc_,), (dst,), tc_)
                if A.off + 1024 > A.words:
                    A.reset()

    def nsa_sample_newrows(kvT):
        for w_, (kc, vc_) in enumerate(((4, 6), (8, 10))):
            for g in range(2):
                kb.copy(newK[:, w_, g, :], kvT[:, kc + g, 0:16], (kvT,), (newK,))
                kb.copy(newVT[:, w_, g, :], kvT[:, vc_ + g, 0:16], (kvT,), (newVT,))

    def page_index_bufs(A):
        return (A.alloc("pti", 64, view=lambda a: a.bitcast(I32)), A.alloc("ptf", 64),
                A.alloc("idx", 64, view=lambda a: a.bitcast(I32)))

    def page_index(A, s_, bufs):
        pti, ptf, idx = bufs
        kb.dma(pti[:, :], ptab.h[s_:s_ + 1, :].to_broadcast([128, 64]), (), (pti,), pti)
        kb.copy(ptf[:, :], pti[:, :], (pti,), (ptf,))
        kb.ts(ptf[:, :], ptf[:, :], 128.0, iotap[:, 0:1], ALU.mult, ALU.add, (ptf, iotap), (ptf,))
        kb.copy(idx[:, :], ptf[:, :], (ptf,), (idx,))
        return idx

    def gather(dst_buf, dst_ap, src_ap, idx, j, eoff=0):
        def fn(e):
            return e.indirect_dma_start(out=dst_ap, out_offset=None, in_=src_ap,
                                        in_offset=bass.IndirectOffsetOnAxis(ap=idx[:, j:j + 1], axis=0), element_offset=eoff)
        return P.op("pool", fn, (idx,), (dst_buf,), dma=True, sem_buf=dst_buf)

    def nsa_sample_compress():
        A = arena
        kcTs = A.alloc("kcTs", 1024, view=lambda a: a.bitcast(BF16).rearrange("p (s g n) -> p s g n", s=4, g=2))
        vcs = A.alloc("vcs", 1024, view=lambda a: a.bitcast(BF16).rearrange("p (s c g d) -> p s c g d", s=4, c=2, g=2))
        smp["kcTs"], smp["vcs"], smp["mark"] = kcTs, vcs, A.off
        W1b = A.alloc("W1b", 8192, view=lambda a: a.bitcast(BF16).rearrange("p (k r c) -> p k r c", k=2, r=32))
        for kv_ in range(2):
            for half in range(2):
                kb.dma(W1b[:, kv_, half * 16:(half + 1) * 16, :],
                       cmp_w1.h[kv_, half * 2048:(half + 1) * 2048, :].rearrange("(r p) c -> p r c", p=128), (), (W1b,), W1b, eng="pool")
        pgs = [A.alloc("pgA", 1024), A.alloc("pgB", 1024)]
        Xb = A.alloc("Xb8", 2048, view=lambda a: a.bitcast(BF16).rearrange("p (c n r) -> p c n r", c=4, r=32))
        hid = A.alloc("hid", 64, view=lambda a: a.rearrange("p (c n) -> p c n", c=2))
        hx = A.alloc("hx", 64, view=lambda a: a.rearrange("p (c n) -> p c n", c=2))
        hb = A.alloc("hb", 32, view=lambda a: a.bitcast(BF16).rearrange("p (c n) -> p c n", c=2))
        vct = A.alloc("vct", 128, view=lambda a: a.bitcast(BF16))
        pe3 = peT[:, :].rearrange("p (r k) -> p k r", k=2)
        nb = 32
        pib = page_index_bufs(A)
        for s_ in range(4):
            idx = page_index(A, s_, pib)
            for grp in range(8):
                for pj in range(8):
                    j = grp * 8 + pj
                    pg = pgs[j % 2]
                    gather(pg, pg[:, 0:1024], ckv_rows[:, :], idx, j)
                    ps = ps_next()
                    for c in range(4):
                        kb.tr(ps[:, c * 128:(c + 1) * 128], pg[:, c * 128:(c + 1) * 128], ident[:, :], (pg, ident), (ps,))
                    for c in range(4):
                        kb.tt(Xb[:, c, pj * 4:(pj + 1) * 4, :], ps[:, c * 128:(c + 1) * 128].rearrange("p (n r) -> p n r", r=32),
                              bcmid(pe3[:, c // 2, :], 4), ALU.add, (ps, peT), (Xb,), eng=("dve"))
                for kv_ in range(2):
                    for g in range(2):
                        c = kv_ * 2 + g
                        for cc in range(2):
                            ps = ps_next()
                            for r in range(32):
                                kb.mm(ps[:, 0:nb], W1b[:, kv_, r, cc * 128:(cc + 1) * 128], Xb[:, c, :, r], r == 0, r == 31, (W1b, Xb), (ps,))
                            kb.copy(hid[:, cc, 0:nb], ps[:, 0:nb], (ps,), (hid,), eng="act")
                        kb.tt(hx[:, :, 0:nb], hid[:, :, 0:nb], hid[:, :, 0:nb], ALU.mult, (hid,), (hx,))
                        kb.ts(hx[:, :, 0:nb], hx[:, :, 0:nb], 0.044715, 1.0, ALU.mult, ALU.add, (hx,), (hx,))
                        kb.tt(hx[:, :, 0:nb], hx[:, :, 0:nb], hid[:, :, 0:nb], ALU.mult, (hx, hid), (hx,))
                        kb.act(hx[:, :, 0:nb], hx[:, :, 0:nb], AF.Sigmoid, (hx,), (hx,), scale=1.5957691216057308)
                        kb.tt(hb[:, :, 0:nb], hx[:, :, 0:nb], hid[:, :, 0:nb], ALU.mult, (hx, hid), (hb,))
                        ps = ps_next()
                        if kv_ == 0:
                            for cc in range(2):
                                kb.mm(ps[:, 0:nb], cw2[:, 0, cc, :], hb[:, cc, 0:nb], cc == 0, cc == 1, (cw2, hb), (ps,))
                            kb.copy(kcTs[:, s_, g, grp * nb:(grp + 1) * nb], ps[:, 0:nb], (ps,), (kcTs,))
                        else:
                            for cc in range(2):
                                kb.mm(ps[0:nb, 0:128], hb[:, cc, 0:nb], cw2[:, 1, cc, :], cc == 0, cc == 1, (cw2, hb), (ps,))
                            kb.copy(vct[0:nb, g * 128:(g + 1) * 128], ps[0:nb, 0:128], (ps,), (vct,))
                po = (grp % 4) * nb
                kb.dma(vcs[po:po + nb, s_, grp // 4, :, :], vct[0:nb, 0:256].rearrange("p (g d) -> p g d", g=2), (vct,), (vcs,), vct)

    def nsa_sample_attend():
        A = arena
        kcTs, vcs = smp["kcTs"], smp["vcs"]
        KTs = A.alloc("KTs", 4104, view=lambda a: a.bitcast(BF16))
        Vs = A.alloc("Vs", 65 * 64, view=lambda a: a.bitcast(BF16).rearrange("p (k d) -> p k d", d=128))
        KW = A.alloc("KW", 264, view=lambda a: a.bitcast(BF16))
        VW = A.alloc("VW", 5 * 64, view=lambda a: a.bitcast(BF16).rearrange("p (k d) -> p k d", d=128))
        pg1 = A.alloc("pgA", 1024)
        pg2 = A.alloc("pgB", 1024)
        pgs = [pg1, pg2]
        wst = pg1
        S = A.alloc("S4", 1024)
        bt = A.alloc("bias4", 520)
        E = A.alloc("E4", 512, view=lambda a: a.bitcast(BF16))
        PTb = A.alloc("PTb4", 32, view=lambda a: a.bitcast(BF16).rearrange("p (k q) -> p k q", q=4))
        acc = A.alloc("acc4", 1024)
        pc = A.alloc("pc4", 256); pcb = A.alloc("pcb4", 128, view=lambda a: a.bitcast(BF16))
        imp = A.alloc("imp4", 256); imp2 = A.alloc("imp24", 136); impw = A.alloc("impw4", 136); selm = A.alloc("selm4", 136)
        mx8 = A.alloc("mx84", 16)
        st = A.alloc("st4", 16)
        gsig = A.alloc("gsig4", 24)
        ps_mod[0] = 7
        pacc = psb[7]

        def attend(h, q_ap, groups, gate_col, first, hsl):
            multi = len(groups) > 1

            def load_bias(gi):
                (W, kT_ap, kT_buf, bias_fn, mask_ap, vt) = groups[gi]
                for t in range(4):
                    kb.dma(bt[t:t + 1, 0:W], bias_fn(t), (TSs, TWs, TCs), (bt,), bt)

            def scores(gi):
                (W, kT_ap, kT_buf, bias_fn, mask_ap, vt) = groups[gi]
                if bias_fn is not None:
                    load_bias(gi)
                c0 = 0
                while c0 < W:
                    w_ = min(512, W - c0)
                    ps = ps_next()
                    kb.mm(ps[0:4, 0:w_], q_ap, kT_ap[:, c0:c0 + w_], True, True, (qT, kT_buf), (ps,))
                    if bias_fn is not None:
                        kb.stt(S[0:4, c0:c0 + w_], ps[0:4, 0:w_], SCALE, bt[0:4, c0:c0 + w_], ALU.mult, ALU.add, (ps, bt), (S,))
                    else:
                        kb.ts(S[0:4, c0:c0 + w_], ps[0:4, 0:w_], SCALE, crow[0:4, h:h + 1], ALU.mult, ALU.add, (ps, crow), (S,))
                    c0 += w_
                return W

            if not multi:
                W = scores(0)
                kb.reduce(st[0:4, 0:1], S[0:4, 0:W], ALU.max, (S,), (st,))
            else:
                kb.copy(st[0:4, 7:8], crow[0:4, h:h + 1], (crow,), (st,))
                for gi in range(len(groups)):
                    (W, kT_ap, kT_buf, bias_fn, mask_ap, vt) = groups[gi]
                    if bias_fn is not None:
                        load_bias(gi)
                        kb.reduce(st[0:4, 5:6], bt[0:4, 0:W], ALU.max, (bt,), (st,))
                        kb.tt(st[0:4, 7:8], st[0:4, 7:8], st[0:4, 5:6], ALU.max, (st,), (st,))
                    c0 = 0
                    while c0 < W:
                        w_ = min(512, W - c0)
                        ps = ps_next()
                        kb.mm(ps[0:4, 0:w_], q_ap, kT_ap[:, c0:c0 + w_], True, True, (qT, kT_buf), (ps,))
                        if gi == 0 and c0 == 0:
                            kb.reduce(st[0:4, 0:1], ps[0:4, 0:w_], ALU.max, (ps,), (st,))
                        else:
                            kb.reduce(st[0:4, 5:6], ps[0:4, 0:w_], ALU.max, (ps,), (st,))
                            kb.tt(st[0:4, 0:1], st[0:4, 0:1], st[0:4, 5:6], ALU.max, (st,), (st,))
                        c0 += w_
                kb.stt(st[0:4, 0:1], st[0:4, 0:1], SCALE, st[0:4, 7:8], ALU.mult, ALU.add, (st,), (st,))
            ts1(st[0:4, 1:2], st[0:4, 0:1], -1.0, ALU.mult, (st,), (st,))
            nmm = sum(len(gp[5]) for gp in groups)
            imm = 0
            for gi in range(len(groups)):
                (W, kT_ap, kT_buf, bias_fn, mask_ap, vt) = groups[gi]
                if multi:
                    scores(gi)
                if mask_ap is None:
                    kb.act(E[0:4, 0:W], S[0:4, 0:W], AF.Exp, (S, st), (E,), bias=st[0:4, 1:2])
                else:
                    kb.act(S[0:4, 0:W], S[0:4, 0:W], AF.Exp, (S, st), (S,), bias=st[0:4, 1:2])
                    if W >= 128:
                        kb.tt(E[0:4, 0:W].rearrange("p (b c) -> p b c", c=64), S[0:4, 0:W].rearrange("p (b c) -> p b c", c=64),
                              mask_ap, ALU.mult, (S, selm), (E,))
                    else:
                        kb.tt(E[0:4, 0:W], S[0:4, 0:W], mask_ap, ALU.mult, (S, selm), (E,))
                if gi == 0:
                    kb.reduce(st[0:4, 2:3], E[0:4, 0:W], ALU.add, (E,), (st,))
                else:
                    kb.reduce(st[0:4, 5:6], E[0:4, 0:W], ALU.add, (E,), (st,))
                    kb.tt(st[0:4, 2:3], st[0:4, 2:3], st[0:4, 5:6], ALU.add, (st,), (st,))
                for k0 in range(0, len(vt), 16):
                    ps = ps_next()
                    pv = psbf(ps)
                    kk_ = min(16, len(vt) - k0)
                    for j in range(kk_):
                        (cc0, cw, _, _) = vt[k0 + j]
                        kb.tr(pv[0:cw, j * 4:(j + 1) * 4], E[0:4, cc0:cc0 + cw], identb[0:4, 0:4], (E, identb), (ps,))
                    kb.copy(PTb[:, 0:kk_, :], pv[:, 0:kk_ * 4].rearrange("p (k q) -> p k q", q=4), (ps,), (PTb,), eng="act")
                    for j in range(kk_):
                        (cc0, cw, v_ap, v_buf) = vt[k0 + j]
                        kb.mm(pacc[0:4, 0:128], PTb[0:cw, j, :], v_ap, imm == 0, imm == nmm - 1, (PTb, v_buf), (pacc,))
                        imm += 1
            ts1(st[0:4, 2:3], st[0:4, 2:3], 1e-30, ALU.max, (st,), (st,))
            kb.recip(st[0:4, 3:4], st[0:4, 2:3], (st,), (st,))
            kb.tt(st[0:4, 4:5], st[0:4, 3:4], gate_col, ALU.mult, (st, gsig), (st,))
            if first:
                ts1(acc[0:4, hsl], pacc[0:4, 0:128], st[0:4, 4:5], ALU.mult, (pacc, st), (acc,))
            else:
                kb.stt(acc[0:4, hsl], pacc[0:4, 0:128], st[0:4, 4:5], acc[0:4, hsl], ALU.mult, ALU.add, (pacc, st, acc), (acc,))

        crow = A.alloc("crow4", 8)
        kb.dma(crow[0:4, 0:8], TSs.h[:, 0:1].rearrange("h o -> o h").to_broadcast([4, 8]), (TSs,), (crow,), crow, nc_ok=True)

        pib = page_index_bufs(A)
        for s_ in range(4):
            cols = slice(s_ * 4, s_ * 4 + 4)
            idx = page_index(A, s_, pib)
            ps = ps_next()
            for k in range(KC):
                kb.mm(ps[0:4, 0:24], hT[:, k, cols], wgate[:, k, :], k == 0, k == KC - 1, (hT, wgate), (ps,))
            kb.act(gsig[0:4, 0:24], ps[0:4, 0:24], AF.Sigmoid, (ps,), (gsig,))
            for g in range(2):
                for j in range(64):
                    pg = pgs[j % 2]
                    gather(pg, pg[:, 0:1024], ckv_rows[:, :], idx, j)
                    if j % 4 == 0:
                        psk = ps_next()
                    kb.tr(psk[:, (j % 4) * 128:(j % 4 + 1) * 128], pg[:, 512 + g * 128:640 + g * 128], ident[:, :], (pg, ident), (psk,))
                    kb.copy(Vs[:, j, :], pg[:, 768 + g * 128:896 + g * 128], (pg,), (Vs,), eng="dve")
                    if j % 4 == 3:
                        kb.copy(KTs[:, (j - 3) * 128:(j + 1) * 128], psk[:, :], (psk,), (KTs,), eng="act")
                kb.copy(KTs[:, PAST:PAST + 4], newK[:, 0, g, cols], (newK,), (KTs,))
                ps = ps_next()
                pvn = psbf(ps)
                kb.tr(pvn[0:4, 0:128], newVT[:, 0, g, cols], identb[:, :], (newVT, identb), (ps,))
                kb.tr(pvn[0:4, 128:256], newVT[:, 1, g, cols], identb[:, :], (newVT, identb), (ps,))
                kb.copy(Vs[0:4, 64, :], pvn[0:4, 0:128], (ps,), (Vs,))
                for j in range(4):
                    kb.dma(wst[:, 0:256].rearrange("p (two d) -> p two d", two=2),
                           cwin.h[s_, j * 128:(j + 1) * 128, :].rearrange("p (two g d) -> p two g d", two=2, g=2)[:, :, g, :], (), (wst,), wst, eng="pool")
                    ps = ps_next()
                    kb.tr(ps[:, 0:128], wst[:, 0:128], ident[:, :], (wst, ident), (ps,))
                    kb.copy(KW[:, j * 128:(j + 1) * 128], ps[:, 0:128], (ps,), (KW,), eng="act")
                    kb.copy(VW[:, j, :], wst[:, 128:256], (wst,), (VW,))
                kb.copy(KW[:, 512:516], newK[:, 1, g, cols], (newK,), (KW,))
                kb.copy(VW[0:4, 4, :], pvn[0:4, 128:256], (ps,), (VW,))
                for hh in range(4):
                    h = g * 4 + hh
                    hsl = slice(h * 128, (h + 1) * 128)
                    kb.dma(bt[0:4, 0:256], TCs.h[h].rearrange("(t n) -> t n", n=256), (TCs,), (bt,), bt)
                    ps = ps_next()
                    kb.mm(ps[0:4, 0:256], qT[:, h, cols], kcTs[:, s_, g, :], True, True, (qT, kcTs), (ps,))
                    kb.stt(S[0:4, 0:256], ps[0:4, 0:256], SCALE, bt[0:4, 0:256], ALU.mult, ALU.add, (ps, bt), (S,))
                    kb.reduce(st[0:4, 0:1], S[0:4, 0:256], ALU.max, (S,), (st,))
                    ts1(st[0:4, 1:2], st[0:4, 0:1], -1.0, ALU.mult, (st,), (st,))
                    kb.act(pc[0:4, 0:256], S[0:4, 0:256], AF.Exp, (S, st), (pc,), bias=st[0:4, 1:2])
                    kb.reduce(st[0:4, 2:3], pc[0:4, 0:256], ALU.add, (pc,), (st,))
                    kb.recip(st[0:4, 3:4], st[0:4, 2:3], (st,), (st,))
                    ts1(pc[0:4, 0:256], pc[0:4, 0:256], st[0:4, 3:4], ALU.mult, (pc, st), (pc,))
                    if hh == 0:
                        kb.copy(imp[0:4, 0:256], pc[0:4, 0:256], (pc,), (imp,))
                    else:
                        kb.tt(imp[0:4, 0:256], imp[0:4, 0:256], pc[0:4, 0:256], ALU.add, (imp, pc), (imp,))
                    kb.copy(pcb[0:4, 0:256], pc[0:4, 0:256], (pc,), (pcb,))
                    ps = ps_next()
                    pv = psbf(ps)
                    for j in range(2):
                        kb.tr(pv[:, j * 4:(j + 1) * 4], pcb[0:4, j * 128:(j + 1) * 128], identb[0:4, 0:4], (pcb, identb), (ps,))
                    kb.copy(PTb[:, 0:2, :], pv[:, 0:8].rearrange("p (k q) -> p k q", q=4), (ps,), (PTb,))
                    for j in range(2):
                        kb.mm(pacc[0:4, 0:128], PTb[:, j, :], vcs[:, s_, j, g, :], j == 0, j == 1, (PTb, vcs), (pacc,))
                    ts1(acc[0:4, hsl], pacc[0:4, 0:128], gsig[0:4, h:h + 1], ALU.mult, (pacc, gsig), (acc,))
                kb.copy(imp2[0:4, 0:136], bs_sb[0:4, 0:136], (bs_sb,), (imp2,))
                kb.reduce(impw[0:4, 0:128], imp[0:4, 0:256].rearrange("p (b two) -> p b two", two=2), ALU.add, (imp,), (impw,))
                kb.tt(imp2[0:4, 0:128], imp2[0:4, 0:128], impw[0:4, 0:128], ALU.add, (imp2, impw), (imp2,))
                P.op("dve", lambda e: e.max(mx8[0:4, 0:8], imp2[0:4, 0:136]), (imp2,), (mx8,))
                P.op("dve", lambda e: e.match_replace(impw[0:4, 0:136], mx8[0:4, 0:8], imp2[0:4, 0:136], -3e38), (mx8, imp2), (impw,))
                P.op("dve", lambda e: e.max(mx8[0:4, 8:16], impw[0:4, 0:136]), (impw,), (mx8,))
                kb.reduce(st[0:4, 6:7], mx8[0:4, 8:16], ALU.min, (mx8,), (st,))
                ts1(selm[0:4, 0:136], imp2[0:4, 0:136], st[0:4, 6:7], ALU.is_ge, (imp2, st), (selm,))
                for hh in range(4):
                    h = g * 4 + hh
                    hsl = slice(h * 128, (h + 1) * 128)
                    q_ap = qT[:, h, cols]
                    groups = []
                    for gi in range(7):
                        k0 = gi * 1024
                        groups.append((1024, KTs[:, k0:k0 + 1024], KTs, None,
                                       selm[0:4, gi * 16:(gi + 1) * 16].unsqueeze(2).to_broadcast([4, 16, 64]),
                                       [(kt * 128, 128, Vs[:, gi * 8 + kt, :], Vs) for kt in range(8)]))
                    for k0 in (7168, 7680):
                        groups.append((512, KTs[:, k0:k0 + 512], KTs,
                                       (lambda t, k0=k0, h=h: TSs.h[h:h + 1, k0 - t + 3:k0 - t + 3 + 512]),
                                       selm[0:4, k0 // 64:k0 // 64 + 8].unsqueeze(2).to_broadcast([4, 8, 64]),
                                       [(kt * 128, 128, Vs[:, k0 // 128 + kt, :], Vs) for kt in range(4)]))
                    groups.append((4, KTs[:, PAST:PAST + 4], KTs,
                                   (lambda t, h=h: TSs.h[h:h + 1, PAST - t + 3:PAST - t + 3 + 4]),
                                   selm[0:4, 128:129].to_broadcast([4, 4]),
                                   [(0, 4, Vs[0:4, 64, :], Vs)]))
                    attend(h, q_ap, groups, gsig[0:4, 8 + h:9 + h], False, hsl)
                    gw = [(516, KW[:, 0:516], KW, (lambda t, h=h: TWs.h[h:h + 1, 3 - t:3 - t + 516]), None,
                           [(j * 128, 128, VW[:, j, :], VW) for j in range(4)] + [(512, 4, VW[0:4, 4, :], VW)])]
                    attend(h, q_ap, gw, gsig[0:4, 16 + h:17 + h], False, hsl)
            for h0 in range(0, 8, 4):
                ps = ps_next()
                for j in range(4):
                    kb.tr(ps[:, j * 4:(j + 1) * 4], acc[0:4, (h0 + j) * 128:(h0 + j + 1) * 128], ident[0:4, 0:4], (acc, ident), (ps,))
                kb.copy(ynsaT[:, h0:h0 + 4, cols], ps[:, 0:16].rearrange("p (h q) -> p h q", h=4), (ps,), (ynsaT,))
        ps_mod[0] = 8

    def merge_and_out(T, tag=None):
        A = arena
        fT = A.alloc("fTm", KC * TMAX)
        t1 = A.alloc("mt1", TMAX); t2 = A.alloc("mt2", TMAX)
        nx = (A.words - A.off) // (SLOT // 2)
        if tag is not None and nx > 0:
            ws.set_extra([A.alloc("wsx%d" % i, SLOT // 2, view=lambda a: a.bitcast(BF16)) for i in range(nx)], tag)
        for mb in range(D // 256):
            for bi, (yT, ) in enumerate(((yrwT,), (ynsaT,))):
                sg = ws.take()
                sb_ = ws.take()
                vg = sg[:, 0:16 * 256].rearrange("p (k n) -> p k n", k=16)
                vb = sb_[:, 0:8 * 256].rearrange("p (k n) -> p k n", k=8)
                for mi in range(2):
                    pg = ps_next(); pb = ps_next()
                    for k in range(KC):
                        kb.mm(pg[:, :T], vg[:, k, mi * 128:(mi + 1) * 128], hT[:, k, :T], k == 0, k == KC - 1, (sg, hT), (pg,))
                    for k in range(8):
                        kb.mm(pb[:, :T], vb[:, k, mi * 128:(mi + 1) * 128], yT[:, k, :T], k == 0, k == 7, (sb_, yT), (pb,))
                    m = mb * 2 + mi
                    tt_ = t1 if mi == 0 else t2
                    kb.act(tmpA[:, :T], pg[:, :T], AF.Sigmoid, (pg,), (tmpA,))
                    if bi == 0:
                        kb.tt(tt_[:, :T], tmpA[:, :T], pb[:, :T], ALU.mult, (tmpA, pb), (tt_,))
                    else:
                        kb.tt(tmpA[:, :T], tmpA[:, :T], pb[:, :T], ALU.mult, (tmpA, pb), (tmpA,))
                        kb.tt(mergedT[:, m, :T], tmpA[:, :T], tt_[:, :T], ALU.add, (tmpA, tt_), (mergedT,))
        for mb in range(D // 256):
            so = ws.take()
            vo = so[:, 0:16 * 256].rearrange("p (k n) -> p k n", k=16)
            for mi in range(2):
                po = ps_next()
                for k in range(KC):
                    kb.mm(po[:, :T], vo[:, k, mi * 128:(mi + 1) * 128], mergedT[:, k, :T], k == 0, k == KC - 1, (so, mergedT), (po,))
                m = mb * 2 + mi
                kb.copy(fT[:, m * TMAX:m * TMAX + T], po[:, :T], (po,), (fT,), eng="act")
        ws.clear_extra()
        post_residual(fT, g_mixpost, 1.0, T)

    if not cfg.get("skip_nsa"):
        nsa_setup()
        nsa_setup_sample()
    for ti, (kind, t0, T) in enumerate(tiles):
        arena.reset()
        stage_box[0] = arena.alloc("stage", KC * TMAX)
        load_x(kind, t0, T)
        ffn(g_f1pre, g_f1post, T, tag=(ti, 1))
        pre_norm(g_mixpre, T)
        arena.reset()
        stage_box[0] = arena.alloc("stage", 8 * TMAX)
        kvT = arena.alloc("kvT", 12 * TMAX, view=lambda a: a.rearrange("p (k t) -> p k t", k=12))
        win_proj_a(T, kvT)
        if kind == "p":
            if not cfg.get("skip_nsa"):
                nsa_cache_update(t0, T, ti, kvT)
            store_tokmajor(kvp.h[t0:t0 + T, :], lambda c: kvT[:, c, :], (kvT,), 8, T)
            if t0 + T > SEQ_ - 512:
                w0 = t0 - (SEQ_ - 512)
                store_tokmajor(winp.h[w0:w0 + T, :], lambda c: kvT[:, 8 + c, :], (kvT,), 4, T)
        else:
            if not cfg.get("skip_nsa"):
                nsa_sample_newrows(kvT)
            store_tokmajor(kvs.h[0:T, :], lambda c: kvT[:, c, :], (kvT,), 8, T)
            store_tokmajor(None, lambda c: kvT[:, 8 + c, :], (kvT,), 4, T)
            stage = stage_box[0]
            for sq_i in range(4):
                kb.dma(wins.h[sq_i, 508:512, :], stage[sq_i * 4:sq_i * 4 + 4, 0:512], (stage,), (), stage)
                kb.dma(wins.h[sq_i, 0:508, :], cwin.h[sq_i, 4:512, :], (), (), stage)
        arena.reset()
        if cfg.get("skip_rwkv"):
            kb.memset(yrwT[:, :, :], 0.0, (yrwT,))
        else:
            rwkv(kind, t0, T, ti)
        arena.reset()
        if kind == "p" and not cfg.get("skip_nsa"):
            nsa_prompt(t0, T, ti)
        if kind == "s" and not cfg.get("skip_nsa"):
            nsa_sample_compress()
            arena.reset_from(smp["mark"])
            nsa_sample_attend()
        arena.reset()
        merge_and_out(T, tag=(ti, 3))
        arena.reset()
        stage_box[0] = arena.alloc("stage", KC * TMAX)
        ffn(g_f2pre, g_f2post, T, second=True, tag=(ti, 2))
        if kind == "p":
            store_tokmajor(yp.h[t0:t0 + T, :], lambda c: xT[:, c, :], (xT,), KC, T)
        else:
            store_tokmajor(ys.h[0:T, :], lambda c: xT[:, c, :], (xT,), KC, T)


_NC_CACHE = {}


def _rwmask(C):
    r = np.arange(C)[:, None]
    c = np.arange(C)[None, :]
    return np.concatenate([(r < c), (r <= c), (r > c)], axis=1).astype(np.float32)


def _bucket(d):
    import math
    n = np.maximum(d, 0)
    nf = np.maximum(n, 1).astype(np.float32)
    large = 16 + (np.log(nf / np.float32(16)) / np.float32(math.log(64)) * np.float32(16)).astype(np.int32)
    large = np.minimum(large, 31)
    return np.where(n < 16, n, large)


def _onehot33(d, valid, masked):
    oh = np.zeros((33,) + d.shape, np.float32)
    b = _bucket(d)
    for k in range(32):
        oh[k] = ((b == k) & valid).astype(np.float32)
    oh[32] = np.where(masked, -30000.0, 0.0)
    return oh


def _nsa_consts():
    import ml_dtypes
    y = np.arange(2176); d = 2047 - y
    oh_sel = _onehot33(d, d >= 0, d < 0)
    y = np.arange(768); d = 639 - y
    oh_win = _onehot33(d, (d >= 0) & (d < 512), (d < 0) | (d >= 512))
    q = np.arange(128)[:, None]; jj = np.arange(67)[None, :]
    d = q - 32 * (jj - 63) - 31
    oh_cmp = _onehot33(d, d >= 0, np.zeros_like(d, bool)).reshape(33, 128 * 67)
    mc = (d >= 0).astype(np.float32)
    bc = np.zeros((128, 16, 32), np.float32)
    for i in range(16):
        pos = 128 * i + np.arange(128)[:, None]
        cur = pos // 64
        sb = np.arange(32)[None, :]
        forced = (sb == 0) | (sb == cur) | (sb == cur - 1)
        vis = sb * 64 <= pos
        bc[:, i, :] = np.where(vis, np.where(forced, 1e4, 0.0), -1e30)
    x = np.arange(8704); d = 8195 - x
    oh_s_sel = _onehot33(d, d >= 0, d < 0)
    x = np.arange(1024); d = 515 - x
    oh_s_win = _onehot33(d, (d >= 0) & (d < 512), (d < 0) | (d >= 512))
    t = np.arange(4)[:, None]; n = np.arange(256)[None, :]
    d = 8192 + t - (32 * n + 31)
    oh_s_cmp = _onehot33(d, d >= 0, np.zeros_like(d, bool)).reshape(33, 1024)
    bs = np.zeros((4, 136), np.float32)
    bs[:, [0, 127, 128]] = 1e4
    bs[:, 129:] = -1e30
    return {"oh_s_sel": oh_s_sel, "oh_s_win": oh_s_win, "oh_s_cmp": oh_s_cmp, "bs": bs,
            "iotap": np.arange(128, dtype=np.float32).reshape(128, 1),
            "oh_sel": oh_sel, "oh_win": oh_win, "oh_cmp": oh_cmp, "mc": mc, "bc": bc,
            "identb": np.eye(128, dtype=np.float32).astype(ml_dtypes.bfloat16)}


def _consts():
    bones = np.zeros((128, 128), np.float32)
    bones[:64, :64] = 1.0
    bones[64:, 64:] = 1.0
    return {"ident": np.eye(128, dtype=np.float32), "ones": np.ones((128, 128), dtype=np.float32),
            "rwmask": _rwmask(64), "rwmask4": _rwmask(4), "blockones": bones,
            "selhi": np.concatenate([np.zeros((64, 64), np.float32), np.eye(64, dtype=np.float32)], 0),
            **_nsa_consts()}


def kernel(**inputs):
    TP = 256
    tiles = [("p", t0, TP) for t0 in range(0, SEQ, TP)] + [("s", 0, 16)]
    cfg = {"tiles": tiles}
    nc = build(cfg)
    f32 = np.float32

    def w(name):
        return np.ascontiguousarray(np.asarray(inputs[name], dtype=f32)[0])

    shared = {k: w(k) for k in
              ("ffn1_pre_g", "ffn1_post_g", "ffn1_w1", "ffn1_w3", "ffn1_w2", "mix_pre_g", "mix_post_g", "w_in",
               "rw_mu", "rw_w0", "rw_w2", "rw_a0", "rw_a2", "rw_g2", "rw_k_k", "rw_k_a", "rw_lnx_w", "rw_lnx_b",
               "w_br_rw", "w_br_nsa", "w_out", "ffn2_pre_g", "ffn2_post_g", "ffn2_w1", "ffn2_w3", "ffn2_w2")}
    shared["rw_r_k"] = w("rw_r_k").reshape(RW_DIM)
    for k in ("cmp_pe", "cmp_w1", "cmp_w2"):
        shared[k] = w(k)
    shared["rel_bias"] = np.ascontiguousarray(np.asarray(inputs["rel_bias"], dtype=f32))
    shared["cache_kv"] = np.ascontiguousarray(np.asarray(inputs["cache_kv"], dtype=f32)[0]).reshape(2560, 128, 1024)
    page_table = np.asarray(inputs["page_table"], dtype=np.int32)
    shared.update(_consts())
    x_prompt = np.asarray(inputs["x_prompt"], dtype=f32)
    x_sample = np.asarray(inputs["x_sample"], dtype=f32)
    cache_win = np.asarray(inputs["cache_win"], dtype=f32)
    state_rwkv = np.asarray(inputs["state_rwkv"], dtype=f32)
    state_shift = np.asarray(inputs["state_shift"], dtype=f32)
    in_maps = []
    for c in range(NCORE):
        m = dict(shared)
        m["xp"] = np.ascontiguousarray(x_prompt[c])
        m["xs"] = np.ascontiguousarray(x_sample[4 * c:4 * c + 4].reshape(16, D))
        m["cache_win"] = np.ascontiguousarray(cache_win[0, 4 * c:4 * c + 4].reshape(4, 512, 512))
        m["state_rwkv"] = np.ascontiguousarray(state_rwkv[0, 4 * c:4 * c + 4])
        m["state_shift"] = np.ascontiguousarray(state_shift[0, 4 * c:4 * c + 4])
        m["page_table"] = np.ascontiguousarray(page_table[4 * c:4 * c + 4])
        in_maps.append(m)
    res = run_bass_kernel_spmd(nc, in_maps, core_ids=list(range(NCORE)))
    R = res.results
    y_prompt = np.stack([R[c]["y_prompt"] for c in range(NCORE)], 0)
    y_sample = np.concatenate([R[c]["y_sample"].reshape(4, 4, D) for c in range(NCORE)], 0)
    kv_prompt = np.stack([R[c]["kv_prompt"].reshape(SEQ, 4, 2, 128) for c in range(NCORE)], 0)[None]
    kv_sample = np.concatenate([R[c]["kv_sample"].reshape(4, 4, 4, 2, 128) for c in range(NCORE)], 0)[None]
    win_prompt = np.stack([R[c]["win_prompt"].reshape(512, 2, 2, 128) for c in range(NCORE)], 0)[None]
    win_sample = np.concatenate([R[c]["win_sample"].reshape(4, 512, 2, 2, 128) for c in range(NCORE)], 0)[None]
    rwkv_prompt = np.stack([R[c]["rwkv_prompt"] for c in range(NCORE)], 0)[None]
    rwkv_sample = np.concatenate([R[c]["rwkv_sample"] for c in range(NCORE)], 0)[None]
    shift_prompt = np.stack([R[c]["shift_prompt"] for c in range(NCORE)], 0)[None]
    shift_sample = np.concatenate([R[c]["shift_sample"] for c in range(NCORE)], 0)[None]
    return (y_prompt, y_sample, kv_prompt, kv_sample, win_prompt, win_sample,
            rwkv_prompt, rwkv_sample, shift_prompt, shift_sample)
```
